# Optimizing a Trainium2 kernel written in Bass

```python
import math
import jax
import jax.numpy as jnp
from jax import lax
import numpy as np

D_MODEL = 1024
BATCH = 2
SEQ = 16384
DEPTH = 2
DEC_BATCH = 16
DEC_SEQ = 4096
PAST_LEN = 128

N_EVEN = (DEPTH + 1) // 2
N_ODD = DEPTH // 2

A_HEAD_DIM = 128
A_W = D_MODEL // 2
A_HEADS = A_W // A_HEAD_DIM
B_KEY_DIM = 64
B_VAL_DIM = 128
B_HEADS = (D_MODEL // 2) // B_VAL_DIM
B_K_W = B_HEADS * B_KEY_DIM
B_V_W = B_HEADS * B_VAL_DIM
GK_RANK = 16
GLA_GATE_NORM = 16.0
C_HEAD_DIM = 64
C_W = D_MODEL // 2
C_HEADS = C_W // C_HEAD_DIM
C_KV_HEADS = C_HEADS // 4
C_GROUP = C_HEADS // C_KV_HEADS
C_KV_W = C_KV_HEADS * C_HEAD_DIM
WINDOW = 128
ATT_BLOCK = 128
ROPE_THETA = 10000.0
D_HEAD_DIM = 64
D_W = D_MODEL // 2
D_HEADS = D_W // D_HEAD_DIM
W_RANK = 32
A_RANK = 32
RWKV_LN_EPS = 64e-5
CHUNK = 64
ALPHA = (2 * DEPTH) ** 0.25
BETA = (8 * DEPTH) ** -0.25

EV_SIZES = (A_W, A_W, A_W, A_W, A_W, B_K_W, B_K_W, B_V_W, GK_RANK, GK_RANK, B_V_W)
EV_IN = sum(EV_SIZES)
D_SIZES = (D_W, D_W, D_W, W_RANK, W_RANK, A_RANK, D_W)
D_SLAB = sum(D_SIZES)
OD_SIZES = (C_W, C_KV_W, C_KV_W, C_W, D_SLAB)
OD_IN = sum(OD_SIZES)

kernel_name = "hybrid_bidir_hgrn2_gla_swa_rwkv7_encoder"


def split_cols(u, sizes):
    return jnp.split(u, np.cumsum(sizes)[:-1].tolist(), axis=-1)


def layer_norm(x, g, b, eps=1e-5):
    xf = x.astype(jnp.float32)
    mu = jnp.mean(xf, axis=-1, keepdims=True)
    var = jnp.mean(jnp.square(xf - mu), axis=-1, keepdims=True)
    return ((xf - mu) * lax.rsqrt(var + eps) * g + b).astype(x.dtype)


def head_rms_norm(o, g, eps=1e-6):
    o = o * lax.rsqrt(jnp.mean(jnp.square(o), axis=-1, keepdims=True) + eps)
    return o.reshape(o.shape[0], o.shape[1], -1) * g


def modulate(x, c, w_mod, b_mod):
    shift, scale, gate = jnp.split(jax.nn.silu(c) @ w_mod + b_mod, 3, axis=-1)
    return x * (1.0 + scale[:, None]) + shift[:, None], gate[:, None]


def residual_post_norm(x, gate, y, ln_g, ln_b):
    return layer_norm(ALPHA * x + (1.0 + gate) * y, ln_g, ln_b)


def bidirectional(fn, fwd, bwd):
    bsz = fwd[0].shape[0]
    stacked = [jnp.concatenate([f, jnp.flip(b, axis=1)], axis=0) for f, b in zip(fwd, bwd)]
    o = fn(*stacked)
    return o[:bsz] + jnp.flip(o[bsz:], axis=1)


def chunk_gated_linear_attention(q, k, v, log_f):
    bsz, seq, heads, dk = q.shape
    n = seq // CHUNK
    shp = lambda t: t.astype(jnp.float32).reshape(bsz, n, CHUNK, heads, t.shape[-1])
    q, k, v, log_f = shp(q), shp(k), shp(v), shp(log_f)
    b = jnp.cumsum(log_f, axis=2)
    b_last = b[:, :, -1:]
    q_dec = q * jnp.exp(b)
    att = jnp.einsum('bnihk,bnjhk->bnhij', q_dec, k * jnp.exp(-b))
    att = jnp.where(jnp.tril(jnp.ones((CHUNK, CHUNK), bool)), att, 0.0)
    o_intra = jnp.einsum('bnhij,bnjhv->bnihv', att, v)
    k_end = k * jnp.exp(b_last - b)
    chunk_decay = jnp.exp(b_last[:, :, 0])

    def step(S, inp):
        q_c, k_c, v_c, d_c = inp
        o_c = jnp.einsum('bchk,bhkv->bchv', q_c, S)
        S = d_c[..., None] * S + jnp.einsum('bchk,bchv->bhkv', k_c, v_c)
        return S, o_c

    S0 = jnp.zeros((bsz, heads, dk, v.shape[-1]), jnp.float32)
    xs = tuple(jnp.moveaxis(t, 1, 0) for t in (q_dec, k_end, v, chunk_decay))
    _, o_inter = lax.scan(step, S0, xs)
    return (o_intra + jnp.moveaxis(o_inter, 0, 1)).reshape(bsz, seq, heads, -1)


def rotary(t):
    seq, hd = t.shape[1], t.shape[-1]
    inv = ROPE_THETA ** (-jnp.arange(0, hd, 2, dtype=jnp.float32) / hd)
    ang = jnp.arange(seq, dtype=jnp.float32)[:, None] * inv[None]
    cos, sin = jnp.cos(ang)[None, :, None, :], jnp.sin(ang)[None, :, None, :]
    t1, t2 = t[..., : hd // 2], t[..., hd // 2:]
    return jnp.concatenate([t1 * cos - t2 * sin, t2 * cos + t1 * sin], axis=-1).astype(t.dtype)


def banded_window_attention(q, k, v, sink):
    bsz, seq, _, hd = q.shape
    nb = seq // ATT_BLOCK
    qb = q.reshape(bsz, nb, ATT_BLOCK, C_KV_HEADS, C_GROUP, hd)

    def band(t):
        tp = jnp.pad(t.reshape(bsz, nb, ATT_BLOCK, C_KV_HEADS, hd), ((0, 0), (1, 1), (0, 0), (0, 0), (0, 0)))
        return jnp.concatenate([tp[:, :-2], tp[:, 1:-1], tp[:, 2:]], axis=2)

    kb, vb = band(k), band(v)
    s = jnp.einsum('bnigrd,bnjgd->bngrij', qb, kb).astype(jnp.float32) * (hd ** -0.5)
    i = jnp.arange(ATT_BLOCK)[:, None]
    j = jnp.arange(3 * ATT_BLOCK)[None, :]
    kpos = (jnp.arange(nb)[:, None, None] - 1) * ATT_BLOCK + j[None]
    valid = (jnp.abs(j - ATT_BLOCK - i)[None] <= WINDOW) & (kpos >= 0) & (kpos < seq)
    s = jnp.where(valid[None, :, None, None], s, -jnp.inf)
    sink_b = sink.astype(jnp.float32).reshape(C_KV_HEADS, C_GROUP)[None, None, :, :, None]
    lse = jnp.logaddexp(jax.nn.logsumexp(s, axis=-1), sink_b)
    p = jnp.exp(s - lse[..., None])
    o = jnp.einsum('bngrij,bnjgd->bnigrd', p.astype(vb.dtype), vb)
    return o.reshape(bsz, seq, C_HEADS, hd)


def rwkv7_scan(r, log_w, k, v, kk, a):
    bsz, _, heads, n = r.shape
    xs = tuple(jnp.moveaxis(t, 1, 0) for t in (r, jnp.exp(log_w), k, v, kk, kk * a))

    def step(S, inp):
        r_t, w_t, k_t, v_t, kk_t, b_t = inp
        s_kk = jnp.einsum('bhvk,bhk->bhv', S, kk_t)
        S = S * w_t[:, :, None, :] - s_kk[..., None] * b_t[:, :, None, :] + v_t[..., None] * k_t[:, :, None, :]
        return S, jnp.einsum('bhvk,bhk->bhv', S, r_t)

    S0 = jnp.zeros((bsz, heads, n, n), jnp.float32)
    _, o = lax.scan(step, S0, xs)
    return jnp.moveaxis(o, 0, 1)


def rwkv7_time_mix(z, mix, w0, w_w2, a0, a_w2, k_k, k_a, r_k, lnx_w, lnx_b):
    bsz, seq, _ = z.shape
    z_prev = jnp.pad(z[:, :-1], ((0, 0), (1, 0), (0, 0)))
    z_next = jnp.pad(z[:, 1:], ((0, 0), (0, 1), (0, 0)))
    z = z + mix[0] * (z_prev - z) + mix[1] * (z_next - z)
    r, k, v, wl_f, wl_b, al, g = split_cols(z, D_SIZES)

    def log_decay(wl, w0_d, w2_d):
        w = -jax.nn.softplus(-(w0_d + jnp.tanh(wl) @ w2_d).astype(jnp.float32)) - 0.5
        return -jnp.exp(w)

    hd = lambda t: t.astype(jnp.float32).reshape(bsz, seq, D_HEADS, D_HEAD_DIM)
    a = jax.nn.sigmoid((a0 + al @ a_w2).astype(jnp.float32))
    kk = hd(k * k_k)
    kk = kk / jnp.maximum(jnp.sqrt(jnp.sum(jnp.square(kk), axis=-1, keepdims=True)), 1e-12)
    k_mod = k.astype(jnp.float32) * (1.0 + (a - 1.0) * k_a)
    r_h, k_h, v_h, a_h = hd(r), hd(k_mod), hd(v), hd(a)
    o = bidirectional(rwkv7_scan,
                      (r_h, hd(log_decay(wl_f, w0[0], w_w2[0])), k_h, v_h, kk, a_h),
                      (r_h, hd(log_decay(wl_b, w0[1], w_w2[1])), k_h, v_h, kk, a_h))
    mu = jnp.mean(o, axis=-1, keepdims=True)
    var = jnp.mean(jnp.square(o - mu), axis=-1, keepdims=True)
    o = ((o - mu) * lax.rsqrt(var + RWKV_LN_EPS)).reshape(bsz, seq, D_W) * lnx_w + lnx_b
    bonus = jnp.sum(r_h * k_h * r_k.reshape(D_HEADS, D_HEAD_DIM), axis=-1, keepdims=True) * v_h
    return (o + bonus.reshape(bsz, seq, D_W)) * jax.nn.silu(g.astype(jnp.float32))


def even_layer(x, c, w_mod, b_mod, w_in, lb, hgrn_norm, gla_w_gk, gla_b_gk, gla_norm, w_out, ln_g, ln_b):
    bsz, seq, _ = x.shape
    h, gate = modulate(x, c, w_mod, b_mod)
    u = h @ w_in
    aq, ai, af_f, af_b, a_gate, bq, bk, bv, bl_f, bl_b, b_gate = split_cols(u, EV_SIZES)

    def hgrn_gates(zf, lb_d):
        f = lb_d + (1.0 - lb_d) * jax.nn.sigmoid(zf.astype(jnp.float32))
        return 1.0 - f, jnp.log(f)

    ak_f, alog_f = hgrn_gates(af_f, lb[0])
    ak_b, alog_b = hgrn_gates(af_b, lb[1])
    hA = lambda t: t.reshape(bsz, seq, A_HEADS, A_HEAD_DIM)
    aq = jax.nn.silu(aq)
    a_o = bidirectional(chunk_gated_linear_attention,
                        (hA(aq), hA(ak_f), hA(ai), hA(alog_f)),
                        (hA(aq), hA(ak_b), hA(ai), hA(alog_b)))
    a_out = head_rms_norm(a_o, hgrn_norm).astype(x.dtype) * jax.nn.silu(a_gate)

    def gla_log_decay(zl, w2, b2):
        return jax.nn.log_sigmoid((zl @ w2 + b2).astype(jnp.float32)) / GLA_GATE_NORM

    hBk = lambda t: t.reshape(bsz, seq, B_HEADS, B_KEY_DIM)
    hBv = lambda t: t.reshape(bsz, seq, B_HEADS, B_VAL_DIM)
    bq = bq * (B_KEY_DIM ** -0.5)
    b_o = bidirectional(chunk_gated_linear_attention,
                        (hBk(bq), hBk(bk), hBv(bv), hBk(gla_log_decay(bl_f, gla_w_gk[0], gla_b_gk[0]))),
                        (hBk(bq), hBk(bk), hBv(bv), hBk(gla_log_decay(bl_b, gla_w_gk[1], gla_b_gk[1]))))
    b_out = head_rms_norm(b_o, gla_norm).astype(x.dtype) * jax.nn.silu(b_gate)

    y = jnp.concatenate([a_out, b_out], axis=-1) @ w_out
    return residual_post_norm(x, gate, y, ln_g, ln_b)


def odd_layer(x, c, w_mod, b_mod, w_in, sink, mix, w0, w_w2, a0, a_w2, k_k, k_a, r_k, lnx_w, lnx_b,
              w_out, ln_g, ln_b):
    bsz, seq, _ = x.shape
    h, gate = modulate(x, c, w_mod, b_mod)
    u = h @ w_in
    cq, ck, cv, c_gate, d_slab = split_cols(u, OD_SIZES)

    q = rotary(cq.reshape(bsz, seq, C_HEADS, C_HEAD_DIM))
    k = rotary(ck.reshape(bsz, seq, C_KV_HEADS, C_HEAD_DIM))
    c_o = banded_window_attention(q, k, cv.reshape(bsz, seq, C_KV_HEADS, C_HEAD_DIM), sink)
    c_out = c_o.reshape(bsz, seq, C_W).astype(x.dtype) * jax.nn.silu(c_gate)

    d_out = rwkv7_time_mix(d_slab, mix, w0, w_w2, a0, a_w2, k_k, k_a, r_k, lnx_w, lnx_b).astype(x.dtype)

    y = jnp.concatenate([c_out, d_out], axis=-1) @ w_out
    return residual_post_norm(x, gate, y, ln_g, ln_b)


def trunk(x, c, ev_w_mod, ev_b_mod, ev_w_in, hgrn_lb_logits, hgrn_norm, gla_w_gk, gla_b_gk, gla_norm,
          ev_w_out, ev_ln_g, ev_ln_b, od_w_mod, od_b_mod, od_w_in, swa_sink, rwkv_mix, rwkv_w0, rwkv_w_w2,
          rwkv_a0, rwkv_a_w2, rwkv_k_k, rwkv_k_a, rwkv_r_k, rwkv_ln_w, rwkv_ln_b, od_w_out, od_ln_g, od_ln_b):
    lb_all = jnp.cumsum(jax.nn.softmax(hgrn_lb_logits.astype(jnp.float32), axis=1), axis=1)
    for layer in range(DEPTH):
        i = layer // 2
        if layer % 2 == 0:
            x = even_layer(x, c, ev_w_mod[i], ev_b_mod[i], ev_w_in[i], lb_all[:, i], hgrn_norm[i],
                           gla_w_gk[i], gla_b_gk[i], gla_norm[i], ev_w_out[i], ev_ln_g[i], ev_ln_b[i])
        else:
            x = odd_layer(x, c, od_w_mod[i], od_b_mod[i], od_w_in[i], swa_sink[i], rwkv_mix[i], rwkv_w0[i],
                          rwkv_w_w2[i], rwkv_a0[i], rwkv_a_w2[i], rwkv_k_k[i], rwkv_k_a[i], rwkv_r_k[i],
                          rwkv_ln_w[i], rwkv_ln_b[i], od_w_out[i], od_ln_g[i], od_ln_b[i])
    return x


def setup_inputs(seed: int = 0) -> dict:
    key = jax.random.key(seed)
    ks = iter(jax.random.split(key, 40))
    nrm = lambda shape, s: jax.random.normal(next(ks), shape, jnp.float32) * s
    D = D_MODEL
    NE, NO = N_EVEN, N_ODD
    return {
        "x_prompt": nrm((BATCH, SEQ, D), 1.0),
        "x_sample": nrm((DEC_BATCH, DEC_SEQ, D), 1.0),
        "c_prompt": nrm((BATCH, D), 1.0),
        "c_sample": nrm((DEC_BATCH, D), 1.0),
        "ev_w_mod": nrm((NE, D, 3 * D), 0.1 * D ** -0.5),
        "ev_b_mod": nrm((NE, 3 * D), 0.02),
        "ev_w_in": nrm((NE, D, EV_IN), D ** -0.5),
        "hgrn_lb_logits": nrm((2, NE + 1, A_W), 0.1),
        "hgrn_norm": 1.0 + nrm((NE, A_W), 0.01),
        "gla_w_gk": nrm((NE, 2, GK_RANK, B_K_W), GK_RANK ** -0.5),
        "gla_b_gk": nrm((NE, 2, B_K_W), 0.1),
        "gla_norm": 1.0 + nrm((NE, B_V_W), 0.01),
        "ev_w_out": nrm((NE, A_W + B_V_W, D), BETA * (A_W + B_V_W) ** -0.5),
        "ev_ln_g": 1.0 + nrm((NE, D), 0.01),
        "ev_ln_b": nrm((NE, D), 0.01),
        "od_w_mod": nrm((NO, D, 3 * D), 0.1 * D ** -0.5),
        "od_b_mod": nrm((NO, 3 * D), 0.02),
        "od_w_in": nrm((NO, D, OD_IN), D ** -0.5),
        "swa_sink": nrm((NO, C_HEADS), 0.5),
        "rwkv_mix": 0.5 * jax.random.uniform(next(ks), (NO, 2, D_SLAB), jnp.float32),
        "rwkv_w0": -1.0 + nrm((NO, 2, D_W), 0.5),
        "rwkv_w_w2": nrm((NO, 2, W_RANK, D_W), 0.5 * W_RANK ** -0.5),
        "rwkv_a0": nrm((NO, D_W), 0.1),
        "rwkv_a_w2": nrm((NO, A_RANK, D_W), 0.5 * A_RANK ** -0.5),
        "rwkv_k_k": 0.85 + nrm((NO, D_W), 0.02),
        "rwkv_k_a": 1.0 + nrm((NO, D_W), 0.02),
        "rwkv_r_k": nrm((NO, D_W), 0.1),
        "rwkv_ln_w": 1.0 + nrm((NO, D_W), 0.01),
        "rwkv_ln_b": nrm((NO, D_W), 0.01),
        "od_w_out": nrm((NO, C_W + D_W, D), BETA * (C_W + D_W) ** -0.5),
        "od_ln_g": 1.0 + nrm((NO, D), 0.01),
        "od_ln_b": nrm((NO, D), 0.01),
    }


def reference(x_prompt, x_sample, c_prompt, c_sample, ev_w_mod, ev_b_mod, ev_w_in, hgrn_lb_logits, hgrn_norm,
              gla_w_gk, gla_b_gk, gla_norm, ev_w_out, ev_ln_g, ev_ln_b, od_w_mod, od_b_mod, od_w_in, swa_sink,
              rwkv_mix, rwkv_w0, rwkv_w_w2, rwkv_a0, rwkv_a_w2, rwkv_k_k, rwkv_k_a, rwkv_r_k, rwkv_ln_w,
              rwkv_ln_b, od_w_out, od_ln_g, od_ln_b):
    weights = (ev_w_mod, ev_b_mod, ev_w_in, hgrn_lb_logits, hgrn_norm, gla_w_gk, gla_b_gk, gla_norm,
               ev_w_out, ev_ln_g, ev_ln_b, od_w_mod, od_b_mod, od_w_in, swa_sink, rwkv_mix, rwkv_w0,
               rwkv_w_w2, rwkv_a0, rwkv_a_w2, rwkv_k_k, rwkv_k_a, rwkv_r_k, rwkv_ln_w, rwkv_ln_b,
               od_w_out, od_ln_g, od_ln_b)
    y_prompt = trunk(x_prompt, c_prompt, *weights)
    y_sample = trunk(x_sample, c_sample, *weights)
    return (y_prompt, y_sample)
```

```python
import numpy as np
from contextlib import ExitStack
import concourse.bass as bass
import concourse.mybir as mybir
from concourse.bass_utils import run_bass_kernel_spmd
from concourse.alu_op_type import AluOpType as ALU

F32 = mybir.dt.float32
BF16 = mybir.dt.bfloat16
AF = mybir.ActivationFunctionType
D = 1024
ALPHA = 4 ** 0.25
NDMA = 24


class Tl:
    def __init__(self, t, name):
        self.t, self.name, self.w, self.r = t, name, {}, {}

    def __getitem__(self, idx):
        v = Vw(self, self.t[idx])
        i0 = idx[0] if isinstance(idx, tuple) else idx
        if isinstance(i0, slice) and i0.start is not None:
            v.p0, v.pn = i0.start, (i0.stop - i0.start)
        return v


class Vw:
    def __init__(self, tile, ap):
        self.tile, self.ap = tile, ap
        self.p0, self.pn = 0, 128

    def __getitem__(self, idx):
        v = Vw(self.tile, self.ap[idx])
        v.p0, v.pn = self.p0, self.pn
        i0 = idx[0] if isinstance(idx, tuple) else idx
        if isinstance(i0, slice) and i0.start is not None:
            v.p0, v.pn = self.p0 + i0.start, (i0.stop - i0.start)
        return v


class Op:
    __slots__ = ("eng", "fn", "deps", "sig", "sigval", "idx", "dma_k")

    def __init__(self, eng, fn, idx):
        self.eng, self.fn, self.idx = eng, fn, idx
        self.deps, self.sig, self.sigval, self.dma_k = set(), False, 0, -1


class Prog:
    ENGS = ("pe", "act", "dve", "pool")

    def __init__(self, nc):
        self.nc, self.ops, self.ndma = nc, [], 0

    def add(self, eng, fn, reads=(), writes=()):
        idx = len(self.ops)
        op = Op(eng, fn, idx)
        isdma = eng == "sp"
        key = ("dma", idx) if isdma else eng
        raw, oth = set(), set()
        for v in reads:
            if v is None or not isinstance(v, Vw):
                continue
            raw.update(v.tile.w.values())
            if getattr(v.tile, "psum", False):
                oth.update(i for k, i in v.tile.r.items() if k != key)
        for v in writes:
            oth.update(v.tile.w.values())
            oth.update(v.tile.r.values())
        for d in raw:
            p = self.ops[d]
            if p.eng == eng and eng == "pe":
                continue
            op.deps.add(d)
        for d in oth:
            p = self.ops[d]
            if p.eng == eng and not isdma:
                continue
            op.deps.add(d)
        for v in reads:
            if v is None or not isinstance(v, Vw):
                continue
            v.tile.r[key] = idx
        for v in writes:
            tl = v.tile
            tl.r = {}
            if isdma:
                tl.w = {k: i for k, i in tl.w.items() if not isinstance(k, tuple)}
            tl.w[key] = idx
        if isdma:
            op.dma_k = self.ndma
            self.ndma += 1
        self.ops.append(op)
        return op

    def setup(self, es):
        nc = self.nc
        self.sems = {e: es.enter_context(nc.semaphore("s_" + e)) for e in self.ENGS}
        self.dsem = [es.enter_context(nc.semaphore("d%d" % i)) for i in range(NDMA)]
        self.cnt = {e: 0 for e in self.ENGS}
        self.waited = {e: {} for e in self.ENGS + ("sp",)}
        self.done = 0

    def emit(self):
        nc = self.nc
        ops, sems, dsem = self.ops, self.sems, self.dsem
        lo = self.done
        for op in ops[lo:]:
            op.deps = {d for d in op.deps if d >= lo}
            for d in op.deps:
                ops[d].sig = True
        for op in ops[lo:]:
            if op.eng != "sp" and op.sig:
                self.cnt[op.eng] += 1
                op.sigval = self.cnt[op.eng]

        def need(op):
            req = {}
            for d in op.deps:
                p = ops[d]
                if p.eng == "sp":
                    s, v = dsem[p.dma_k % NDMA], 16 * (p.dma_k // NDMA + 1)
                else:
                    s, v = sems[p.eng], p.sigval
                k = id(s)
                if k not in req or req[k][1] < v:
                    req[k] = (s, v)
            return req

        def run(engname, e):
            waited = self.waited[engname]
            for op in ops[lo:]:
                if op.eng != engname:
                    continue
                req = need(op)
                if engname == "sp" and op.dma_k >= NDMA:
                    s = dsem[op.dma_k % NDMA]
                    v = 16 * (op.dma_k // NDMA)
                    if id(s) not in req or req[id(s)][1] < v:
                        req[id(s)] = (s, v)
                for k, (s, v) in req.items():
                    if waited.get(k, 0) < v:
                        e.wait_ge(s, v)
                        waited[k] = v
                ins = op.fn(e)
                if engname == "sp":
                    ins.then_inc(dsem[op.dma_k % NDMA], 16)
                elif op.sig:
                    ins.then_inc(sems[engname], 1)
            if engname == "sp":
                for i in range(min(NDMA, self.ndma)):
                    last = ((self.ndma - 1 - i) // NDMA) * NDMA + i
                    v = 16 * (last // NDMA + 1)
                    if waited.get(id(dsem[i]), 0) < v:
                        e.wait_ge(dsem[i], v)
                        waited[id(dsem[i])] = v

        with nc.Block() as block:
            @block.sync
            def _(e):
                run("sp", e)

            @block.tensor
            def _(e):
                run("pe", e)

            @block.scalar
            def _(e):
                run("act", e)

            @block.vector
            def _(e):
                run("dve", e)

            @block.gpsimd
            def _(e):
                run("pool", e)
        self.done = len(ops)


def _ap(v):
    return v.ap if isinstance(v, Vw) else v


class K:
    def __init__(self, nc, es):
        self.nc, self.es, self.P = nc, es, Prog(nc)
        self.n = 0

    def sb(self, shape, dt=F32, name=None):
        self.n += 1
        nm = "%s%d" % (name or "t", self.n)
        return Tl(self.es.enter_context(self.nc.sbuf_tensor(nm, list(shape), dt)), nm)

    def ps(self, shape, name=None):
        self.n += 1
        nm = "%s%d" % (name or "p", self.n)
        t = Tl(self.es.enter_context(self.nc.psum_tensor(nm, list(shape), F32)), nm)
        t.psum = True
        return t

    def mm(self, out, lhsT, rhs, start=True, stop=True):
        rg = (lhsT.p0, lhsT.pn, out.p0, out.pn)
        last = getattr(self, "last_rg", (0, 128, 0, 128))
        tiled = lambda g: g[1] < 128 or g[3] < 128
        if rg != last and (tiled(rg) or tiled(last)) and getattr(self, "dummy", None) is not None:
            dps, dl, dr_ = self.dummy
            self.P.add("pe", lambda e: e.matmul(dps.ap, dl.ap, dr_.ap, start=True, stop=True), [dl], [dps])
        self.last_rg = rg
        self.P.add("pe", lambda e: e.matmul(out.ap, lhsT.ap, rhs.ap, start=start, stop=stop),
                   [lhsT, rhs], [out])

    def act(self, out, in_, func, scale=1.0, bias=0.0, accum=None, eng="act"):
        kw = {}
        if accum is not None:
            kw["accum_out"] = accum.ap
        self.P.add(eng, lambda e: e.activation(out.ap, in_.ap, func, bias=_ap(bias), scale=_ap(scale), **kw),
                   [in_, scale, bias], [out] + ([accum] if accum is not None else []))

    def tt(self, out, a, b, op, eng="dve"):
        self.P.add(eng, lambda e: e.tensor_tensor(out.ap, a.ap, b.ap, op), [a, b], [out])

    def ts(self, out, a, s1, s2, op0, op1=None, eng="dve"):
        if op1 is None:
            self.P.add(eng, lambda e: e.tensor_scalar(out.ap, a.ap, _ap(s1), None, op0), [a, s1], [out])
        else:
            self.P.add(eng, lambda e: e.tensor_scalar(out.ap, a.ap, _ap(s1), _ap(s2), op0, op1), [a, s1, s2], [out])

    def stt(self, out, a, s, b, op0, op1):
        self.P.add("dve", lambda e: e.scalar_tensor_tensor(out.ap, a.ap, _ap(s), b.ap, op0, op1), [a, s, b], [out])

    def scan(self, out, d0, d1, init, op0, op1):
        self.P.add("dve", lambda e: e.tensor_tensor_scan(out.ap, d0.ap, d1.ap, init, op0, op1), [d0, d1], [out])

    def copy(self, out, in_, eng="dve"):
        if eng == "act":
            self.P.add("act", lambda e: e.copy(out.ap, in_.ap), [in_], [out])
        else:
            self.P.add(eng, lambda e: e.tensor_copy(out.ap, in_.ap), [in_], [out])

    def memset(self, out, val, eng="dve"):
        self.P.add(eng, lambda e: e.memset(out.ap, val), [], [out])

    def sig(self, out, in_, nbias=0.0, e_out=None):
        e = e_out if e_out is not None else out
        self.act(e, in_, AF.Exp, scale=-1.0, bias=nbias)
        self.ts(out, e, 1.0, None, ALU.add)
        self.recip(out, out)

    def rsqrt(self, out, in_, scale, eps):
        self.act(out, in_, AF.Ln, scale=scale, bias=eps)
        self.act(out, out, AF.Exp, scale=-0.5)

    def recip(self, out, in_):
        self.P.add("dve", lambda e: e.reciprocal(out.ap, in_.ap), [in_], [out])

    def dma(self, out, in_):
        self.P.add("sp", lambda e: e.dma_start(out=out.ap, in_=in_.ap), [in_], [out])


class Rot:
    def __init__(self, items):
        self.items, self.i = items, 0

    def nxt(self):
        self.i += 1
        return self.items[(self.i - 1) % len(self.items)]


EV_COLS = dict(aq=0, ai=512, af_f=1024, af_b=1536, a_gate=2048, bq=2560, bk=2816, bv=3072, bl_f=3584, bl_b=3600,
               b_gate=3616, zf=4128, zb=4384)
WC = 4640


def build(NSEG, L, layers=(0, 1), LDT=BF16, use_pool=True):
    nc = bass.Bass("TRN2", target_bir_lowering=False)
    es = ExitStack()
    kb = K(nc, es)
    TT_ = NSEG * L
    NT = TT_ // 128
    TPS = L // 128

    def din(name, shape):
        return Tl(nc.dram_tensor(name, list(shape), F32, kind="ExternalInput").ap(), name)

    x_d = nc.dram_tensor("x", [TT_, D], F32, kind="ExternalInput").ap()
    y_d = nc.dram_tensor("y", [TT_, D], F32, kind="ExternalOutput").ap()
    x1_d = nc.dram_tensor("x1s", [TT_, D], F32, kind="Internal").ap()
    ob_d = nc.dram_tensor("obs", [TT_, D], F32, kind="Internal").ap()
    xT = [Tl(x_d, "x%d" % i) for i in range(NT)]
    yT = [Tl(y_d, "y%d" % i) for i in range(NT)]
    x1T = [Tl(x1_d, "x1%d" % i) for i in range(NT)]
    obT = [Tl(ob_d, "ob%d" % i) for i in range(NT)]
    rows = lambda n: slice(n * 128, (n + 1) * 128)

    cT_d = din("cT", [128, 8 * NSEG])
    flg_d = din("flags", [128, 2 * NSEG])
    cst_d = din("consts", [128, 4 * 128])
    ev = {}
    for nm, shp in [("ev_w_mod", [D, 3 * D]), ("ev_b_modT", [128, 24]), ("ev_b_mod", [1, 3 * D]), ("ev_w_in", [D, 4128]),
                    ("ev_w_blT", [16, 2 * D]), ("gla_w_gk", [16, 512]), ("gla_b_gkT", [128, 4]), ("lblT", [128, 16]),
                    ("ev_rows", [1, 3 * D]), ("ev_w_out", [D, D])]:
        ev[nm] = din(nm, shp)

    od = {}
    for nm, shp in [("od_w_mod", [D, 3 * D]), ("od_b_modT", [128, 24]), ("od_b_mod", [1, 3 * D]), ("od_w_in", [D, 4064]),
                    ("od_rows", [1, 3 * D]), ("od_w_out", [D, D]), ("consts2", [128, 6 * 512 + 256]), ("mixT", [128, 2, 19]),
                    ("vecT", [128, 4, 8]), ("w_w2", [32, 2, 512]), ("a_w2", [32, 512]), ("sink", [1, 8]),
                    ("ropeT", [128, 2, TT_])]:
        od[nm] = din(nm, shp)

    with es:
        kb.P.setup(es)
        cst = kb.sb([128, 512], F32, "cst")
        kb.dma(cst[:], cst_d[:, :])
        ident, Jm, mask, ones = cst[:, 0:128], cst[:, 128:256], cst[:, 256:384], cst[:, 384:512]
        flg = kb.sb([128, 2 * NSEG], F32, "flg")
        kb.dma(flg[:], flg_d[:, :])
        cT = kb.sb([128, 8 * NSEG], F32, "cT")
        kb.dma(cT[:], cT_d[:, :])
        scT = kb.sb([128, 8 * NSEG], F32, "scT")
        kb.sig(scT[:], cT[:])
        kb.tt(scT[:], scT[:], cT[:], ALU.mult)
        zer = kb.sb([128, 128], F32, "zer")
        kb.memset(zer[:], 0.0)

        PD = Rot([kb.ps([128, 1024], "pd") for _ in range(2)])
        PS = Rot([kb.ps([128, 512], "ps") for _ in range(3)])
        psd = kb.ps([128, 512], "psd")
        identg = kb.sb([128, 128], BF16, "identg")
        kb.copy(identg[:], ident)
        kb.dummy = (psd[:, 0:2], identg[:], identg[:, 0:2])

        WH = {}
        Wout = kb.sb([128, 8, D], BF16, "Wout")
        modT = kb.sb([128, 24 * NSEG], F32, "modT")
        bmT = kb.sb([128, 24], F32, "bmT")
        g1b = kb.sb([128, NSEG, D], BF16, "g1b")

        tmpR = {}

        TMPN = [2]

        def tmp(nm, shape=(128, 128), n=None, dt=F32):
            n = n or TMPN[0]
            if nm not in tmpR:
                tmpR[nm] = Rot([kb.sb(shape, dt, nm) for _ in range(n)])
            return tmpR[nm].nxt()

        def load_weights(w_in_d, ncols, w_out_d, w_mod_d, b_modT_d, b_mod_d, rows_d, extra=None, ro=0):
            WH["ro"] = ro
            ses = ExitStack()
            old_es, old_tmp = kb.es, dict(tmpR)
            kb.es = ses
            stg = Rot([kb.sb([128, 2048], F32, "stg") for _ in range(2)])
            brow = kb.sb([1, 3 * D], F32, "brow")
            rrow = kb.sb([1, 3 * D], F32, "rrow")
            for k in range(8):
                for c0 in range(0, ncols, 2048):
                    c1 = min(ncols, c0 + 2048)
                    s = stg.nxt()
                    kb.dma(s[:, 0:c1 - c0], w_in_d[k * 128:(k + 1) * 128, c0:c1])
                    kb.copy(WH["Win"][:, k, c0:c1], s[:, 0:c1 - c0], eng="pool" if (k % 2) else "dve")
                s = stg.nxt()
                kb.dma(s[:, 0:D], w_out_d[k * 128:(k + 1) * 128, :])
                kb.copy(Wout[:, k, :], s[:, 0:D], eng="act")
            kb.dma(bmT[:], b_modT_d[:, :])
            kb.dma(brow[:], b_mod_d[:, :])
            kb.dma(rrow[:], rows_d[:, :])
            pm = PD.nxt()
            for jb in range(12):
                wm = stg.nxt()
                for k in range(8):
                    kb.dma(wm[:, k * 256:(k + 1) * 256], w_mod_d[k * 128:(k + 1) * 128, jb * 256:(jb + 1) * 256])
                for jj in range(2):
                    j = jb * 2 + jj
                    for k in range(8):
                        kb.mm(pm[:, j * NSEG:(j + 1) * NSEG], wm[:, k * 256 + jj * 128:k * 256 + (jj + 1) * 128],
                              scT[:, k * NSEG:(k + 1) * NSEG], start=(k == 0), stop=(k == 7))
            for j in range(24):
                kb.act(modT[:, j * NSEG:(j + 1) * NSEG], pm[:, j * NSEG:(j + 1) * NSEG], AF.Identity,
                       bias=bmT[:, j:j + 1], scale=1.0)
            kb.ts(modT[:, 8 * NSEG:16 * NSEG], modT[:, 8 * NSEG:16 * NSEG], 1.0, None, ALU.add)
            for s_ in range(NSEG):
                pg = PD.nxt()
                for q4 in range(4):
                    wm = stg.nxt()
                    for k in range(8):
                        kb.dma(wm[:, k * 256:(k + 1) * 256], w_mod_d[k * 128:(k + 1) * 128, 2 * D + q4 * 256:2 * D + (q4 + 1) * 256])
                    for k in range(8):
                        scB = tmp("scB")
                        kb.act(scB[:], zer[:], AF.Identity, bias=scT[:, k * NSEG + s_:k * NSEG + s_ + 1], scale=1.0)
                        kb.mm(pg[:, q4 * 256:(q4 + 1) * 256], scB[:], wm[:, k * 256:(k + 1) * 256], start=(k == 0), stop=False)
                    kb.mm(pg[:, q4 * 256:(q4 + 1) * 256], ones[0:1, :], brow[0:1, 2 * D + q4 * 256:2 * D + (q4 + 1) * 256],
                          start=False, stop=True)
                kb.ts(g1b[:, s_, :], pg[:], 1.0, None, ALU.add)
            for r in range(ro, 3):
                pg = PD.nxt()
                for hf in range(2):
                    kb.mm(pg[:, hf * 512:(hf + 1) * 512], ones[0:1, :], rrow[0:1, r * D + hf * 512:r * D + (hf + 1) * 512])
                kb.copy(WH["rowsb"][:, r - ro, :], pg[:], eng="act")
            if extra is not None:
                extra()
            kb.P.emit()
            ses.close()
            kb.es = old_es
            tmpR.clear()
            tmpR.update(old_tmp)

        NB = 2
        xtR = Rot([kb.sb([128, D], F32, "xt") for _ in range(2)])
        hTR = Rot([kb.sb([128, 8, 128], BF16, "hT") for _ in range(NB)])
        from contextlib import contextmanager

        @contextmanager
        def layer_scope():
            ses = ExitStack()
            old_es, old_tmp = kb.es, dict(tmpR)
            kb.es = ses
            try:
                yield
            finally:
                kb.P.emit()
                ses.close()
                kb.es = old_es
                tmpR.clear()
                tmpR.update(old_tmp)

        EVT = {}

        def gla_unit(hT, dr, flip, psO, ocol, V, vcol, qsrc, ksrc, zsrc, S, sidx, heads, kind):
            lbT, bgk = EVT["lbT"], EVT["bgk"]
            def proj(col):
                p = PS.nxt()
                for k in range(8):
                    kb.mm(p[:, 0:128], WH["Win"][:, k, col:col + 128], hT[:, k, :], start=(k == 0), stop=(k == 7))
                return p
            pq = proj(qsrc)
            q = tmp("q")
            sn = tmp("sn")
            lf = tmp("lf")
            if kind == "A":
                h = sidx
                kb.sig(q[:], pq[:, 0:128])
                kb.tt(q[:], q[:], pq[:, 0:128], ALU.mult)
                pf = proj(zsrc)
                sg = tmp("sg")
                ee = tmp("ee")
                kb.sig(sg[:], pf[:, 0:128], e_out=ee[:])
                kb.act(lf[:], sg[:], AF.Ln, scale=lbT[:, dr * 8 + 4 + h:dr * 8 + 5 + h], bias=lbT[:, dr * 8 + h:dr * 8 + h + 1])
                kb.stt(sn[:], ee[:], lbT[:, dr * 8 + 4 + h:dr * 8 + 5 + h], sg[:], ALU.mult, ALU.mult)
                esc = 1.0
                sop = ALU.add
            else:
                p_ = sidx
                kb.act(q[:], pq[:, 0:128], AF.Identity, scale=0.125)
                pk = proj(ksrc)
                kb.copy(sn[:], pk[:, 0:128], eng="act")
                pf = proj(zsrc)
                sg = tmp("sg")
                kb.act(sg[:], pf[:, 0:128], AF.Exp, scale=-1.0, bias=EVT["nbgk"][:, dr * 2 + p_:dr * 2 + p_ + 1])
                kb.act(lf[:], sg[:], AF.Ln, bias=1.0)
                esc = 1.0 / 16.0
                sop = ALU.subtract
            B = tmp("B")
            kb.scan(B[:], ones, lf[:], 0.0, ALU.mult, sop)
            nb = tmp("nb", (128, 4))
            kb.ts(nb[:, 0:1], B[:, 63:64], -esc, None, ALU.mult)
            kb.ts(nb[:, 1:2], B[:, 63:64], esc, None, ALU.mult)
            kb.ts(nb[:, 2:3], B[:, 127:128], esc, None, ALU.mult)
            E1, E1n, E2, E3 = tmp("E1"), tmp("E1n"), tmp("E2"), tmp("E3")
            kb.act(E1[:], B[:], AF.Exp, scale=esc, bias=nb[:, 0:1])
            kb.act(E1n[:], B[:], AF.Exp, scale=-esc, bias=nb[:, 1:2])
            kb.act(E2[:], B[:], AF.Exp, scale=esc)
            kb.act(E3[:], B[:], AF.Exp, scale=-esc, bias=nb[:, 2:3])
            qd, qi, kd, ke = tmp("qd"), tmp("qi"), tmp("kd"), tmp("ke")
            kb.tt(qd[:], q[:], E1[:], ALU.mult, eng="pool")
            qi_o = qi[:, ::-1] if flip else qi[:]
            kb.tt(qi_o, q[:], E2[:], ALU.mult)
            kb.tt(kd[:], sn[:], E1n[:], ALU.mult, eng="pool")
            kb.tt(ke[:], sn[:], E3[:], ALU.mult)
            pT = PS.nxt()
            kb.mm(pT[:, 0:128], ke[:], ident)
            keT = tmp("keT")
            kb.copy(keT[:], pT[:, 0:128], eng="act")
            nh = len(heads)
            dk = 128 // nh

            def partB():
                for hi, hd in enumerate(heads):
                    pr = slice(hi * dk, (hi + 1) * dk)
                    pA = PS.nxt()
                    kb.mm(pA[:, 0:128], kd[pr, :], qd[pr, :])
                    A = tmp("A")
                    A_o = A[:, ::-1] if flip else A[:]
                    kb.tt(A_o, pA[:, 0:128], mask, ALU.mult)
                    oc = slice(ocol + hd * 128, ocol + (hd + 1) * 128)
                    vc = slice(vcol + hd * 128, vcol + (hd + 1) * 128)
                    Sh = S[pr, hd, :]
                    kb.mm(psO[:, oc], A[:], V[:, vc], start=True, stop=False)
                    kb.mm(psO[:, oc], qi[pr, :], Sh, start=False, stop=True)
                    pS_ = PS.nxt()
                    kb.mm(pS_[pr, 0:128], keT[:, pr], V[:, vc])
                    kb.stt(Sh, Sh, E2[pr, 127:128], pS_[pr, 0:128], ALU.mult, ALU.add)
            return partB

        def even_sweep(dr):
            S_A, S_B, oacc, VR = EVT["S_A"], EVT["S_B"], EVT["oacc"], EVT["VR"]
            flip = dr == 1
            order = range(NT - 1, -1, -1) if flip else range(NT)
            xsrc = xT
            kb.memset(S_A[:], 0.0)
            kb.memset(S_B[:], 0.0)
            for n in order:
                seg = n // TPS
                first = (n % TPS == (TPS - 1 if flip else 0))
                if first:
                    fc = flg[:, (NSEG if flip else 0) + seg:(NSEG if flip else 0) + seg + 1]
                    kb.ts(S_A[:], S_A[:], fc, None, ALU.mult)
                    kb.ts(S_B[:], S_B[:], fc, None, ALU.mult)
                xt = xtR.nxt()
                kb.dma(xt[:], xsrc[n][rows(n), :])
                pX = PD.nxt()
                for k in range(8):
                    kb.mm(pX[:, k * 128:(k + 1) * 128], xt[:, k * 128:(k + 1) * 128], Jm if flip else ident)
                hT = hTR.nxt()
                for k in range(8):
                    kb.act(hT[:, k, :], pX[:, k * 128:(k + 1) * 128], AF.Identity,
                           scale=modT[:, (8 + k) * NSEG + seg:(8 + k) * NSEG + seg + 1],
                           bias=modT[:, k * NSEG + seg:k * NSEG + seg + 1])
                V = VR.nxt()
                pV = PD.nxt()
                for hf, col in enumerate((EV_COLS["ai"], EV_COLS["bv"])):
                    for k in range(8):
                        kb.mm(pV[:, hf * 512:(hf + 1) * 512], hT[:, k, :], WH["Win"][:, k, col:col + 512],
                              start=(k == 0), stop=(k == 7))
                kb.copy(V[:], pV[:], eng="act")
                psO = PD.nxt()
                pend = None
                for u in range(6):
                    if u < 4:
                        h = u
                        b_ = gla_unit(hT, dr, flip, psO, 0, V, 0, EV_COLS["aq"] + h * 128, None,
                                      EV_COLS["af_b" if flip else "af_f"] + h * 128, S_A, h, [h], "A")
                    else:
                        p_ = u - 4
                        b_ = gla_unit(hT, dr, flip, psO, 512, V, 512, EV_COLS["bq"] + p_ * 128, EV_COLS["bk"] + p_ * 128,
                                      EV_COLS["zb" if flip else "zf"] + p_ * 128, S_B, p_, [2 * p_, 2 * p_ + 1], "B")
                    if pend is not None:
                        pend()
                    pend = b_
                pend()
                o = oacc.nxt()
                if flip:
                    kb.copy(o[:], psO[:], eng="act")
                    kb.dma(obT[n][rows(n), :], o[:])
                    continue
                ob = tmp("ob", (128, D), 1)
                kb.dma(ob[:], obT[n][rows(n), :])
                kb.tt(o[:], psO[:], ob[:], ALU.add)
                ss = tmp("ss", (128, 8))
                sq = tmp("sq", (128, 128))
                for h8 in range(8):
                    kb.act(sq[:], o[:, h8 * 128:(h8 + 1) * 128], AF.Square, accum=ss[:, h8:h8 + 1])
                rs = tmp("rs", (128, 8))
                kb.rsqrt(rs[:], ss[:], 1.0 / 128.0, 1e-6)
                pG = PD.nxt()
                for hf, col in enumerate((EV_COLS["a_gate"], EV_COLS["b_gate"])):
                    for k in range(8):
                        kb.mm(pG[:, hf * 512:(hf + 1) * 512], hT[:, k, :], WH["Win"][:, k, col:col + 512],
                              start=(k == 0), stop=(k == 7))
                sgt = tmp("sgt", (128, D), 1)
                kb.sig(sgt[:], pG[:])
                on = o
                for h8 in range(8):
                    cs = slice(h8 * 128, (h8 + 1) * 128)
                    kb.stt(on[:, cs], o[:, cs], rs[:, h8:h8 + 1], WH["rowsb"][:, 0, cs], ALU.mult, ALU.mult)
                kb.tt(on[:], on[:], sgt[:], ALU.mult, eng="pool")
                kb.tt(on[:], on[:], pG[:], ALU.mult)
                finish(n, seg, on, xt, x1T)

        def finish(n, seg, on, xt, dstT):
            pT = PD.nxt()
            for k in range(8):
                kb.mm(pT[:, k * 128:(k + 1) * 128], on[:, k * 128:(k + 1) * 128], ident)
            onT = tmp("onT", (128, 8, 128), 1, dt=BF16)
            kb.copy(onT[:, 0:4, :], pT[:, 0:512], eng="act")
            kb.copy(onT[:, 4:8, :], pT[:, 512:1024], eng="dve")
            finish2(n, seg, onT, xt, dstT)

        def finish2(n, seg, onT, xt, dstT):
            pY = PD.nxt()
            for hf in range(2):
                for k in range(8):
                    kb.mm(pY[:, hf * 512:(hf + 1) * 512], onT[:, k, :], Wout[:, k, hf * 512:(hf + 1) * 512],
                          start=(k == 0), stop=(k == 7))
            kb.tt(pY[:], pY[:], g1b[:, seg, :], ALU.mult)
            r = xt
            kb.stt(r[:], xt[:], ALPHA, pY[:], ALU.mult, ALU.add)
            st = tmp("st", (128, 12))
            kb.P.add("dve", lambda e: e.bn_stats(st.t[:, 0:6], r.t[:, 0:512]), [r[:]], [st[:]])
            kb.P.add("dve", lambda e: e.bn_stats(st.t[:, 6:12], r.t[:, 512:1024]), [r[:]], [st[:]])
            mv = tmp("mv", (128, 4))
            kb.P.add("dve", lambda e: e.bn_aggr(mv.t[:, 0:2], st.t[:, 0:12]), [st[:]], [mv[:]])
            kb.rsqrt(mv[:, 3:4], mv[:, 1:2], 1.0, 1e-5)
            yo = r
            ro = WH.get("ro", 0)
            kb.ts(yo[:], r[:], mv[:, 0:1], mv[:, 3:4], ALU.subtract, ALU.mult)
            kb.tt(yo[:], yo[:], WH["rowsb"][:, 1 - ro, :], ALU.mult, eng="pool")
            kb.tt(yo[:], yo[:], WH["rowsb"][:, 2 - ro, :], ALU.add)
            kb.dma(dstT[n][rows(n), :], yo[:])

        if 0 in layers:
          with layer_scope():
            WH["Win"] = kb.sb([128, 8, WC], BF16, "Win")
            WH["rowsb"] = kb.sb([128, 3, D], F32, "rowsb")
            S_A = EVT["S_A"] = kb.sb([128, 4, 128], F32, "S_A")
            S_B = EVT["S_B"] = kb.sb([128, 4, 128], F32, "S_B")
            lbT = EVT["lbT"] = kb.sb([128, 16], F32, "lbT")
            bgk = EVT["bgk"] = kb.sb([128, 4], F32, "bgk")
            EVT["oacc"] = Rot([kb.sb([128, D], F32, "oacc") for _ in range(1)])
            EVT["VR"] = Rot([kb.sb([128, D], F32, "V") for _ in range(1)])
            def ev_extra():
                wbl = kb.sb([16, 2 * D], F32, "wbl")
                w2 = kb.sb([16, 512], F32, "w2")
                kb.dma(wbl[:], ev["ev_w_blT"][:, :])
                kb.dma(w2[:], ev["gla_w_gk"][:, :])
                for dr in range(2):
                    for k in range(8):
                        p = PS.nxt()
                        kb.mm(p[:, 0:256], wbl[:, dr * D + k * 128:dr * D + (k + 1) * 128], w2[:, dr * 256:(dr + 1) * 256])
                        c0 = EV_COLS["zf"] + dr * 256
                        kb.copy(WH["Win"][:, k, c0:c0 + 256], p[:, 0:256], eng="act")
            load_weights(ev["ev_w_in"], 4128, ev["ev_w_out"], ev["ev_w_mod"], ev["ev_b_modT"], ev["ev_b_mod"], ev["ev_rows"], ev_extra)
            lbl = kb.sb([128, 16], F32, "lbl")
            kb.dma(lbl[:], ev["lblT"][:, :])
            for dr in range(2):
                dlt = tmp("dlt", (128, 4))
                kb.tt(dlt[:], lbl[:, dr * 8:dr * 8 + 4], lbl[:, dr * 8 + 4:dr * 8 + 8], ALU.subtract)
                kb.sig(lbT[:, dr * 8:dr * 8 + 4], dlt[:])
                kb.ts(lbT[:, dr * 8 + 4:dr * 8 + 8], lbT[:, dr * 8:dr * 8 + 4], -1.0, 1.0, ALU.mult, ALU.add)
            kb.dma(bgk[:], ev["gla_b_gkT"][:, :])
            EVT["nbgk"] = kb.sb([128, 4], F32, "nbgk")
            kb.ts(EVT["nbgk"][:], bgk[:], -1.0, None, ALU.mult)
            even_sweep(1)
            even_sweep(0)
        if 1 in layers:
          with layer_scope():
            WH["Win"] = kb.sb([128, 8, 4064], BF16, "Win")
            WH["rowsb"] = kb.sb([128, 2, D], F32, "rowsb")
            OD = dict(cq=0, ck=512, cv=640, cg=768, r=1280, k=1792, v=2304, wl_f=2816, wl_b=2848, al=2880, g=2912,
                      rq=3424, rk=3936)
            src = x1T if 0 in layers else xT
            c2 = kb.sb([128, 6 * 512], BF16, "c2")
            c2f = kb.sb([128, 256], F32, "c2f")
            kb.dma(c2f[:], od["consts2"][:, 3072:3328])
            m_su, m_iu, m_sl, eye8 = c2[:, 0:512], c2[:, 512:1024], c2[:, 1024:1536], c2[:, 1536:2048]
            m_ge4, m_le4 = c2[:, 2048:2560], c2[:, 2560:3072]
            oblk, cmk = c2f[:, 0:128], c2f[:, 128:256]
            mixT = kb.sb([128, 3, 19], F32, "mixT")
            kb.dma(mixT[:, 0:2, :], od["mixT"][:, :, :])
            vecT = kb.sb([128, 4, 8], F32, "vecT")
            kb.dma(vecT[:], od["vecT"][:, :, :])
            nw0 = kb.sb([128, 4, 3], F32, "nw0")
            sinkE = kb.sb([128, 8], F32, "sinkE")
            H = kb.sb([128, 4, 64], F32, "H")
            Hb = kb.sb([128, 4, 64], LDT, "Hb")
            identb = kb.sb([128, 128], LDT, "identb")
            oblkb_t = kb.sb([128, 128], LDT, "oblkb")
            oblkb = oblkb_t[:]
            aw2b = kb.sb([32, 512], LDT, "aw2b")
            ww2b = kb.sb([32, 2, 512], LDT, "ww2b")
            zrR = Rot([kb.sb([128, 19, 130], BF16, "zr") for _ in range(3)])
            kTR = Rot([kb.sb([128, 128], F32, "kT") for _ in range(3)])
            v65R = Rot([kb.sb([128, 2, 65], BF16, "v65") for _ in range(3)])
            hTo = hTR
            xto = xtR
            ropR = Rot([kb.sb([128, 2, 128], F32, "rop") for _ in range(2)])
            obT_d = nc.dram_tensor("obT", [512, TT_], F32, kind="Internal").ap()
            obTT = [Tl(obT_d, "obT%d" % i) for i in range(NT)]
            rope_d = od["ropeT"]

            def od_extra():
                for q6 in range(3):
                    cs_ = kb.sb([128, 1024], F32, "c2s")
                    kb.dma(cs_[:], od["consts2"][:, q6 * 1024:(q6 + 1) * 1024])
                    kb.copy(c2[:, q6 * 1024:(q6 + 1) * 1024], cs_[:])
                kb.copy(identb[:], ident)
                kb.copy(oblkb_t[:], oblk)
                ww2 = kb.sb([32, 2, 512], F32, "ww2")
                kb.dma(ww2[:], od["w_w2"][:, :, :])
                aw2 = kb.sb([32, 512], F32, "aw2")
                kb.dma(aw2[:], od["a_w2"][:, :])
                kb.copy(ww2b[:], ww2[:])
                kb.copy(aw2b[:], aw2[:])
                srow = kb.sb([1, 8], F32, "srow")
                kb.dma(srow[:], od["sink"][:, :])
                p = PS.nxt()
                kb.mm(p[:, 0:8], ones[0:1, :], srow[0:1, :])
                kb.act(sinkE[:], p[:, 0:8], AF.Exp)
                kb.ts(nw0[:, :, 0:2], vecT[:, :, 6:8], -1.0, None, ALU.mult)
                kb.ts(nw0[:, :, 2:3], vecT[:, :, 0:1], -1.0, None, ALU.mult)
                kb.tt(mixT[:, 2, :], mixT[:, 0, :], mixT[:, 1, :], ALU.add)
                kb.ts(mixT[:, 2, :], mixT[:, 2, :], -1.0, 1.0, ALU.mult, ALU.add)
            load_weights(od["od_w_in"], 4064, od["od_w_out"], od["od_w_mod"], od["od_b_modT"], od["od_b_mod"], od["od_rows"], od_extra, ro=1)

            tr = lambda cb: slice(64 * cb, 64 * cb + 64)
            hc = lambda h: slice(64 * h, 64 * h + 64)
            pc = lambda p, cb: slice((p * 2 + cb) * 64, (p * 2 + cb + 1) * 64)
            T5 = lambda nm, dt=F32: tmp(nm, (128, 512), 1, dt=dt)
            LD = LDT
            PL = "pool" if use_pool else "dve"
            TMPN[0] = 1

            def projF(hT, col, m=128):
                p = PS.nxt()
                for k in range(8):
                    kb.mm(p[0:m, 0:128], WH["Win"][:, k, col:col + m], hT[:, k, :], start=(k == 0), stop=(k == 7))
                return p

            def stage1(n, flip, dirn, need_g, attn):
                seg = n // TPS
                xt = xto.nxt()
                kb.dma(xt[:], src[n][rows(n), :])
                pX = PD.nxt()
                for k in range(8):
                    kb.mm(pX[:, k * 128:(k + 1) * 128], xt[:, k * 128:(k + 1) * 128], Jm if flip else ident)
                hT = hTo.nxt()
                for k in range(8):
                    kb.act(hT[:, k, :], pX[:, k * 128:(k + 1) * 128], AF.Identity,
                           scale=modT[:, (8 + k) * NSEG + seg:(8 + k) * NSEG + seg + 1],
                           bias=modT[:, k * NSEG + seg:k * NSEG + seg + 1])
                zr = zrR.nxt()
                groups = [("r", 0), ("k", 4), ("v", 8)] + ([("g", 12)] if need_g else [])
                for gi, (nm, j0) in enumerate(groups):
                    for j in range(4):
                        p = projF(hT, OD[nm] + j * 128)
                        kb.copy(zr[:, j0 + j, 1:129], p[:, 0:128], eng="act" if (j % 2) else "dve")
                wj = 17 if dirn == 1 else 16
                p = projF(hT, OD["wl_b" if dirn == 1 else "wl_f"], 32)
                kb.copy(zr[0:32, wj, 1:129], p[0:32, 0:128], eng="act")
                p = projF(hT, OD["al"], 32)
                kb.copy(zr[0:32, 18, 1:129], p[0:32, 0:128], eng="dve")
                st = dict(n=n, seg=seg, xt=xt, hT=hT, zr=zr)
                if attn:
                    rop = ropR.nxt()
                    kb.dma(rop[:], rope_d[:, :, n * 128:(n + 1) * 128])
                    st["rop"] = rop
                    pk, pkr = projF(hT, OD["ck"]), projF(hT, OD["rk"])
                    t1, t2 = tmp("rt1"), tmp("rt2")
                    kb.tt(t1[:], pk[:, 0:128], rop[:, 0, :], ALU.mult)
                    kb.tt(t2[:], pkr[:, 0:128], rop[:, 1, :], ALU.mult)
                    kT = kTR.nxt()
                    kb.tt(kT[:], t1[:], t2[:], ALU.add)
                    pv = PS.nxt()
                    for k in range(8):
                        kb.mm(pv[:, 0:128], hT[:, k, :], WH["Win"][:, k, OD["cv"]:OD["cv"] + 128], start=(k == 0), stop=(k == 7))
                    v65 = v65R.nxt()
                    kb.memset(v65[:, :, 64:65], 1.0)
                    for g in range(2):
                        kb.copy(v65[:, g, 0:64], pv[:, g * 64:(g + 1) * 64], eng="act")
                    st["kT"], st["v65"] = kT, v65
                return st

            def halo(cur, prv, nxt, flip):
                zr = cur["zr"]
                for nb, dst, srccol in ((prv, 0, 128), (nxt, 129, 1)):
                    if nb is None:
                        kb.memset(zr[:, :, dst:dst + 1], 0.0)
                        continue
                    if nb["seg"] == cur["seg"]:
                        kb.copy(zr[:, :, dst:dst + 1], nb["zr"][:, :, srccol:srccol + 1])
                    else:
                        sg = max(nb["seg"], cur["seg"])
                        kb.ts(zr[:, :, dst:dst + 1], nb["zr"][:, :, srccol:srccol + 1], flg[:, sg:sg + 1], None, ALU.mult)

            def rwkv_tile(cur, flip, dirn, final):
                zr = cur["zr"]
                seg = cur["seg"]
                n = cur["n"]
                mp, mn = (1, 0) if flip else (0, 1)
                zm = tmp("zm", (128, 19, 128), 1, dt=BF16)
                wj = 17 if dirn == 1 else 16
                js = list(range(12)) + (list(range(12, 16)) if final else []) + [wj, 18]
                for j in js:
                    m_ = 32 if j >= 16 else 128
                    kb.act(zm[0:m_, j, :], zr[0:m_, j, 1:129], AF.Identity, scale=mixT[0:m_, 2, j:j + 1])
                    kb.stt(zm[0:m_, j, :], zr[0:m_, j, 0:128], mixT[0:m_, mp, j:j + 1], zm[0:m_, j, :], ALU.mult, ALU.add)
                    kb.stt(zm[0:m_, j, :], zr[0:m_, j, 2:130], mixT[0:m_, mn, j:j + 1], zm[0:m_, j, :], ALU.mult, ALU.add)
                th = tmp("th", (32, 128), 1, dt=LD)
                thf = tmp("thf", (32, 128), 1)
                kb.act(thf[:], zm[0:32, wj, :], AF.Exp, scale=-2.0)
                kb.ts(thf[:], thf[:], 1.0, None, ALU.add)
                kb.recip(thf[:], thf[:])
                kb.ts(th[:], thf[:], 2.0, -1.0, ALU.mult, ALU.add)
                F = {nm: T5("F" + nm, LD) for nm in ("rd", "kd", "bd", "kkd")}
                gam = tmp("gam", (128, 4, 2), 1)
                Tm = {nm: T5("T" + nm, LD) for nm in ("V", "KK", "KE", "NBE")}
                bon = T5("bon") if final else None
                for p in range(4):
                    r_, k_, vT = zm[:, p, :], zm[:, 4 + p, :], zm[:, 8 + p, :]
                    alv = zm[0:32, 18, :]
                    if LD != BF16:
                        vTt = tmp("vTf")
                        kb.copy(vTt[:], vT, eng="act")
                        vT = vTt[:]
                        alt = tmp("alf", (32, 128), 1)
                        kb.copy(alt[:], alv)
                        alv = alt[:]
                    vc = lambda i: vecT[:, p, i:i + 1]
                    pa = PS.nxt()
                    kb.mm(pa[:, 0:128], aw2b[:, p * 128:(p + 1) * 128], alv)
                    a = tmp("a")
                    kb.sig(a[:], pa[:, 0:128], nbias=nw0[:, p, 2:3])
                    pw = PS.nxt()
                    kb.mm(pw[:, 0:128], ww2b[:, dirn, p * 128:(p + 1) * 128], th[:])
                    e1 = tmp("e1")
                    kb.act(e1[:], pw[:, 0:128], AF.Exp, scale=-1.0, bias=nw0[:, p, dirn:dirn + 1])
                    kb.act(e1[:], e1[:], AF.Ln, bias=1.0)
                    ew = tmp("ew")
                    kb.act(ew[:], e1[:], AF.Exp, scale=-1.0, bias=-0.5)
                    c_ = tmp("c_")
                    kb.scan(c_[:], cmk, ew[:], 0.0, ALU.mult, ALU.subtract)
                    eCt, eN, eX, eE = tmp("eC"), tmp("eN"), tmp("eX"), tmp("eE")
                    eC = eCt[:]
                    kb.act(eC, c_[:], AF.Exp)
                    kb.copy(gam[:, p, 0:1], eCt[:, 63:64])
                    kb.copy(gam[:, p, 1:2], eCt[:, 127:128])
                    kb.act(eN[:], c_[:], AF.Exp, scale=-1.0)
                    cx = tmp("cx")
                    kb.tt(cx[:], c_[:], ew[:], ALU.add)
                    kb.act(eX[:], cx[:], AF.Exp)
                    for cb in range(2):
                        kb.act(eE[:, tr(cb)], c_[:, tr(cb)], AF.Exp, scale=-1.0, bias=c_[:, 64 * cb + 63:64 * cb + 64])
                    kk0 = tmp("kk0")
                    kb.ts(kk0[:], k_, vc(1), None, ALU.mult)
                    sq = tmp("sq", dt=LD)
                    kb.act(sq[:], kk0[:], AF.Square)
                    pss = PS.nxt()
                    kb.mm(pss[:, 0:128], oblkb, sq[:])
                    nr = tmp("nr")
                    kb.ts(nr[:], pss[:, 0:128], 1e-24, None, ALU.max)
                    kb.act(nr[:], nr[:], AF.Ln)
                    kb.act(nr[:], nr[:], AF.Exp, scale=-0.5)
                    kk = tmp("kk")
                    kb.tt(kk[:], kk0[:], nr[:], ALU.mult)
                    t1 = tmp("t1")
                    kb.ts(t1[:], a[:], 1.0, vc(2), ALU.subtract, ALU.mult)
                    km = tmp("km")
                    kb.stt(km[:], t1[:], 1.0, k_, ALU.add, ALU.mult)
                    b_ = tmp("b_")
                    kb.tt(b_[:], kk[:], a[:], ALU.mult, eng=PL)
                    P_ = slice(p * 128, (p + 1) * 128)
                    kb.tt(F["rd"][:, P_], r_, eC, ALU.mult, eng=PL)
                    kb.tt(F["kd"][:, P_], km[:], eN[:], ALU.mult)
                    kb.tt(F["bd"][:, P_], b_[:], eN[:], ALU.mult, eng=PL)
                    kb.tt(F["kkd"][:, P_], kk[:], eX[:], ALU.mult)
                    kef = tmp("kef", dt=LD)
                    kb.tt(kef[:], km[:], eE[:], ALU.mult, eng=PL)
                    nbf = tmp("nbf", dt=LD)
                    kb.stt(nbf[:], b_[:], -1.0, eE[:], ALU.mult, ALU.mult)
                    if final:
                        rk = tmp("rk", dt=LD)
                        kb.stt(rk[:], r_, vc(3), km[:], ALU.mult, ALU.mult)
                        pb = PS.nxt()
                        kb.mm(pb[:, 0:128], oblkb, rk[:])
                        kb.tt(bon[:, P_], pb[:, 0:128], vT, ALU.mult)
                    pt = PS.nxt()
                    for i_, srcv in enumerate((vT, F["kkd"][:, P_], kef[:], nbf[:])):
                        kb.mm(pt[:, i_ * 128:(i_ + 1) * 128], srcv, identb[:])
                    for i_, dst in enumerate(("V", "KK", "KE", "NBE")):
                        kb.copy(Tm[dst][:, P_], pt[:, i_ * 128:(i_ + 1) * 128], eng="act" if i_ % 2 else "dve")
                rd, kd, bd, kkd = F["rd"], F["kd"], F["bd"], F["kkd"]
                V, KK, KE, NBE = Tm["V"], Tm["KK"], Tm["KE"], Tm["NBE"]
                frs = lambda hp: slice(64 * hp, 64 * hp + 64)

                def newt(nm):
                    return tmp(nm, (128, 512), 5, dt=LD) if nm == "S5" else T5(nm, LD)

                def scores(lhs, rhs, msk, nm, neg=False):
                    ps = PS.nxt()
                    for hp in range(2):
                        for cb in range(2):
                            for p in range(4):
                                h = 2 * p + hp
                                kb.mm(ps[tr(cb), hc(h)], lhs[frs(hp), pc(p, cb)], rhs[frs(hp), pc(p, cb)])
                    o_ = newt(nm)
                    if neg:
                        kb.stt(o_[:], ps[:], -1.0, msk, ALU.mult, ALU.mult)
                    else:
                        kb.tt(o_[:], ps[:], msk, ALU.mult)
                    return o_

                def bprod(A, B, nm, add=None, eng="act"):
                    ps = PS.nxt()
                    for cb in range(2):
                        for h in range(8):
                            kb.mm(ps[tr(cb), hc(h)], A[tr(cb), hc(h)], B[tr(cb), hc(h)])
                    o_ = newt(nm)
                    if add is None:
                        kb.copy(o_[:], ps[:], eng=eng)
                    else:
                        kb.tt(o_[:], ps[:], add[:], ALU.add)
                    return o_

                def fprod(pairs):
                    ps = PS.nxt()
                    for cb in range(2):
                        for hp in range(2):
                            for p in range(4):
                                h = 2 * p + hp
                                for i, (A, B) in enumerate(pairs):
                                    kb.mm(ps[frs(hp), pc(p, cb)], A[tr(cb), hc(h)], B[tr(cb), hc(h)], start=(i == 0), stop=(i == len(pairs) - 1))
                    return ps

                LkT = scores(kd, kkd, m_su, "LkT")
                MkT = scores(kd, rd, m_iu, "MkT")
                Nn = scores(bd, kkd, m_su, "S5")
                nMbT = scores(bd, rd, m_iu, "nMbT", neg=True)
                Ll = scores(kkd, bd, m_sl, "S5")
                Y = tmp("S5", (128, 512), 5, dt=LD)
                kb.tt(Y[:], eye8, Nn[:], ALU.subtract)
                Lp, Np = Ll, Nn
                for lvl in range(5):
                    L2 = bprod(Np, Lp, "S5", eng="act")
                    if lvl < 4:
                        N2 = bprod(Lp, Np, "S5", eng="dve")
                    Y = bprod(L2, Y, "S5", add=Y)
                    Lp, Np = L2, (N2 if lvl < 4 else None)
                TT = Y
                Zs = bprod(LkT, V, "Fkkd")
                Ws = bprod(TT, KK, "Fkd", eng="dve")
                U0 = bprod(TT, Zs, "Fbd")
                pP = fprod([(Ws, NBE)])
                PTs = T5("PTs", LD)
                kb.copy(PTs[:], pP[:], eng="act")
                pQ = fprod([(KE, V), (NBE, U0)])
                Qs = T5("LkT", LD)
                kb.copy(Qs[:], pQ[:], eng="dve")
                pR = fprod([(Ws, nMbT)])
                RpT = T5("RpT", LD)
                kb.tt(RpT[:], pR[:], rd[:], ALU.add)
                pOd = PD.nxt()
                pO0, pO1 = pOd[:, 0:512], pOd[:, 512:1024]
                for cb in range(2):
                    for hp in range(2):
                        for p in range(4):
                            h = 2 * p + hp
                            kb.mm(pO0[frs(hp), pc(p, cb)], V[tr(cb), hc(h)], MkT[tr(cb), hc(h)], start=True, stop=False)
                            kb.mm(pO0[frs(hp), pc(p, cb)], U0[tr(cb), hc(h)], nMbT[tr(cb), hc(h)], start=False, stop=True)
                first = (n % TPS == (TPS - 1 if flip else 0))
                if first:
                    kb.ts(H[:], H[:], flg[:, (NSEG if flip else 0) + seg:(NSEG if flip else 0) + seg + 1], None, ALU.mult)
                    kb.copy(Hb[:], H[:], eng="act")
                for cb in range(2):
                    for hp in range(2):
                        for p in range(4):
                            kb.mm(pO1[frs(hp), pc(p, cb)], Hb[frs(hp), p, :], RpT[frs(hp), pc(p, cb)])
                    pH = PS.nxt()
                    for hp in range(2):
                        for p in range(4):
                            kb.mm(pH[frs(hp), p * 64:(p + 1) * 64], PTs[frs(hp), pc(p, cb)], Hb[frs(hp), p, :], start=True, stop=False)
                            kb.mm(pH[frs(hp), p * 64:(p + 1) * 64], identb[frs(hp), frs(hp)], Qs[frs(hp), pc(p, cb)], start=False, stop=True)
                    for p in range(4):
                        kb.stt(H[:, p, :], H[:, p, :], gam[:, p, cb:cb + 1], pH[:, p * 64:(p + 1) * 64], ALU.mult, ALU.add)
                    kb.copy(Hb[:], H[:], eng="act")
                return pOd, zm, bon

            def odd_bwd():
                kb.memset(H[:], 0.0)
                kb.memset(Hb[:], 0.0)
                order = list(range(NT - 1, -1, -1))
                sts = {}
                for i, n in enumerate(order + [None]):
                    if n is not None:
                        sts[n] = stage1(n, True, 1, False, False)
                    if i == 0:
                        continue
                    m = order[i - 1]
                    prv = sts.get(order[i - 2]) if i >= 2 else None
                    halo(sts[m], prv, sts.get(n) if n is not None else None, True)
                    pOd, _, _ = rwkv_tile(sts[m], True, 1, False)
                    o_ = T5("obl")
                    o2 = T5("o2f")
                    kb.copy(o2[:], pOd[:, 0:512], eng="act")
                    for p in range(4):
                        kb.tt(o_[:, p * 128:(p + 1) * 128][:, ::-1], pOd[:, 512 + p * 128:512 + (p + 1) * 128], o2[:, p * 128:(p + 1) * 128], ALU.add)
                        kb.dma(obTT[m][p * 128:(p + 1) * 128, m * 128:(m + 1) * 128], o_[:, p * 128:(p + 1) * 128])
                    if i >= 2:
                        del sts[order[i - 2]]

            def attn_tile(cur, prv, nxt):
                hT, rop, seg = cur["hT"], cur["rop"], cur["seg"]
                qr = T5("qrf")
                for r in range(4):
                    pq, pqr = projF(hT, OD["cq"] + r * 128), projF(hT, OD["rq"] + r * 128)
                    t1, t2 = tmp("rt1"), tmp("rt2")
                    kb.tt(t1[:], pq[:, 0:128], rop[:, 0, :], ALU.mult)
                    kb.tt(t2[:], pqr[:, 0:128], rop[:, 1, :], ALU.mult)
                    kb.tt(qr[:, r * 128:(r + 1) * 128], t1[:], t2[:], ALU.add)
                pOa = PD.nxt()
                for g in range(2):
                    gr = slice(64 * g, 64 * g + 64)
                    Pts = []
                    for nb, msk in ((prv, m_ge4), (cur, None), (nxt, m_le4)):
                        if nb is None:
                            continue
                        pS_ = PS.nxt()
                        kb.mm(pS_[:], nb["kT"][gr, :], qr[gr, :])
                        Pt = tmp("Pt", (128, 512), 3, dt=BF16)
                        kb.act(Pt[:], pS_[:], AF.Exp, scale=0.125)
                        if msk is not None:
                            if nb["seg"] == seg:
                                kb.tt(Pt[:], Pt[:], msk, ALU.mult)
                            else:
                                sg = max(nb["seg"], seg)
                                kb.tt(Pt[:], Pt[:], msk, ALU.mult)
                                kb.ts(Pt[:], Pt[:], flg[:, sg:sg + 1], None, ALU.mult)
                        Pts.append((Pt, nb["v65"]))
                    for r in range(4):
                        for i, (Pt, v65) in enumerate(Pts):
                            kb.mm(pOa[:, g * 512 + r * 65:g * 512 + (r + 1) * 65], Pt[:, r * 128:(r + 1) * 128], v65[:, g, :],
                                  start=(i == 0), stop=(i == len(Pts) - 1))
                den = tmp("den", (128, 8))
                for g in range(2):
                    for r in range(4):
                        h = 4 * g + r
                        c0 = g * 512 + r * 65
                        kb.tt(den[:, h:h + 1], pOa[:, c0 + 64:c0 + 65], sinkE[:, h:h + 1], ALU.add)
                kb.recip(den[:], den[:])
                co = T5("cof")
                for g in range(2):
                    for r in range(4):
                        h = 4 * g + r
                        c0 = g * 512 + r * 65
                        kb.ts(co[:, h * 64:(h + 1) * 64], pOa[:, c0:c0 + 64], den[:, h:h + 1], None, ALU.mult)
                pG = PS.nxt()
                for k in range(8):
                    kb.mm(pG[:], hT[:, k, :], WH["Win"][:, k, OD["cg"]:OD["cg"] + 512], start=(k == 0), stop=(k == 7))
                sg_ = T5("sgf")
                kb.sig(sg_[:], pG[:])
                kb.tt(co[:], co[:], sg_[:], ALU.mult, eng="pool")
                kb.tt(co[:], co[:], pG[:], ALU.mult)
                return co

            def odd_fwd():
                kb.memset(H[:], 0.0)
                kb.memset(Hb[:], 0.0)
                order = list(range(NT))
                sts = {}
                for i, n in enumerate(order + [None]):
                    if n is not None:
                        sts[n] = stage1(n, False, 0, True, True)
                    if i == 0:
                        continue
                    m = order[i - 1]
                    cur = sts[m]
                    prv = sts.get(order[i - 2]) if i >= 2 else None
                    nxt = sts.get(n) if n is not None else None
                    halo(cur, prv, nxt, False)
                    pOd, zm, bon = rwkv_tile(cur, False, 0, True)
                    ob = T5("obl")
                    for p in range(4):
                        kb.dma(ob[:, p * 128:(p + 1) * 128], obTT[m][p * 128:(p + 1) * 128, m * 128:(m + 1) * 128])
                    od_ = ob
                    kb.tt(od_[:], pOd[:, 0:512], ob[:], ALU.add)
                    kb.tt(od_[:], pOd[:, 512:1024], od_[:], ALU.add)
                    onT = tmp("onT", (128, 8, 128), 1, dt=BF16)
                    for p in range(4):
                        vc = lambda i_: vecT[:, p, i_:i_ + 1]
                        pm_ = PS.nxt()
                        kb.mm(pm_[:, 0:128], oblk, od_[:, p * 128:(p + 1) * 128])
                        cen = tmp("cen")
                        kb.stt(cen[:], pm_[:, 0:128], -1.0 / 64.0, od_[:, p * 128:(p + 1) * 128], ALU.mult, ALU.add)
                        sq = tmp("sqf")
                        kb.act(sq[:], cen[:], AF.Square)
                        pv_ = PS.nxt()
                        kb.mm(pv_[:, 0:128], oblk, sq[:])
                        rs = tmp("rsd")
                        kb.rsqrt(rs[:], pv_[:, 0:128], 1.0 / 64.0, 64e-5)
                        kb.tt(cen[:], cen[:], rs[:], ALU.mult)
                        kb.ts(cen[:], cen[:], vc(4), vc(5), ALU.mult, ALU.add)
                        kb.tt(cen[:], cen[:], bon[:, p * 128:(p + 1) * 128], ALU.add)
                        sgg = tmp("sgg")
                        kb.sig(sgg[:], zm[:, 12 + p, :])
                        kb.tt(cen[:], cen[:], sgg[:], ALU.mult, eng="pool")
                        kb.tt(onT[:, 4 + p, :], cen[:], zm[:, 12 + p, :], ALU.mult)
                    co = attn_tile(cur, prv, nxt)
                    pT = PS.nxt()
                    for r in range(4):
                        kb.mm(pT[:, r * 128:(r + 1) * 128], co[:, r * 128:(r + 1) * 128], ident)
                    kb.copy(onT[:, 0:4, :], pT[:], eng="act")
                    xr_ = xto.nxt()
                    kb.dma(xr_[:], src[m][rows(m), :])
                    finish2(m, cur["seg"], onT, xr_, yT)
                    if i >= 2:
                        del sts[order[i - 2]]

            odd_bwd()
            odd_fwd()
        if 1 not in layers:
            for n in range(NT):
                t_ = xtR.nxt()
                kb.dma(t_[:], x1T[n][rows(n), :])
                kb.dma(yT[n][rows(n), :], t_[:])
        kb.P.emit()
    return nc


def consts():
    c = np.zeros((128, 512), np.float32)
    c[:, 0:128] = np.eye(128)
    c[:, 128:256] = np.eye(128)[::-1]
    j = np.arange(128)[:, None]
    i = np.arange(128)[None, :]
    c[:, 256:384] = (j <= i)
    c[:, 384:512] = 1.0
    return c


def shared_inputs(w):
    f = lambda a: np.ascontiguousarray(a, dtype=np.float32)
    m = {}
    m["consts"] = consts()
    m["ev_w_mod"] = f(w["ev_w_mod"][0])
    m["ev_b_modT"] = f(w["ev_b_mod"][0].reshape(24, 128).T)
    m["ev_b_mod"] = f(w["ev_b_mod"][0].reshape(1, -1))
    m["ev_w_in"] = f(w["ev_w_in"][0])
    wi = w["ev_w_in"][0]
    m["ev_w_blT"] = f(np.concatenate([wi[:, 3584:3600].T, wi[:, 3600:3616].T], axis=1))
    m["gla_w_gk"] = f(np.concatenate([w["gla_w_gk"][0, 0], w["gla_w_gk"][0, 1]], axis=1))
    m["gla_b_gkT"] = f(w["gla_b_gk"][0].reshape(2, 2, 128).transpose(2, 0, 1).reshape(128, 4))
    m["ev_rows"] = f(np.concatenate([w["hgrn_norm"][0], w["gla_norm"][0], w["ev_ln_g"][0], w["ev_ln_b"][0]]).reshape(1, -1))
    m["ev_w_out"] = f(w["ev_w_out"][0])
    m["lblT"] = f(w["hgrn_lb_logits"].reshape(2, 2, 4, 128).transpose(3, 0, 1, 2).reshape(128, 16))
    return m


def consts2():
    c = np.zeros((128, 6 * 512 + 256), np.float32)
    r = (np.arange(128) % 64)[:, None]
    q = (np.arange(512) % 64)[None, :]
    c[:, 0:512] = r < q
    c[:, 512:1024] = r <= q
    c[:, 1024:1536] = r > q
    c[:, 1536:2048] = r == q
    j = np.arange(128)[:, None]
    i = (np.arange(512) % 128)[None, :]
    c[:, 2048:2560] = j >= i
    c[:, 2560:3072] = j <= i
    a = np.arange(128)
    c[:, 3072:3200] = (a[:, None] // 64) == (a[None, :] // 64)
    cm = np.ones((128, 128), np.float32)
    cm[:, 0] = 0.0
    cm[:, 64] = 0.0
    c[:, 3200:3328] = cm
    return c


def rope_table(pos):
    inv = (10000.0 ** (-np.arange(0, 64, 2, dtype=np.float32) / np.float32(64))).astype(np.float32)
    ang = (pos.astype(np.float32)[:, None] * inv[None, :]).astype(np.float32)
    cos, sin = np.cos(ang).astype(np.float32), np.sin(ang).astype(np.float32)
    cc = np.concatenate([cos, cos], axis=1).T
    ss = np.concatenate([-sin, sin], axis=1).T
    t = np.stack([np.concatenate([cc, cc], 0), np.concatenate([ss, ss], 0)], axis=1)
    return np.ascontiguousarray(t, dtype=np.float32)


def od_shared(w):
    f = lambda a: np.ascontiguousarray(a, dtype=np.float32)
    m = {}
    wi = w["od_w_in"][0]
    qcols = np.concatenate([np.concatenate([np.arange(r * 64, r * 64 + 64), np.arange((4 + r) * 64, (4 + r) * 64 + 64)])
                            for r in range(4)])
    swap = lambda cols: np.concatenate([np.concatenate([cols[i * 64 + 32:i * 64 + 64], cols[i * 64:i * 64 + 32]])
                                        for i in range(len(cols) // 64)])
    kcols = np.arange(512, 640)
    order = np.concatenate([qcols, np.arange(512, 3424), swap(qcols), swap(kcols)])
    m["od_w_in"] = f(wi[:, order])
    m["od_w_mod"] = f(w["od_w_mod"][0])
    m["od_b_modT"] = f(w["od_b_mod"][0].reshape(24, 128).T)
    m["od_b_mod"] = f(w["od_b_mod"][0].reshape(1, -1))
    m["od_rows"] = f(np.concatenate([np.zeros(D, np.float32), w["od_ln_g"][0], w["od_ln_b"][0]]).reshape(1, -1))
    m["od_w_out"] = f(w["od_w_out"][0])
    m["consts2"] = consts2()
    mix = w["rwkv_mix"][0]
    mt = np.zeros((128, 2, 19), np.float32)
    starts = [0, 128, 256, 384, 512, 640, 768, 896, 1024, 1152, 1280, 1408, 1632, 1760, 1888, 2016, 1536, 1568, 1600]
    for j, st in enumerate(starts):
        n_ = 32 if j >= 16 else 128
        mt[0:n_, :, j] = mix[:, st:st + n_].T
    m["mixT"] = mt
    vecs = [w["rwkv_a0"][0], w["rwkv_k_k"][0], w["rwkv_k_a"][0], w["rwkv_r_k"][0], w["rwkv_ln_w"][0], w["rwkv_ln_b"][0],
            w["rwkv_w0"][0, 0], w["rwkv_w0"][0, 1]]
    m["vecT"] = f(np.stack([v.reshape(4, 128) for v in vecs], axis=-1).transpose(1, 0, 2))
    m["w_w2"] = f(w["rwkv_w_w2"][0].transpose(1, 0, 2))
    m["a_w2"] = f(w["rwkv_a_w2"][0])
    m["sink"] = f(w["swa_sink"][0].reshape(1, 8))
    return m


_NC_CACHE = {}


def kernel(**inputs):
    NSEG, L = 4, 4096
    w = {k: np.asarray(v) for k, v in inputs.items()}
    xp, xs, cp, cs = w["x_prompt"], w["x_sample"], w["c_prompt"], w["c_sample"]
    shared = shared_inputs(w)
    shared.update(od_shared(w))
    plan = []
    for b in range(2):
        plan.append([("p", b, k) for k in range(4)])
    samp = [[0, 1, 2], [3, 4, 5], [6, 7, 8], [9, 10, 11], [12, 13], [14, 15]]
    for lst in samp:
        plan.append([("s", i, 0) for i in lst] + [("s", lst[0], 0)] * (4 - len(lst)))
    in_maps = []
    for core in range(8):
        m = dict(shared)
        xs_, cs_, pos = [], [], []
        fl = np.zeros((128, 2 * NSEG), np.float32)
        for s_, (kind, b, k) in enumerate(plan[core]):
            if kind == "p":
                xs_.append(xp[b, k * L:(k + 1) * L])
                cs_.append(cp[b])
                pos.append(np.arange(k * L, (k + 1) * L, dtype=np.float32))
                if k > 0:
                    fl[:, s_] = 1.0
                if k < 3:
                    fl[:, NSEG + s_] = 1.0
            else:
                xs_.append(xs[b])
                cs_.append(cs[b])
                pos.append(np.arange(L, dtype=np.float32))
        m["x"] = np.ascontiguousarray(np.concatenate(xs_, axis=0), dtype=np.float32)
        cc = np.stack(cs_, axis=0)
        m["cT"] = np.ascontiguousarray(cc.reshape(NSEG, 8, 128).transpose(2, 1, 0).reshape(128, 8 * NSEG), dtype=np.float32)
        m["flags"] = fl
        m["ropeT"] = rope_table(np.concatenate(pos))
        in_maps.append(m)
    if "nc" not in _NC_CACHE:
        _NC_CACHE["nc"] = build(NSEG, L)
    res = run_bass_kernel_spmd(_NC_CACHE["nc"], in_maps, core_ids=list(range(8)))
    yp = np.zeros_like(xp)
    ys = np.zeros_like(xs)
    for core in range(8):
        y = res.results[core]["y"].reshape(NSEG, L, D)
        seen = set()
        for s_, (kind, b, k) in enumerate(plan[core]):
            if kind == "p":
                yp[b, k * L:(k + 1) * L] = y[s_]
            elif b not in seen:
                ys[b] = y[s_]
                seen.add(b)
    return (yp, ys)
```

```python
import numpy as np
from contextlib import ExitStack
import concourse.bass as bass
import concourse.mybir as mybir
from concourse.bass_utils import run_bass_kernel_spmd
from concourse.alu_op_type import AluOpType as ALU

F32 = mybir.dt.float32
BF16 = mybir.dt.bfloat16
AF = mybir.ActivationFunctionType
D = 1024
ALPHA = 4 ** 0.25
NDMA = 24


class Tl:
    def __init__(self, t, name):
        self.t, self.name, self.w, self.r = t, name, {}, {}

    def __getitem__(self, idx):
        v = Vw(self, self.t[idx])
        i0 = idx[0] if isinstance(idx, tuple) else idx
        if isinstance(i0, slice) and i0.start is not None:
            v.p0, v.pn = i0.start, (i0.stop - i0.start)
        return v


class Vw:
    def __init__(self, tile, ap):
        self.tile, self.ap = tile, ap
        self.p0, self.pn = 0, 128

    def __getitem__(self, idx):
        v = Vw(self.tile, self.ap[idx])
        v.p0, v.pn = self.p0, self.pn
        i0 = idx[0] if isinstance(idx, tuple) else idx
        if isinstance(i0, slice) and i0.start is not None:
            v.p0, v.pn = self.p0 + i0.start, (i0.stop - i0.start)
        return v


class Op:
    __slots__ = ("eng", "fn", "deps", "sig", "sigval", "idx", "dma_k")

    def __init__(self, eng, fn, idx):
        self.eng, self.fn, self.idx = eng, fn, idx
        self.deps, self.sig, self.sigval, self.dma_k = set(), False, 0, -1


class Prog:
    ENGS = ("pe", "act", "dve", "pool")

    def __init__(self, nc):
        self.nc, self.ops, self.ndma = nc, [], 0

    def add(self, eng, fn, reads=(), writes=()):
        idx = len(self.ops)
        op = Op(eng, fn, idx)
        isdma = eng == "sp"
        key = ("dma", idx) if isdma else eng
        raw, oth = set(), set()
        for v in reads:
            if v is None or not isinstance(v, Vw):
                continue
            raw.update(v.tile.w.values())
            if getattr(v.tile, "psum", False):
                oth.update(i for k, i in v.tile.r.items() if k != key)
        for v in writes:
            oth.update(v.tile.w.values())
            oth.update(v.tile.r.values())
        for d in raw:
            p = self.ops[d]
            if p.eng == eng and eng == "pe":
                continue
            op.deps.add(d)
        for d in oth:
            p = self.ops[d]
            if p.eng == eng and not isdma:
                continue
            op.deps.add(d)
        for v in reads:
            if v is None or not isinstance(v, Vw):
                continue
            v.tile.r[key] = idx
        for v in writes:
            tl = v.tile
            tl.r = {}
            if isdma:
                tl.w = {k: i for k, i in tl.w.items() if not isinstance(k, tuple)}
            tl.w[key] = idx
        if isdma:
            op.dma_k = self.ndma
            self.ndma += 1
        self.ops.append(op)
        return op

    def setup(self, es):
        nc = self.nc
        self.sems = {e: es.enter_context(nc.semaphore("s_" + e)) for e in self.ENGS}
        self.dsem = [es.enter_context(nc.semaphore("d%d" % i)) for i in range(NDMA)]
        self.cnt = {e: 0 for e in self.ENGS}
        self.waited = {e: {} for e in self.ENGS + ("sp",)}
        self.done = 0

    def emit(self):
        nc = self.nc
        ops, sems, dsem = self.ops, self.sems, self.dsem
        lo = self.done
        for op in ops[lo:]:
            op.deps = {d for d in op.deps if d >= lo}
            for d in op.deps:
                ops[d].sig = True
        for op in ops[lo:]:
            if op.eng != "sp" and op.sig:
                self.cnt[op.eng] += 1
                op.sigval = self.cnt[op.eng]

        def need(op):
            req = {}
            for d in op.deps:
                p = ops[d]
                if p.eng == "sp":
                    s, v = dsem[p.dma_k % NDMA], 16 * (p.dma_k // NDMA + 1)
                else:
                    s, v = sems[p.eng], p.sigval
                k = id(s)
                if k not in req or req[k][1] < v:
                    req[k] = (s, v)
            return req

        def run(engname, e):
            waited = self.waited[engname]
            for op in ops[lo:]:
                if op.eng != engname:
                    continue
                req = need(op)
                if engname == "sp" and op.dma_k >= NDMA:
                    s = dsem[op.dma_k % NDMA]
                    v = 16 * (op.dma_k // NDMA)
                    if id(s) not in req or req[id(s)][1] < v:
                        req[id(s)] = (s, v)
                for k, (s, v) in req.items():
                    if waited.get(k, 0) < v:
                        e.wait_ge(s, v)
                        waited[k] = v
                ins = op.fn(e)
                if engname == "sp":
                    ins.then_inc(dsem[op.dma_k % NDMA], 16)
                elif op.sig:
                    ins.then_inc(sems[engname], 1)
            if engname == "sp":
                for i in range(min(NDMA, self.ndma)):
                    last = ((self.ndma - 1 - i) // NDMA) * NDMA + i
                    v = 16 * (last // NDMA + 1)
                    if waited.get(id(dsem[i]), 0) < v:
                        e.wait_ge(dsem[i], v)
                        waited[id(dsem[i])] = v

        with nc.Block() as block:
            @block.sync
            def _(e):
                run("sp", e)

            @block.tensor
            def _(e):
                run("pe", e)

            @block.scalar
            def _(e):
                run("act", e)

            @block.vector
            def _(e):
                run("dve", e)

            @block.gpsimd
            def _(e):
                run("pool", e)
        self.done = len(ops)


def _ap(v):
    return v.ap if isinstance(v, Vw) else v


class K:
    def __init__(self, nc, es):
        self.nc, self.es, self.P = nc, es, Prog(nc)
        self.n = 0

    def sb(self, shape, dt=F32, name=None):
        self.n += 1
        nm = "%s%d" % (name or "t", self.n)
        return Tl(self.es.enter_context(self.nc.sbuf_tensor(nm, list(shape), dt)), nm)

    def ps(self, shape, name=None):
        self.n += 1
        nm = "%s%d" % (name or "p", self.n)
        t = Tl(self.es.enter_context(self.nc.psum_tensor(nm, list(shape), F32)), nm)
        t.psum = True
        return t

    def mm(self, out, lhsT, rhs, start=True, stop=True):
        rg = (lhsT.p0, lhsT.pn, out.p0, out.pn)
        last = getattr(self, "last_rg", (0, 128, 0, 128))
        tiled = lambda g: g[1] < 128 or g[3] < 128
        if rg != last and (tiled(rg) or tiled(last)) and getattr(self, "dummy", None) is not None:
            dps, dl, dr_ = self.dummy
            self.P.add("pe", lambda e: e.matmul(dps.ap, dl.ap, dr_.ap, start=True, stop=True), [dl], [dps])
        self.last_rg = rg
        self.P.add("pe", lambda e: e.matmul(out.ap, lhsT.ap, rhs.ap, start=start, stop=stop),
                   [lhsT, rhs], [out])

    def act(self, out, in_, func, scale=1.0, bias=0.0, accum=None, eng="act"):
        kw = {}
        if accum is not None:
            kw["accum_out"] = accum.ap
        self.P.add(eng, lambda e: e.activation(out.ap, in_.ap, func, bias=_ap(bias), scale=_ap(scale), **kw),
                   [in_, scale, bias], [out] + ([accum] if accum is not None else []))

    def tt(self, out, a, b, op, eng="dve"):
        self.P.add(eng, lambda e: e.tensor_tensor(out.ap, a.ap, b.ap, op), [a, b], [out])

    def ts(self, out, a, s1, s2, op0, op1=None, eng="dve"):
        if op1 is None:
            self.P.add(eng, lambda e: e.tensor_scalar(out.ap, a.ap, _ap(s1), None, op0), [a, s1], [out])
        else:
            self.P.add(eng, lambda e: e.tensor_scalar(out.ap, a.ap, _ap(s1), _ap(s2), op0, op1), [a, s1, s2], [out])

    def stt(self, out, a, s, b, op0, op1):
        self.P.add("dve", lambda e: e.scalar_tensor_tensor(out.ap, a.ap, _ap(s), b.ap, op0, op1), [a, s, b], [out])

    def scan(self, out, d0, d1, init, op0, op1):
        self.P.add("dve", lambda e: e.tensor_tensor_scan(out.ap, d0.ap, d1.ap, init, op0, op1), [d0, d1], [out])

    def copy(self, out, in_, eng="dve"):
        if eng == "act":
            self.P.add("act", lambda e: e.copy(out.ap, in_.ap), [in_], [out])
        else:
            self.P.add(eng, lambda e: e.tensor_copy(out.ap, in_.ap), [in_], [out])

    def memset(self, out, val, eng="dve"):
        self.P.add(eng, lambda e: e.memset(out.ap, val), [], [out])

    def sig(self, out, in_, nbias=0.0, e_out=None):
        e = e_out if e_out is not None else out
        self.act(e, in_, AF.Exp, scale=-1.0, bias=nbias)
        self.ts(out, e, 1.0, None, ALU.add)
        self.recip(out, out)

    def rsqrt(self, out, in_, scale, eps):
        self.act(out, in_, AF.Ln, scale=scale, bias=eps)
        self.act(out, out, AF.Exp, scale=-0.5)

    def recip(self, out, in_):
        self.P.add("dve", lambda e: e.reciprocal(out.ap, in_.ap), [in_], [out])

    def dma(self, out, in_):
        self.P.add("sp", lambda e: e.dma_start(out=out.ap, in_=in_.ap), [in_], [out])


class Rot:
    def __init__(self, items):
        self.items, self.i = items, 0

    def nxt(self):
        self.i += 1
        return self.items[(self.i - 1) % len(self.items)]


EV_COLS = dict(aq=0, ai=512, af_f=1024, af_b=1536, a_gate=2048, bq=2560, bk=2816, bv=3072, bl_f=3584, bl_b=3600,
               b_gate=3616, zf=4128, zb=4384)
WC = 4640


def build(NSEG, L, layers=(0, 1), LDT=BF16, use_pool=True):
    nc = bass.Bass("TRN2", target_bir_lowering=False)
    es = ExitStack()
    kb = K(nc, es)
    TT_ = NSEG * L
    NT = TT_ // 128
    TPS = L // 128

    def din(name, shape):
        return Tl(nc.dram_tensor(name, list(shape), F32, kind="ExternalInput").ap(), name)

    x_d = nc.dram_tensor("x", [TT_, D], F32, kind="ExternalInput").ap()
    y_d = nc.dram_tensor("y", [TT_, D], F32, kind="ExternalOutput").ap()
    x1_d = nc.dram_tensor("x1s", [TT_, D], F32, kind="Internal").ap()
    ob_d = nc.dram_tensor("obs", [TT_, D], F32, kind="Internal").ap()
    xT = [Tl(x_d, "x%d" % i) for i in range(NT)]
    yT = [Tl(y_d, "y%d" % i) for i in range(NT)]
    x1T = [Tl(x1_d, "x1%d" % i) for i in range(NT)]
    obT = [Tl(ob_d, "ob%d" % i) for i in range(NT)]
    rows = lambda n: slice(n * 128, (n + 1) * 128)

    cT_d = din("cT", [128, 8 * NSEG])
    flg_d = din("flags", [128, 2 * NSEG])
    cst_d = din("consts", [128, 4 * 128])
    ev = {}
    for nm, shp in [("ev_w_mod", [D, 3 * D]), ("ev_b_modT", [128, 24]), ("ev_b_mod", [1, 3 * D]), ("ev_w_in", [D, 4128]),
                    ("ev_w_blT", [16, 2 * D]), ("gla_w_gk", [16, 512]), ("gla_b_gkT", [128, 4]), ("lblT", [128, 16]),
                    ("ev_rows", [1, 3 * D]), ("ev_w_out", [D, D])]:
        ev[nm] = din(nm, shp)

    od = {}
    for nm, shp in [("od_w_mod", [D, 3 * D]), ("od_b_modT", [128, 24]), ("od_b_mod", [1, 3 * D]), ("od_w_in", [D, 4064]),
                    ("od_rows", [1, 3 * D]), ("od_w_out", [D, D]), ("consts2", [128, 6 * 512 + 256]), ("mixT", [128, 2, 19]),
                    ("vecT", [128, 4, 8]), ("w_w2", [32, 2, 512]), ("a_w2", [32, 512]), ("sink", [1, 8]),
                    ("ropeT", [128, 2, TT_])]:
        od[nm] = din(nm, shp)

    with es:
        kb.P.setup(es)
        cst = kb.sb([128, 512], F32, "cst")
        kb.dma(cst[:], cst_d[:, :])
        ident, Jm, mask, ones = cst[:, 0:128], cst[:, 128:256], cst[:, 256:384], cst[:, 384:512]
        flg = kb.sb([128, 2 * NSEG], F32, "flg")
        kb.dma(flg[:], flg_d[:, :])
        cT = kb.sb([128, 8 * NSEG], F32, "cT")
        kb.dma(cT[:], cT_d[:, :])
        scT = kb.sb([128, 8 * NSEG], F32, "scT")
        kb.sig(scT[:], cT[:])
        kb.tt(scT[:], scT[:], cT[:], ALU.mult)
        zer = kb.sb([128, 128], F32, "zer")
        kb.memset(zer[:], 0.0)

        PD = Rot([kb.ps([128, 1024], "pd") for _ in range(2)])
        PS = Rot([kb.ps([128, 512], "ps") for _ in range(3)])
        psd = kb.ps([128, 512], "psd")
        identg = kb.sb([128, 128], BF16, "identg")
        kb.copy(identg[:], ident)
        kb.dummy = (psd[:, 0:2], identg[:], identg[:, 0:2])

        WH = {}
        Wout = kb.sb([128, 8, D], BF16, "Wout")
        modT = kb.sb([128, 24 * NSEG], F32, "modT")
        bmT = kb.sb([128, 24], F32, "bmT")
        g1b = kb.sb([128, NSEG, D], BF16, "g1b")

        tmpR = {}

        TMPN = [2]

        def tmp(nm, shape=(128, 128), n=None, dt=F32):
            n = n or TMPN[0]
            if nm not in tmpR:
                tmpR[nm] = Rot([kb.sb(shape, dt, nm) for _ in range(n)])
            return tmpR[nm].nxt()

        def load_weights(w_in_d, ncols, w_out_d, w_mod_d, b_modT_d, b_mod_d, rows_d, extra=None, ro=0):
            WH["ro"] = ro
            ses = ExitStack()
            old_es, old_tmp = kb.es, dict(tmpR)
            kb.es = ses
            stg = Rot([kb.sb([128, 2048], F32, "stg") for _ in range(2)])
            brow = kb.sb([1, 3 * D], F32, "brow")
            rrow = kb.sb([1, 3 * D], F32, "rrow")
            for k in range(8):
                for c0 in range(0, ncols, 2048):
                    c1 = min(ncols, c0 + 2048)
                    s = stg.nxt()
                    kb.dma(s[:, 0:c1 - c0], w_in_d[k * 128:(k + 1) * 128, c0:c1])
                    kb.copy(WH["Win"][:, k, c0:c1], s[:, 0:c1 - c0], eng="pool" if (k % 2) else "dve")
                s = stg.nxt()
                kb.dma(s[:, 0:D], w_out_d[k * 128:(k + 1) * 128, :])
                kb.copy(Wout[:, k, :], s[:, 0:D], eng="act")
            kb.dma(bmT[:], b_modT_d[:, :])
            kb.dma(brow[:], b_mod_d[:, :])
            kb.dma(rrow[:], rows_d[:, :])
            pm = PD.nxt()
            for jb in range(12):
                wm = stg.nxt()
                for k in range(8):
                    kb.dma(wm[:, k * 256:(k + 1) * 256], w_mod_d[k * 128:(k + 1) * 128, jb * 256:(jb + 1) * 256])
                for jj in range(2):
                    j = jb * 2 + jj
                    for k in range(8):
                        kb.mm(pm[:, j * NSEG:(j + 1) * NSEG], wm[:, k * 256 + jj * 128:k * 256 + (jj + 1) * 128],
                              scT[:, k * NSEG:(k + 1) * NSEG], start=(k == 0), stop=(k == 7))
            for j in range(24):
                kb.act(modT[:, j * NSEG:(j + 1) * NSEG], pm[:, j * NSEG:(j + 1) * NSEG], AF.Identity,
                       bias=bmT[:, j:j + 1], scale=1.0)
            kb.ts(modT[:, 8 * NSEG:16 * NSEG], modT[:, 8 * NSEG:16 * NSEG], 1.0, None, ALU.add)
            for s_ in range(NSEG):
                pg = PD.nxt()
                for q4 in range(4):
                    wm = stg.nxt()
                    for k in range(8):
                        kb.dma(wm[:, k * 256:(k + 1) * 256], w_mod_d[k * 128:(k + 1) * 128, 2 * D + q4 * 256:2 * D + (q4 + 1) * 256])
                    for k in range(8):
                        scB = tmp("scB")
                        kb.act(scB[:], zer[:], AF.Identity, bias=scT[:, k * NSEG + s_:k * NSEG + s_ + 1], scale=1.0)
                        kb.mm(pg[:, q4 * 256:(q4 + 1) * 256], scB[:], wm[:, k * 256:(k + 1) * 256], start=(k == 0), stop=False)
                    kb.mm(pg[:, q4 * 256:(q4 + 1) * 256], ones[0:1, :], brow[0:1, 2 * D + q4 * 256:2 * D + (q4 + 1) * 256],
                          start=False, stop=True)
                kb.ts(g1b[:, s_, :], pg[:], 1.0, None, ALU.add)
            for r in range(ro, 3):
                pg = PD.nxt()
                for hf in range(2):
                    kb.mm(pg[:, hf * 512:(hf + 1) * 512], ones[0:1, :], rrow[0:1, r * D + hf * 512:r * D + (hf + 1) * 512])
                kb.copy(WH["rowsb"][:, r - ro, :], pg[:], eng="act")
            if extra is not None:
                extra()
            kb.P.emit()
            ses.close()
            kb.es = old_es
            tmpR.clear()
            tmpR.update(old_tmp)

        NB = 2
        xtR = Rot([kb.sb([128, D], F32, "xt") for _ in range(2)])
        hTR = Rot([kb.sb([128, 8, 128], BF16, "hT") for _ in range(NB)])
        from contextlib import contextmanager

        @contextmanager
        def layer_scope():
            ses = ExitStack()
            old_es, old_tmp = kb.es, dict(tmpR)
            kb.es = ses
            try:
                yield
            finally:
                kb.P.emit()
                ses.close()
                kb.es = old_es
                tmpR.clear()
                tmpR.update(old_tmp)

        EVT = {}

        def gla_unit(hT, dr, flip, psO, ocol, V, vcol, qsrc, ksrc, zsrc, S, sidx, heads, kind):
            lbT, bgk = EVT["lbT"], EVT["bgk"]
            def proj(col):
                p = PS.nxt()
                for k in range(8):
                    kb.mm(p[:, 0:128], WH["Win"][:, k, col:col + 128], hT[:, k, :], start=(k == 0), stop=(k == 7))
                return p
            pq = proj(qsrc)
            q = tmp("q")
            sn = tmp("sn")
            lf = tmp("lf")
            if kind == "A":
                h = sidx
                kb.sig(q[:], pq[:, 0:128])
                kb.tt(q[:], q[:], pq[:, 0:128], ALU.mult)
                pf = proj(zsrc)
                sg = tmp("sg")
                ee = tmp("ee")
                kb.sig(sg[:], pf[:, 0:128], e_out=ee[:])
                kb.act(lf[:], sg[:], AF.Ln, scale=lbT[:, dr * 8 + 4 + h:dr * 8 + 5 + h], bias=lbT[:, dr * 8 + h:dr * 8 + h + 1])
                kb.stt(sn[:], ee[:], lbT[:, dr * 8 + 4 + h:dr * 8 + 5 + h], sg[:], ALU.mult, ALU.mult)
                esc = 1.0
                sop = ALU.add
            else:
                p_ = sidx
                kb.act(q[:], pq[:, 0:128], AF.Identity, scale=0.125)
                pk = proj(ksrc)
                kb.copy(sn[:], pk[:, 0:128], eng="act")
                pf = proj(zsrc)
                sg = tmp("sg")
                kb.act(sg[:], pf[:, 0:128], AF.Exp, scale=-1.0, bias=EVT["nbgk"][:, dr * 2 + p_:dr * 2 + p_ + 1])
                kb.act(lf[:], sg[:], AF.Ln, bias=1.0)
                esc = 1.0 / 16.0
                sop = ALU.subtract
            B = tmp("B")
            kb.scan(B[:], ones, lf[:], 0.0, ALU.mult, sop)
            nb = tmp("nb", (128, 4))
            kb.ts(nb[:, 0:1], B[:, 63:64], -esc, None, ALU.mult)
            kb.ts(nb[:, 1:2], B[:, 63:64], esc, None, ALU.mult)
            kb.ts(nb[:, 2:3], B[:, 127:128], esc, None, ALU.mult)
            E1, E1n, E2, E3 = tmp("E1"), tmp("E1n"), tmp("E2"), tmp("E3")
            kb.act(E1[:], B[:], AF.Exp, scale=esc, bias=nb[:, 0:1])
            kb.act(E1n[:], B[:], AF.Exp, scale=-esc, bias=nb[:, 1:2])
            kb.act(E2[:], B[:], AF.Exp, scale=esc)
            kb.act(E3[:], B[:], AF.Exp, scale=-esc, bias=nb[:, 2:3])
            qd, qi, kd, ke = tmp("qd"), tmp("qi"), tmp("kd"), tmp("ke")
            kb.tt(qd[:], q[:], E1[:], ALU.mult, eng="pool")
            qi_o = qi[:, ::-1] if flip else qi[:]
            kb.tt(qi_o, q[:], E2[:], ALU.mult)
            kb.tt(kd[:], sn[:], E1n[:], ALU.mult, eng="pool")
            kb.tt(ke[:], sn[:], E3[:], ALU.mult)
            nh = len(heads)
            dk = 128 // nh

            def partB():
                pT = PS.nxt()
                kb.mm(pT[:, 0:128], ke[:], ident)
                keT = tmp("keT")
                kb.copy(keT[:], pT[:, 0:128], eng="act")
                for hi, hd in enumerate(heads):
                    pr = slice(hi * dk, (hi + 1) * dk)
                    pA = PS.nxt()
                    kb.mm(pA[:, 0:128], kd[pr, :], qd[pr, :])
                    A = tmp("A")
                    A_o = A[:, ::-1] if flip else A[:]
                    kb.tt(A_o, pA[:, 0:128], mask, ALU.mult)
                    oc = slice(ocol + hd * 128, ocol + (hd + 1) * 128)
                    vc = slice(vcol + hd * 128, vcol + (hd + 1) * 128)
                    Sh = S[pr, hd, :]
                    kb.mm(psO[:, oc], A[:], V[:, vc], start=True, stop=False)
                    kb.mm(psO[:, oc], qi[pr, :], Sh, start=False, stop=True)
                    pS_ = PS.nxt()
                    kb.mm(pS_[pr, 0:128], keT[:, pr], V[:, vc])
                    kb.stt(Sh, Sh, E2[pr, 127:128], pS_[pr, 0:128], ALU.mult, ALU.add)
            return partB

        def even_sweep(dr):
            S_A, S_B, oacc, VR = EVT["S_A"], EVT["S_B"], EVT["oacc"], EVT["VR"]
            flip = dr == 1
            order = range(NT - 1, -1, -1) if flip else range(NT)
            xsrc = xT
            kb.memset(S_A[:], 0.0)
            kb.memset(S_B[:], 0.0)
            PDs = PD.items
            order = list(order)

            def stageA(n):
                seg = n // TPS
                xt = xtR.nxt()
                kb.dma(xt[:], xsrc[n][rows(n), :])
                pX = PDs[1]
                for k in range(8):
                    kb.mm(pX[:, k * 128:(k + 1) * 128], xt[:, k * 128:(k + 1) * 128], Jm if flip else ident)
                hT = hTR.nxt()
                for k in range(8):
                    kb.act(hT[:, k, :], pX[:, k * 128:(k + 1) * 128], AF.Identity,
                           scale=modT[:, (8 + k) * NSEG + seg:(8 + k) * NSEG + seg + 1],
                           bias=modT[:, k * NSEG + seg:k * NSEG + seg + 1])
                V = VR.nxt()
                pV = PDs[1]
                for hf, col in enumerate((EV_COLS["ai"], EV_COLS["bv"])):
                    for k in range(8):
                        kb.mm(pV[:, hf * 512:(hf + 1) * 512], hT[:, k, :], WH["Win"][:, k, col:col + 512],
                              start=(k == 0), stop=(k == 7))
                kb.copy(V[:], pV[:], eng="act")
                return dict(xt=xt, hT=hT, V=V, seg=seg)

            nxtA = stageA(order[0])
            for i_n, n in enumerate(order):
                cur = nxtA
                seg, xt, hT, V = cur["seg"], cur["xt"], cur["hT"], cur["V"]
                first = (n % TPS == (TPS - 1 if flip else 0))
                if first:
                    fc = flg[:, (NSEG if flip else 0) + seg:(NSEG if flip else 0) + seg + 1]
                    kb.ts(S_A[:], S_A[:], fc, None, ALU.mult)
                    kb.ts(S_B[:], S_B[:], fc, None, ALU.mult)
                psO = PDs[0]
                pend = None
                for u in range(6):
                    if u < 4:
                        h = u
                        b_ = gla_unit(hT, dr, flip, psO, 0, V, 0, EV_COLS["aq"] + h * 128, None,
                                      EV_COLS["af_b" if flip else "af_f"] + h * 128, S_A, h, [h], "A")
                    else:
                        p_ = u - 4
                        b_ = gla_unit(hT, dr, flip, psO, 512, V, 512, EV_COLS["bq"] + p_ * 128, EV_COLS["bk"] + p_ * 128,
                                      EV_COLS["zb" if flip else "zf"] + p_ * 128, S_B, p_, [2 * p_, 2 * p_ + 1], "B")
                    if pend is not None:
                        pend()
                    pend = b_
                    if u == 3 and i_n + 1 < len(order):
                        nxtA = stageA(order[i_n + 1])
                pend()
                o = oacc.nxt()
                if flip:
                    kb.copy(o[:], psO[:], eng="act")
                    kb.dma(obT[n][rows(n), :], o[:])
                    continue
                ob = tmp("ob", (128, D), 1)
                kb.dma(ob[:], obT[n][rows(n), :])
                kb.tt(o[:], psO[:], ob[:], ALU.add)
                ss = tmp("ss", (128, 8))
                sq = tmp("sq", (128, 128))
                for h8 in range(8):
                    kb.act(sq[:], o[:, h8 * 128:(h8 + 1) * 128], AF.Square, accum=ss[:, h8:h8 + 1])
                rs = tmp("rs", (128, 8))
                kb.rsqrt(rs[:], ss[:], 1.0 / 128.0, 1e-6)
                pG = PDs[1]
                for hf, col in enumerate((EV_COLS["a_gate"], EV_COLS["b_gate"])):
                    for k in range(8):
                        kb.mm(pG[:, hf * 512:(hf + 1) * 512], hT[:, k, :], WH["Win"][:, k, col:col + 512],
                              start=(k == 0), stop=(k == 7))
                sgt = tmp("sgt", (128, D), 1)
                kb.sig(sgt[:], pG[:])
                on = o
                for h8 in range(8):
                    cs = slice(h8 * 128, (h8 + 1) * 128)
                    kb.stt(on[:, cs], o[:, cs], rs[:, h8:h8 + 1], WH["rowsb"][:, 0, cs], ALU.mult, ALU.mult)
                kb.tt(on[:], on[:], sgt[:], ALU.mult, eng="pool")
                kb.tt(on[:], on[:], pG[:], ALU.mult)
                finish(n, seg, on, xt, x1T, PDs)

        def finish(n, seg, on, xt, dstT, PDs):
            pT = PDs[0]
            for k in range(8):
                kb.mm(pT[:, k * 128:(k + 1) * 128], on[:, k * 128:(k + 1) * 128], ident)
            onT = tmp("onT", (128, 8, 128), 1, dt=BF16)
            kb.copy(onT[:, 0:4, :], pT[:, 0:512], eng="act")
            kb.copy(onT[:, 4:8, :], pT[:, 512:1024], eng="dve")
            finish2(n, seg, onT, xt, dstT, PDs[1])

        def finish2(n, seg, onT, xt, dstT, pY=None):
            if pY is None:
                pY = PD.nxt()
            for hf in range(2):
                for k in range(8):
                    kb.mm(pY[:, hf * 512:(hf + 1) * 512], onT[:, k, :], Wout[:, k, hf * 512:(hf + 1) * 512],
                          start=(k == 0), stop=(k == 7))
            kb.tt(pY[:], pY[:], g1b[:, seg, :], ALU.mult)
            r = xt
            kb.stt(r[:], xt[:], ALPHA, pY[:], ALU.mult, ALU.add)
            st = tmp("st", (128, 12))
            kb.P.add("dve", lambda e: e.bn_stats(st.t[:, 0:6], r.t[:, 0:512]), [r[:]], [st[:]])
            kb.P.add("dve", lambda e: e.bn_stats(st.t[:, 6:12], r.t[:, 512:1024]), [r[:]], [st[:]])
            mv = tmp("mv", (128, 4))
            kb.P.add("dve", lambda e: e.bn_aggr(mv.t[:, 0:2], st.t[:, 0:12]), [st[:]], [mv[:]])
            kb.rsqrt(mv[:, 3:4], mv[:, 1:2], 1.0, 1e-5)
            yo = r
            ro = WH.get("ro", 0)
            kb.ts(yo[:], r[:], mv[:, 0:1], mv[:, 3:4], ALU.subtract, ALU.mult)
            kb.tt(yo[:], yo[:], WH["rowsb"][:, 1 - ro, :], ALU.mult, eng="pool")
            kb.tt(yo[:], yo[:], WH["rowsb"][:, 2 - ro, :], ALU.add)
            kb.dma(dstT[n][rows(n), :], yo[:])

        if 0 in layers:
          with layer_scope():
            WH["Win"] = kb.sb([128, 8, WC], BF16, "Win")
            WH["rowsb"] = kb.sb([128, 3, D], F32, "rowsb")
            S_A = EVT["S_A"] = kb.sb([128, 4, 128], F32, "S_A")
            S_B = EVT["S_B"] = kb.sb([128, 4, 128], F32, "S_B")
            lbT = EVT["lbT"] = kb.sb([128, 16], F32, "lbT")
            bgk = EVT["bgk"] = kb.sb([128, 4], F32, "bgk")
            EVT["oacc"] = Rot([kb.sb([128, D], F32, "oacc") for _ in range(1)])
            EVT["VR"] = Rot([kb.sb([128, D], F32, "V") for _ in range(2)])
            def ev_extra():
                wbl = kb.sb([16, 2 * D], F32, "wbl")
                w2 = kb.sb([16, 512], F32, "w2")
                kb.dma(wbl[:], ev["ev_w_blT"][:, :])
                kb.dma(w2[:], ev["gla_w_gk"][:, :])
                for dr in range(2):
                    for k in range(8):
                        p = PS.nxt()
                        kb.mm(p[:, 0:256], wbl[:, dr * D + k * 128:dr * D + (k + 1) * 128], w2[:, dr * 256:(dr + 1) * 256])
                        c0 = EV_COLS["zf"] + dr * 256
                        kb.copy(WH["Win"][:, k, c0:c0 + 256], p[:, 0:256], eng="act")
            load_weights(ev["ev_w_in"], 4128, ev["ev_w_out"], ev["ev_w_mod"], ev["ev_b_modT"], ev["ev_b_mod"], ev["ev_rows"], ev_extra)
            lbl = kb.sb([128, 16], F32, "lbl")
            kb.dma(lbl[:], ev["lblT"][:, :])
            for dr in range(2):
                dlt = tmp("dlt", (128, 4))
                kb.tt(dlt[:], lbl[:, dr * 8:dr * 8 + 4], lbl[:, dr * 8 + 4:dr * 8 + 8], ALU.subtract)
                kb.sig(lbT[:, dr * 8:dr * 8 + 4], dlt[:])
                kb.ts(lbT[:, dr * 8 + 4:dr * 8 + 8], lbT[:, dr * 8:dr * 8 + 4], -1.0, 1.0, ALU.mult, ALU.add)
            kb.dma(bgk[:], ev["gla_b_gkT"][:, :])
            EVT["nbgk"] = kb.sb([128, 4], F32, "nbgk")
            kb.ts(EVT["nbgk"][:], bgk[:], -1.0, None, ALU.mult)
            even_sweep(1)
            even_sweep(0)
        if 1 in layers:
          with layer_scope():
            WH["Win"] = kb.sb([128, 8, 4064], BF16, "Win")
            WH["rowsb"] = kb.sb([128, 2, D], F32, "rowsb")
            OD = dict(cq=0, ck=512, cv=640, cg=768, r=1280, k=1792, v=2304, wl_f=2816, wl_b=2848, al=2880, g=2912,
                      rq=3424, rk=3936)
            src = x1T if 0 in layers else xT
            c2 = kb.sb([128, 6 * 512], BF16, "c2")
            c2f = kb.sb([128, 256], F32, "c2f")
            kb.dma(c2f[:], od["consts2"][:, 3072:3328])
            m_su, m_iu, m_sl, eye8 = c2[:, 0:512], c2[:, 512:1024], c2[:, 1024:1536], c2[:, 1536:2048]
            m_ge4, m_le4 = c2[:, 2048:2560], c2[:, 2560:3072]
            oblk, cmk = c2f[:, 0:128], c2f[:, 128:256]
            mixT = kb.sb([128, 3, 19], F32, "mixT")
            kb.dma(mixT[:, 0:2, :], od["mixT"][:, :, :])
            vecT = kb.sb([128, 4, 8], F32, "vecT")
            kb.dma(vecT[:], od["vecT"][:, :, :])
            nw0 = kb.sb([128, 4, 3], F32, "nw0")
            sinkE = kb.sb([128, 8], F32, "sinkE")
            H = kb.sb([128, 4, 64], F32, "H")
            Hb = kb.sb([128, 4, 64], LDT, "Hb")
            identb = kb.sb([128, 128], LDT, "identb")
            oblkb_t = kb.sb([128, 128], LDT, "oblkb")
            oblkb = oblkb_t[:]
            aw2b = kb.sb([32, 512], LDT, "aw2b")
            ww2b = kb.sb([32, 2, 512], LDT, "ww2b")
            zrR = Rot([kb.sb([128, 19, 130], BF16, "zr") for _ in range(3)])
            kTR = Rot([kb.sb([128, 128], BF16, "kT") for _ in range(3)])
            v65R = Rot([kb.sb([128, 2, 65], BF16, "v65") for _ in range(3)])
            hTo = hTR
            xto = xtR
            ropR = Rot([kb.sb([128, 2, 128], F32, "rop") for _ in range(2)])
            obT_d = nc.dram_tensor("obT", [512, TT_], F32, kind="Internal").ap()
            obTT = [Tl(obT_d, "obT%d" % i) for i in range(NT)]
            rope_d = od["ropeT"]

            def od_extra():
                for q6 in range(3):
                    cs_ = kb.sb([128, 1024], F32, "c2s")
                    kb.dma(cs_[:], od["consts2"][:, q6 * 1024:(q6 + 1) * 1024])
                    kb.copy(c2[:, q6 * 1024:(q6 + 1) * 1024], cs_[:])
                kb.copy(identb[:], ident)
                kb.copy(oblkb_t[:], oblk)
                ww2 = kb.sb([32, 2, 512], F32, "ww2")
                kb.dma(ww2[:], od["w_w2"][:, :, :])
                aw2 = kb.sb([32, 512], F32, "aw2")
                kb.dma(aw2[:], od["a_w2"][:, :])
                kb.copy(ww2b[:], ww2[:])
                kb.copy(aw2b[:], aw2[:])
                srow = kb.sb([1, 8], F32, "srow")
                kb.dma(srow[:], od["sink"][:, :])
                p = PS.nxt()
                kb.mm(p[:, 0:8], ones[0:1, :], srow[0:1, :])
                kb.act(sinkE[:], p[:, 0:8], AF.Exp)
                kb.ts(nw0[:, :, 0:2], vecT[:, :, 6:8], -1.0, None, ALU.mult)
                kb.ts(nw0[:, :, 2:3], vecT[:, :, 0:1], -1.0, None, ALU.mult)
                kb.tt(mixT[:, 2, :], mixT[:, 0, :], mixT[:, 1, :], ALU.add)
                kb.ts(mixT[:, 2, :], mixT[:, 2, :], -1.0, 1.0, ALU.mult, ALU.add)
            load_weights(od["od_w_in"], 4064, od["od_w_out"], od["od_w_mod"], od["od_b_modT"], od["od_b_mod"], od["od_rows"], od_extra, ro=1)

            tr = lambda cb: slice(64 * cb, 64 * cb + 64)
            hc = lambda h: slice(64 * h, 64 * h + 64)
            pc = lambda p, cb: slice((p * 2 + cb) * 64, (p * 2 + cb + 1) * 64)
            T5 = lambda nm, dt=F32: tmp(nm, (128, 512), 1, dt=dt)
            LD = LDT
            PL = "pool" if use_pool else "dve"
            TMPN[0] = 1

            def projF(hT, col, m=128):
                p = PS.nxt()
                for k in range(8):
                    kb.mm(p[0:m, 0:128], WH["Win"][:, k, col:col + m], hT[:, k, :], start=(k == 0), stop=(k == 7))
                return p

            def stage1(n, flip, dirn, need_g, attn):
                seg = n // TPS
                xt = xto.nxt()
                kb.dma(xt[:], src[n][rows(n), :])
                pX = PD.nxt()
                for k in range(8):
                    kb.mm(pX[:, k * 128:(k + 1) * 128], xt[:, k * 128:(k + 1) * 128], Jm if flip else ident)
                hT = hTo.nxt()
                for k in range(8):
                    kb.act(hT[:, k, :], pX[:, k * 128:(k + 1) * 128], AF.Identity,
                           scale=modT[:, (8 + k) * NSEG + seg:(8 + k) * NSEG + seg + 1],
                           bias=modT[:, k * NSEG + seg:k * NSEG + seg + 1])
                zr = zrR.nxt()
                groups = [("r", 0), ("k", 4), ("v", 8)] + ([("g", 12)] if need_g else [])
                for gi, (nm, j0) in enumerate(groups):
                    for j in range(4):
                        p = projF(hT, OD[nm] + j * 128)
                        kb.copy(zr[:, j0 + j, 1:129], p[:, 0:128], eng="act" if (j % 2) else "dve")
                wj = 17 if dirn == 1 else 16
                p = projF(hT, OD["wl_b" if dirn == 1 else "wl_f"], 32)
                kb.copy(zr[0:32, wj, 1:129], p[0:32, 0:128], eng="act")
                p = projF(hT, OD["al"], 32)
                kb.copy(zr[0:32, 18, 1:129], p[0:32, 0:128], eng="dve")
                st = dict(n=n, seg=seg, xt=xt, hT=hT, zr=zr)
                if attn:
                    rop = ropR.nxt()
                    kb.dma(rop[:], rope_d[:, :, n * 128:(n + 1) * 128])
                    st["rop"] = rop
                    pk, pkr = projF(hT, OD["ck"]), projF(hT, OD["rk"])
                    t1, t2 = tmp("rt1"), tmp("rt2")
                    kb.tt(t1[:], pk[:, 0:128], rop[:, 0, :], ALU.mult)
                    kb.tt(t2[:], pkr[:, 0:128], rop[:, 1, :], ALU.mult)
                    kT = kTR.nxt()
                    kb.tt(kT[:], t1[:], t2[:], ALU.add)
                    pv = PS.nxt()
                    for k in range(8):
                        kb.mm(pv[:, 0:128], hT[:, k, :], WH["Win"][:, k, OD["cv"]:OD["cv"] + 128], start=(k == 0), stop=(k == 7))
                    v65 = v65R.nxt()
                    kb.memset(v65[:, :, 64:65], 1.0)
                    for g in range(2):
                        kb.copy(v65[:, g, 0:64], pv[:, g * 64:(g + 1) * 64], eng="act")
                    st["kT"], st["v65"] = kT, v65
                return st

            def halo(cur, prv, nxt, flip):
                zr = cur["zr"]
                for nb, dst, srccol in ((prv, 0, 128), (nxt, 129, 1)):
                    if nb is None:
                        kb.memset(zr[:, :, dst:dst + 1], 0.0)
                        continue
                    if nb["seg"] == cur["seg"]:
                        kb.copy(zr[:, :, dst:dst + 1], nb["zr"][:, :, srccol:srccol + 1])
                    else:
                        sg = max(nb["seg"], cur["seg"])
                        kb.ts(zr[:, :, dst:dst + 1], nb["zr"][:, :, srccol:srccol + 1], flg[:, sg:sg + 1], None, ALU.mult)

            def rwkv_tile(cur, flip, dirn, final):
                zr = cur["zr"]
                seg = cur["seg"]
                n = cur["n"]
                mp, mn = (1, 0) if flip else (0, 1)
                zm = tmp("zm", (128, 19, 128), 1, dt=BF16)
                wj = 17 if dirn == 1 else 16
                js = list(range(12)) + (list(range(12, 16)) if final else []) + [wj, 18]
                for j in js:
                    m_ = 32 if j >= 16 else 128
                    kb.act(zm[0:m_, j, :], zr[0:m_, j, 1:129], AF.Identity, scale=mixT[0:m_, 2, j:j + 1])
                    kb.stt(zm[0:m_, j, :], zr[0:m_, j, 0:128], mixT[0:m_, mp, j:j + 1], zm[0:m_, j, :], ALU.mult, ALU.add)
                    kb.stt(zm[0:m_, j, :], zr[0:m_, j, 2:130], mixT[0:m_, mn, j:j + 1], zm[0:m_, j, :], ALU.mult, ALU.add)
                th = tmp("th", (32, 128), 1, dt=LD)
                thf = tmp("thf", (32, 128), 1)
                kb.act(thf[:], zm[0:32, wj, :], AF.Exp, scale=-2.0)
                kb.ts(thf[:], thf[:], 1.0, None, ALU.add)
                kb.recip(thf[:], thf[:])
                kb.ts(th[:], thf[:], 2.0, -1.0, ALU.mult, ALU.add)
                F = {nm: T5("F" + nm, LD) for nm in ("rd", "kd", "bd", "kkd")}
                gam = tmp("gam", (128, 4, 2), 1)
                Tm = {nm: T5("T" + nm, LD) for nm in ("V", "KK", "KE", "NBE")}
                bon = T5("bon") if final else None
                tp = lambda nm, **kw: tmp(nm, n=2, **kw)

                def pair_gen(p):
                    r_, k_, vT = zm[:, p, :], zm[:, 4 + p, :], zm[:, 8 + p, :]
                    alv = zm[0:32, 18, :]
                    vc = lambda i: vecT[:, p, i:i + 1]
                    pa = PS.nxt()
                    kb.mm(pa[:, 0:128], aw2b[:, p * 128:(p + 1) * 128], alv)
                    a = tp("a")
                    kb.sig(a[:], pa[:, 0:128], nbias=nw0[:, p, 2:3])
                    yield
                    pw = PS.nxt()
                    kb.mm(pw[:, 0:128], ww2b[:, dirn, p * 128:(p + 1) * 128], th[:])
                    e1 = tp("e1")
                    kb.act(e1[:], pw[:, 0:128], AF.Exp, scale=-1.0, bias=nw0[:, p, dirn:dirn + 1])
                    yield
                    kb.act(e1[:], e1[:], AF.Ln, bias=1.0)
                    ew = tp("ew")
                    kb.act(ew[:], e1[:], AF.Exp, scale=-1.0, bias=-0.5)
                    c_ = tp("c_")
                    kb.scan(c_[:], cmk, ew[:], 0.0, ALU.mult, ALU.subtract)
                    yield
                    eCt, eN, eX, eE = tp("eC"), tp("eN"), tp("eX"), tp("eE")
                    eC = eCt[:]
                    kb.act(eC, c_[:], AF.Exp)
                    kb.copy(gam[:, p, 0:1], eCt[:, 63:64])
                    kb.copy(gam[:, p, 1:2], eCt[:, 127:128])
                    kb.act(eN[:], c_[:], AF.Exp, scale=-1.0)
                    cx = tp("cx")
                    kb.tt(cx[:], c_[:], ew[:], ALU.add)
                    yield
                    kb.act(eX[:], cx[:], AF.Exp)
                    for cb in range(2):
                        kb.act(eE[:, tr(cb)], c_[:, tr(cb)], AF.Exp, scale=-1.0, bias=c_[:, 64 * cb + 63:64 * cb + 64])
                    kk0 = tp("kk0")
                    kb.ts(kk0[:], k_, vc(1), None, ALU.mult)
                    sq = tp("sq", dt=LD)
                    kb.act(sq[:], kk0[:], AF.Square)
                    yield
                    pss = PS.nxt()
                    kb.mm(pss[:, 0:128], oblkb, sq[:])
                    nr = tp("nr")
                    kb.ts(nr[:], pss[:, 0:128], 1e-24, None, ALU.max)
                    yield
                    kb.act(nr[:], nr[:], AF.Ln)
                    kb.act(nr[:], nr[:], AF.Exp, scale=-0.5)
                    kk = tp("kk")
                    kb.tt(kk[:], kk0[:], nr[:], ALU.mult)
                    t1 = tp("t1")
                    kb.ts(t1[:], a[:], 1.0, vc(2), ALU.subtract, ALU.mult)
                    km = tp("km")
                    kb.stt(km[:], t1[:], 1.0, k_, ALU.add, ALU.mult)
                    yield
                    b_ = tp("b_")
                    kb.tt(b_[:], kk[:], a[:], ALU.mult, eng=PL)
                    P_ = slice(p * 128, (p + 1) * 128)
                    kb.tt(F["rd"][:, P_], r_, eC, ALU.mult, eng=PL)
                    kb.tt(F["kd"][:, P_], km[:], eN[:], ALU.mult)
                    kb.tt(F["kkd"][:, P_], kk[:], eX[:], ALU.mult)
                    yield
                    kb.tt(F["bd"][:, P_], b_[:], eN[:], ALU.mult, eng=PL)
                    kef = tp("kef", dt=LD)
                    kb.tt(kef[:], km[:], eE[:], ALU.mult, eng=PL)
                    nbf = tp("nbf", dt=LD)
                    kb.stt(nbf[:], b_[:], -1.0, eE[:], ALU.mult, ALU.mult)
                    yield
                    if final:
                        rk = tp("rk", dt=LD)
                        kb.stt(rk[:], r_, vc(3), km[:], ALU.mult, ALU.mult)
                        pb = PS.nxt()
                        kb.mm(pb[:, 0:128], oblkb, rk[:])
                        kb.tt(bon[:, P_], pb[:, 0:128], vT, ALU.mult)
                        yield
                    pt = PS.nxt()
                    for i_, srcv in enumerate((vT, F["kkd"][:, P_], kef[:], nbf[:])):
                        kb.mm(pt[:, i_ * 128:(i_ + 1) * 128], srcv, identb[:])
                    for i_, dst in enumerate(("V", "KK", "KE", "NBE")):
                        kb.copy(Tm[dst][:, P_], pt[:, i_ * 128:(i_ + 1) * 128], eng="act" if i_ % 2 else "dve")

                for pp in ((0, 1), (2, 3)):
                    gens = [pair_gen(p) for p in pp]
                    while gens:
                        for g_ in list(gens):
                            try:
                                next(g_)
                            except StopIteration:
                                gens.remove(g_)
                rd, kd, bd, kkd = F["rd"], F["kd"], F["bd"], F["kkd"]
                V, KK, KE, NBE = Tm["V"], Tm["KK"], Tm["KE"], Tm["NBE"]
                frs = lambda hp: slice(64 * hp, 64 * hp + 64)

                def newt(nm):
                    return tmp(nm, (128, 512), 5, dt=LD) if nm == "S5" else T5(nm, LD)

                def scores(lhs, rhs, msk, nm, neg=False):
                    ps = PS.nxt()
                    for hp in range(2):
                        for cb in range(2):
                            for p in range(4):
                                h = 2 * p + hp
                                kb.mm(ps[tr(cb), hc(h)], lhs[frs(hp), pc(p, cb)], rhs[frs(hp), pc(p, cb)])
                    o_ = newt(nm)
                    if neg:
                        kb.stt(o_[:], ps[:], -1.0, msk, ALU.mult, ALU.mult)
                    else:
                        kb.tt(o_[:], ps[:], msk, ALU.mult)
                    return o_

                def bprod(A, B, nm, add=None, eng="act"):
                    ps = PS.nxt()
                    for cb in range(2):
                        for h in range(8):
                            kb.mm(ps[tr(cb), hc(h)], A[tr(cb), hc(h)], B[tr(cb), hc(h)])
                    o_ = newt(nm)
                    if add is None:
                        kb.copy(o_[:], ps[:], eng=eng)
                    else:
                        kb.tt(o_[:], ps[:], add[:], ALU.add)
                    return o_

                def fprod(pairs):
                    ps = PS.nxt()
                    for cb in range(2):
                        for hp in range(2):
                            for p in range(4):
                                h = 2 * p + hp
                                for i, (A, B) in enumerate(pairs):
                                    kb.mm(ps[frs(hp), pc(p, cb)], A[tr(cb), hc(h)], B[tr(cb), hc(h)], start=(i == 0), stop=(i == len(pairs) - 1))
                    return ps

                LkT = scores(kd, kkd, m_su, "LkT")
                MkT = scores(kd, rd, m_iu, "MkT")
                Nn = scores(bd, kkd, m_su, "S5")
                nMbT = scores(bd, rd, m_iu, "nMbT", neg=True)
                Ll = scores(kkd, bd, m_sl, "S5")
                Y = tmp("S5", (128, 512), 5, dt=LD)
                kb.tt(Y[:], eye8, Nn[:], ALU.subtract)
                Lp, Np = Ll, Nn
                for lvl in range(5):
                    L2 = bprod(Np, Lp, "S5", eng="act")
                    if lvl < 4:
                        N2 = bprod(Lp, Np, "S5", eng="dve")
                    Y = bprod(L2, Y, "S5", add=Y)
                    Lp, Np = L2, (N2 if lvl < 4 else None)
                TT = Y
                Zs = bprod(LkT, V, "Fkkd")
                Ws = bprod(TT, KK, "Fkd", eng="dve")
                U0 = bprod(TT, Zs, "Fbd")
                pP = fprod([(Ws, NBE)])
                PTs = T5("PTs", LD)
                kb.copy(PTs[:], pP[:], eng="act")
                pQ = fprod([(KE, V), (NBE, U0)])
                Qs = T5("LkT", LD)
                kb.copy(Qs[:], pQ[:], eng="dve")
                pR = fprod([(Ws, nMbT)])
                RpT = T5("RpT", LD)
                kb.tt(RpT[:], pR[:], rd[:], ALU.add)
                pOd = PD.nxt()
                pO0, pO1 = pOd[:, 0:512], pOd[:, 512:1024]
                for cb in range(2):
                    for hp in range(2):
                        for p in range(4):
                            h = 2 * p + hp
                            kb.mm(pO0[frs(hp), pc(p, cb)], V[tr(cb), hc(h)], MkT[tr(cb), hc(h)], start=True, stop=False)
                            kb.mm(pO0[frs(hp), pc(p, cb)], U0[tr(cb), hc(h)], nMbT[tr(cb), hc(h)], start=False, stop=True)
                first = (n % TPS == (TPS - 1 if flip else 0))
                if first:
                    kb.ts(H[:], H[:], flg[:, (NSEG if flip else 0) + seg:(NSEG if flip else 0) + seg + 1], None, ALU.mult)
                    kb.copy(Hb[:], H[:], eng="act")
                for cb in range(2):
                    for hp in range(2):
                        for p in range(4):
                            kb.mm(pO1[frs(hp), pc(p, cb)], Hb[frs(hp), p, :], RpT[frs(hp), pc(p, cb)])
                    pH = PS.nxt()
                    for hp in range(2):
                        for p in range(4):
                            kb.mm(pH[frs(hp), p * 64:(p + 1) * 64], PTs[frs(hp), pc(p, cb)], Hb[frs(hp), p, :], start=True, stop=False)
                            kb.mm(pH[frs(hp), p * 64:(p + 1) * 64], identb[frs(hp), frs(hp)], Qs[frs(hp), pc(p, cb)], start=False, stop=True)
                    for p in range(4):
                        kb.stt(H[:, p, :], H[:, p, :], gam[:, p, cb:cb + 1], pH[:, p * 64:(p + 1) * 64], ALU.mult, ALU.add)
                    kb.copy(Hb[:], H[:], eng="act")
                return pOd, zm, bon

            def odd_bwd():
                kb.memset(H[:], 0.0)
                kb.memset(Hb[:], 0.0)
                order = list(range(NT - 1, -1, -1))
                sts = {}
                for i, n in enumerate(order + [None]):
                    if n is not None:
                        sts[n] = stage1(n, True, 1, False, False)
                    if i == 0:
                        continue
                    m = order[i - 1]
                    prv = sts.get(order[i - 2]) if i >= 2 else None
                    halo(sts[m], prv, sts.get(n) if n is not None else None, True)
                    pOd, _, _ = rwkv_tile(sts[m], True, 1, False)
                    o_ = T5("obl")
                    o2 = T5("o2f")
                    kb.copy(o2[:], pOd[:, 0:512], eng="act")
                    for p in range(4):
                        kb.tt(o_[:, p * 128:(p + 1) * 128][:, ::-1], pOd[:, 512 + p * 128:512 + (p + 1) * 128], o2[:, p * 128:(p + 1) * 128], ALU.add)
                        kb.dma(obTT[m][p * 128:(p + 1) * 128, m * 128:(m + 1) * 128], o_[:, p * 128:(p + 1) * 128])
                    if i >= 2:
                        del sts[order[i - 2]]

            def attn_tile(cur, prv, nxt):
                hT, rop, seg = cur["hT"], cur["rop"], cur["seg"]
                qr = T5("qrb", BF16)
                for r in range(4):
                    pq, pqr = projF(hT, OD["cq"] + r * 128), projF(hT, OD["rq"] + r * 128)
                    t1, t2 = tmp("rt1"), tmp("rt2")
                    kb.tt(t1[:], pq[:, 0:128], rop[:, 0, :], ALU.mult)
                    kb.tt(t2[:], pqr[:, 0:128], rop[:, 1, :], ALU.mult)
                    kb.tt(qr[:, r * 128:(r + 1) * 128], t1[:], t2[:], ALU.add)
                pOa = PD.nxt()
                for g in range(2):
                    gr = slice(64 * g, 64 * g + 64)
                    Pts = []
                    for nb, msk in ((prv, m_ge4), (cur, None), (nxt, m_le4)):
                        if nb is None:
                            continue
                        pS_ = PS.nxt()
                        kb.mm(pS_[:], nb["kT"][gr, :], qr[gr, :])
                        Pt = tmp("Pt", (128, 512), 3, dt=BF16)
                        kb.act(Pt[:], pS_[:], AF.Exp, scale=0.125)
                        if msk is not None:
                            if nb["seg"] == seg:
                                kb.tt(Pt[:], Pt[:], msk, ALU.mult)
                            else:
                                sg = max(nb["seg"], seg)
                                kb.tt(Pt[:], Pt[:], msk, ALU.mult)
                                kb.ts(Pt[:], Pt[:], flg[:, sg:sg + 1], None, ALU.mult)
                        Pts.append((Pt, nb["v65"]))
                    for r in range(4):
                        for i, (Pt, v65) in enumerate(Pts):
                            kb.mm(pOa[:, g * 512 + r * 65:g * 512 + (r + 1) * 65], Pt[:, r * 128:(r + 1) * 128], v65[:, g, :],
                                  start=(i == 0), stop=(i == len(Pts) - 1))
                den = tmp("den", (128, 8))
                for g in range(2):
                    for r in range(4):
                        h = 4 * g + r
                        c0 = g * 512 + r * 65
                        kb.tt(den[:, h:h + 1], pOa[:, c0 + 64:c0 + 65], sinkE[:, h:h + 1], ALU.add)
                kb.recip(den[:], den[:])
                co = T5("cof")
                for g in range(2):
                    for r in range(4):
                        h = 4 * g + r
                        c0 = g * 512 + r * 65
                        kb.ts(co[:, h * 64:(h + 1) * 64], pOa[:, c0:c0 + 64], den[:, h:h + 1], None, ALU.mult)
                pG = PS.nxt()
                for k in range(8):
                    kb.mm(pG[:], hT[:, k, :], WH["Win"][:, k, OD["cg"]:OD["cg"] + 512], start=(k == 0), stop=(k == 7))
                sg_ = T5("sgf")
                kb.sig(sg_[:], pG[:])
                kb.tt(co[:], co[:], sg_[:], ALU.mult, eng="pool")
                kb.tt(co[:], co[:], pG[:], ALU.mult)
                return co

            def odd_fwd():
                kb.memset(H[:], 0.0)
                kb.memset(Hb[:], 0.0)
                order = list(range(NT))
                sts = {}
                for i, n in enumerate(order + [None]):
                    if n is not None:
                        sts[n] = stage1(n, False, 0, True, True)
                    if i == 0:
                        continue
                    m = order[i - 1]
                    cur = sts[m]
                    prv = sts.get(order[i - 2]) if i >= 2 else None
                    nxt = sts.get(n) if n is not None else None
                    halo(cur, prv, nxt, False)
                    pOd, zm, bon = rwkv_tile(cur, False, 0, True)
                    ob = T5("obl")
                    for p in range(4):
                        kb.dma(ob[:, p * 128:(p + 1) * 128], obTT[m][p * 128:(p + 1) * 128, m * 128:(m + 1) * 128])
                    od_ = ob
                    kb.tt(od_[:], pOd[:, 0:512], ob[:], ALU.add)
                    kb.tt(od_[:], pOd[:, 512:1024], od_[:], ALU.add)
                    onT = tmp("onT", (128, 8, 128), 1, dt=BF16)
                    for p in range(4):
                        vc = lambda i_: vecT[:, p, i_:i_ + 1]
                        pm_ = PS.nxt()
                        kb.mm(pm_[:, 0:128], oblk, od_[:, p * 128:(p + 1) * 128])
                        cen = tmp("cen")
                        kb.stt(cen[:], pm_[:, 0:128], -1.0 / 64.0, od_[:, p * 128:(p + 1) * 128], ALU.mult, ALU.add)
                        sq = tmp("sqf")
                        kb.act(sq[:], cen[:], AF.Square)
                        pv_ = PS.nxt()
                        kb.mm(pv_[:, 0:128], oblk, sq[:])
                        rs = tmp("rsd")
                        kb.rsqrt(rs[:], pv_[:, 0:128], 1.0 / 64.0, 64e-5)
                        kb.tt(cen[:], cen[:], rs[:], ALU.mult)
                        kb.ts(cen[:], cen[:], vc(4), vc(5), ALU.mult, ALU.add)
                        kb.tt(cen[:], cen[:], bon[:, p * 128:(p + 1) * 128], ALU.add)
                        sgg = tmp("sgg")
                        kb.sig(sgg[:], zm[:, 12 + p, :])
                        kb.tt(cen[:], cen[:], sgg[:], ALU.mult, eng="pool")
                        kb.tt(onT[:, 4 + p, :], cen[:], zm[:, 12 + p, :], ALU.mult)
                    co = attn_tile(cur, prv, nxt)
                    pT = PS.nxt()
                    for r in range(4):
                        kb.mm(pT[:, r * 128:(r + 1) * 128], co[:, r * 128:(r + 1) * 128], ident)
                    kb.copy(onT[:, 0:4, :], pT[:], eng="act")
                    xr_ = xto.nxt()
                    kb.dma(xr_[:], src[m][rows(m), :])
                    finish2(m, cur["seg"], onT, xr_, yT)
                    if i >= 2:
                        del sts[order[i - 2]]

            odd_bwd()
            odd_fwd()
        if 1 not in layers:
            for n in range(NT):
                t_ = xtR.nxt()
                kb.dma(t_[:], x1T[n][rows(n), :])
                kb.dma(yT[n][rows(n), :], t_[:])
        kb.P.emit()
    return nc


def consts():
    c = np.zeros((128, 512), np.float32)
    c[:, 0:128] = np.eye(128)
    c[:, 128:256] = np.eye(128)[::-1]
    j = np.arange(128)[:, None]
    i = np.arange(128)[None, :]
    c[:, 256:384] = (j <= i)
    c[:, 384:512] = 1.0
    return c


def shared_inputs(w):
    f = lambda a: np.ascontiguousarray(a, dtype=np.float32)
    m = {}
    m["consts"] = consts()
    m["ev_w_mod"] = f(w["ev_w_mod"][0])
    m["ev_b_modT"] = f(w["ev_b_mod"][0].reshape(24, 128).T)
    m["ev_b_mod"] = f(w["ev_b_mod"][0].reshape(1, -1))
    m["ev_w_in"] = f(w["ev_w_in"][0])
    wi = w["ev_w_in"][0]
    m["ev_w_blT"] = f(np.concatenate([wi[:, 3584:3600].T, wi[:, 3600:3616].T], axis=1))
    m["gla_w_gk"] = f(np.concatenate([w["gla_w_gk"][0, 0], w["gla_w_gk"][0, 1]], axis=1))
    m["gla_b_gkT"] = f(w["gla_b_gk"][0].reshape(2, 2, 128).transpose(2, 0, 1).reshape(128, 4))
    m["ev_rows"] = f(np.concatenate([w["hgrn_norm"][0], w["gla_norm"][0], w["ev_ln_g"][0], w["ev_ln_b"][0]]).reshape(1, -1))
    m["ev_w_out"] = f(w["ev_w_out"][0])
    m["lblT"] = f(w["hgrn_lb_logits"].reshape(2, 2, 4, 128).transpose(3, 0, 1, 2).reshape(128, 16))
    return m


def consts2():
    c = np.zeros((128, 6 * 512 + 256), np.float32)
    r = (np.arange(128) % 64)[:, None]
    q = (np.arange(512) % 64)[None, :]
    c[:, 0:512] = r < q
    c[:, 512:1024] = r <= q
    c[:, 1024:1536] = r > q
    c[:, 1536:2048] = r == q
    j = np.arange(128)[:, None]
    i = (np.arange(512) % 128)[None, :]
    c[:, 2048:2560] = j >= i
    c[:, 2560:3072] = j <= i
    a = np.arange(128)
    c[:, 3072:3200] = (a[:, None] // 64) == (a[None, :] // 64)
    cm = np.ones((128, 128), np.float32)
    cm[:, 0] = 0.0
    cm[:, 64] = 0.0
    c[:, 3200:3328] = cm
    return c


def rope_table(pos):
    inv = (10000.0 ** (-np.arange(0, 64, 2, dtype=np.float32) / np.float32(64))).astype(np.float32)
    ang = (pos.astype(np.float32)[:, None] * inv[None, :]).astype(np.float32)
    cos, sin = np.cos(ang).astype(np.float32), np.sin(ang).astype(np.float32)
    cc = np.concatenate([cos, cos], axis=1).T
    ss = np.concatenate([-sin, sin], axis=1).T
    t = np.stack([np.concatenate([cc, cc], 0), np.concatenate([ss, ss], 0)], axis=1)
    return np.ascontiguousarray(t, dtype=np.float32)


def od_shared(w):
    f = lambda a: np.ascontiguousarray(a, dtype=np.float32)
    m = {}
    wi = w["od_w_in"][0]
    qcols = np.concatenate([np.concatenate([np.arange(r * 64, r * 64 + 64), np.arange((4 + r) * 64, (4 + r) * 64 + 64)])
                            for r in range(4)])
    swap = lambda cols: np.concatenate([np.concatenate([cols[i * 64 + 32:i * 64 + 64], cols[i * 64:i * 64 + 32]])
                                        for i in range(len(cols) // 64)])
    kcols = np.arange(512, 640)
    order = np.concatenate([qcols, np.arange(512, 3424), swap(qcols), swap(kcols)])
    m["od_w_in"] = f(wi[:, order])
    m["od_w_mod"] = f(w["od_w_mod"][0])
    m["od_b_modT"] = f(w["od_b_mod"][0].reshape(24, 128).T)
    m["od_b_mod"] = f(w["od_b_mod"][0].reshape(1, -1))
    m["od_rows"] = f(np.concatenate([np.zeros(D, np.float32), w["od_ln_g"][0], w["od_ln_b"][0]]).reshape(1, -1))
    m["od_w_out"] = f(w["od_w_out"][0])
    m["consts2"] = consts2()
    mix = w["rwkv_mix"][0]
    mt = np.zeros((128, 2, 19), np.float32)
    starts = [0, 128, 256, 384, 512, 640, 768, 896, 1024, 1152, 1280, 1408, 1632, 1760, 1888, 2016, 1536, 1568, 1600]
    for j, st in enumerate(starts):
        n_ = 32 if j >= 16 else 128
        mt[0:n_, :, j] = mix[:, st:st + n_].T
    m["mixT"] = mt
    vecs = [w["rwkv_a0"][0], w["rwkv_k_k"][0], w["rwkv_k_a"][0], w["rwkv_r_k"][0], w["rwkv_ln_w"][0], w["rwkv_ln_b"][0],
            w["rwkv_w0"][0, 0], w["rwkv_w0"][0, 1]]
    m["vecT"] = f(np.stack([v.reshape(4, 128) for v in vecs], axis=-1).transpose(1, 0, 2))
    m["w_w2"] = f(w["rwkv_w_w2"][0].transpose(1, 0, 2))
    m["a_w2"] = f(w["rwkv_a_w2"][0])
    m["sink"] = f(w["swa_sink"][0].reshape(1, 8))
    return m


_NC_CACHE = {}


def kernel(**inputs):
    NSEG, L = 4, 4096
    w = {k: np.asarray(v) for k, v in inputs.items()}
    xp, xs, cp, cs = w["x_prompt"], w["x_sample"], w["c_prompt"], w["c_sample"]
    shared = shared_inputs(w)
    shared.update(od_shared(w))
    plan = []
    for b in range(2):
        plan.append([("p", b, k) for k in range(4)])
    samp = [[0, 1, 2], [3, 4, 5], [6, 7, 8], [9, 10, 11], [12, 13], [14, 15]]
    for lst in samp:
        plan.append([("s", i, 0) for i in lst] + [("s", lst[0], 0)] * (4 - len(lst)))
    in_maps = []
    for core in range(8):
        m = dict(shared)
        xs_, cs_, pos = [], [], []
        fl = np.zeros((128, 2 * NSEG), np.float32)
        for s_, (kind, b, k) in enumerate(plan[core]):
            if kind == "p":
                xs_.append(xp[b, k * L:(k + 1) * L])
                cs_.append(cp[b])
                pos.append(np.arange(k * L, (k + 1) * L, dtype=np.float32))
                if k > 0:
                    fl[:, s_] = 1.0
                if k < 3:
                    fl[:, NSEG + s_] = 1.0
            else:
                xs_.append(xs[b])
                cs_.append(cs[b])
                pos.append(np.arange(L, dtype=np.float32))
        m["x"] = np.ascontiguousarray(np.concatenate(xs_, axis=0), dtype=np.float32)
        cc = np.stack(cs_, axis=0)
        m["cT"] = np.ascontiguousarray(cc.reshape(NSEG, 8, 128).transpose(2, 1, 0).reshape(128, 8 * NSEG), dtype=np.float32)
        m["flags"] = fl
        m["ropeT"] = rope_table(np.concatenate(pos))
        in_maps.append(m)
    if "nc" not in _NC_CACHE:
        _NC_CACHE["nc"] = build(NSEG, L)
    res = run_bass_kernel_spmd(_NC_CACHE["nc"], in_maps, core_ids=list(range(8)))
    yp = np.zeros_like(xp)
    ys = np.zeros_like(xs)
    for core in range(8):
        y = res.results[core]["y"].reshape(NSEG, L, D)
        seen = set()
        for s_, (kind, b, k) in enumerate(plan[core]):
            if kind == "p":
                yp[b, k * L:(k + 1) * L] = y[s_]
            elif b not in seen:
                ys[b] = y[s_]
                seen.add(b)
    return (yp, ys)
```

```python
import numpy as np
from contextlib import ExitStack
import concourse.bass as bass
import concourse.mybir as mybir
from concourse.bass_utils import run_bass_kernel_spmd
from concourse.alu_op_type import AluOpType as ALU

F32 = mybir.dt.float32
BF16 = mybir.dt.bfloat16
AF = mybir.ActivationFunctionType
D = 1024
ALPHA = 4 ** 0.25
NDMA = 24


class Tl:
    def __init__(self, t, name):
        self.t, self.name, self.w, self.r = t, name, {}, {}

    def __getitem__(self, idx):
        v = Vw(self, self.t[idx])
        i0 = idx[0] if isinstance(idx, tuple) else idx
        if isinstance(i0, slice) and i0.start is not None:
            v.p0, v.pn = i0.start, (i0.stop - i0.start)
        return v


class Vw:
    def __init__(self, tile, ap):
        self.tile, self.ap = tile, ap
        self.p0, self.pn = 0, 128

    def __getitem__(self, idx):
        v = Vw(self.tile, self.ap[idx])
        v.p0, v.pn = self.p0, self.pn
        i0 = idx[0] if isinstance(idx, tuple) else idx
        if isinstance(i0, slice) and i0.start is not None:
            v.p0, v.pn = self.p0 + i0.start, (i0.stop - i0.start)
        return v


class Op:
    __slots__ = ("eng", "fn", "deps", "sig", "sigval", "idx", "dma_k")

    def __init__(self, eng, fn, idx):
        self.eng, self.fn, self.idx = eng, fn, idx
        self.deps, self.sig, self.sigval, self.dma_k = set(), False, 0, -1


class Prog:
    ENGS = ("pe", "act", "dve", "pool")

    def __init__(self, nc):
        self.nc, self.ops, self.ndma = nc, [], 0

    def add(self, eng, fn, reads=(), writes=()):
        idx = len(self.ops)
        op = Op(eng, fn, idx)
        isdma = eng == "sp"
        key = ("dma", idx) if isdma else eng
        raw, oth = set(), set()
        for v in reads:
            if v is None or not isinstance(v, Vw):
                continue
            raw.update(v.tile.w.values())
            if getattr(v.tile, "psum", False):
                oth.update(i for k, i in v.tile.r.items() if k != key)
        for v in writes:
            oth.update(v.tile.w.values())
            oth.update(v.tile.r.values())
        for d in raw:
            p = self.ops[d]
            if p.eng == eng and eng == "pe":
                continue
            op.deps.add(d)
        for d in oth:
            p = self.ops[d]
            if p.eng == eng and not isdma:
                continue
            op.deps.add(d)
        for v in reads:
            if v is None or not isinstance(v, Vw):
                continue
            v.tile.r[key] = idx
        for v in writes:
            tl = v.tile
            tl.r = {}
            if isdma:
                tl.w = {k: i for k, i in tl.w.items() if not isinstance(k, tuple)}
            tl.w[key] = idx
        if isdma:
            op.dma_k = self.ndma
            self.ndma += 1
        self.ops.append(op)
        return op

    def setup(self, es):
        nc = self.nc
        self.sems = {e: es.enter_context(nc.semaphore("s_" + e)) for e in self.ENGS}
        self.dsem = [es.enter_context(nc.semaphore("d%d" % i)) for i in range(NDMA)]
        self.cnt = {e: 0 for e in self.ENGS}
        self.waited = {e: {} for e in self.ENGS + ("sp",)}
        self.done = 0

    def emit(self):
        nc = self.nc
        ops, sems, dsem = self.ops, self.sems, self.dsem
        lo = self.done
        for op in ops[lo:]:
            op.deps = {d for d in op.deps if d >= lo}
            for d in op.deps:
                ops[d].sig = True
        for op in ops[lo:]:
            if op.eng != "sp" and op.sig:
                self.cnt[op.eng] += 1
                op.sigval = self.cnt[op.eng]

        def need(op):
            req = {}
            for d in op.deps:
                p = ops[d]
                if p.eng == "sp":
                    s, v = dsem[p.dma_k % NDMA], 16 * (p.dma_k // NDMA + 1)
                else:
                    s, v = sems[p.eng], p.sigval
                k = id(s)
                if k not in req or req[k][1] < v:
                    req[k] = (s, v)
            return req

        def run(engname, e):
            waited = self.waited[engname]
            for op in ops[lo:]:
                if op.eng != engname:
                    continue
                req = need(op)
                if engname == "sp" and op.dma_k >= NDMA:
                    s = dsem[op.dma_k % NDMA]
                    v = 16 * (op.dma_k // NDMA)
                    if id(s) not in req or req[id(s)][1] < v:
                        req[id(s)] = (s, v)
                for k, (s, v) in req.items():
                    if waited.get(k, 0) < v:
                        e.wait_ge(s, v)
                        waited[k] = v
                ins = op.fn(e)
                if engname == "sp":
                    ins.then_inc(dsem[op.dma_k % NDMA], 16)
                elif op.sig:
                    ins.then_inc(sems[engname], 1)
            if engname == "sp":
                for i in range(min(NDMA, self.ndma)):
                    last = ((self.ndma - 1 - i) // NDMA) * NDMA + i
                    v = 16 * (last // NDMA + 1)
                    if waited.get(id(dsem[i]), 0) < v:
                        e.wait_ge(dsem[i], v)
                        waited[id(dsem[i])] = v

        with nc.Block() as block:
            @block.sync
            def _(e):
                run("sp", e)

            @block.tensor
            def _(e):
                run("pe", e)

            @block.scalar
            def _(e):
                run("act", e)

            @block.vector
            def _(e):
                run("dve", e)

            @block.gpsimd
            def _(e):
                run("pool", e)
        self.done = len(ops)


def v3(vw, a):
    return Vw(vw.tile, vw.ap.rearrange("p (a b) -> p a b", a=a))


def bc(vw, shape):
    return Vw(vw.tile, vw.ap[:, :, None].broadcast_to(list(shape)))


def _ap(v):
    return v.ap if isinstance(v, Vw) else v


class K:
    def __init__(self, nc, es):
        self.nc, self.es, self.P = nc, es, Prog(nc)
        self.n = 0

    def sb(self, shape, dt=F32, name=None):
        self.n += 1
        nm = "%s%d" % (name or "t", self.n)
        return Tl(self.es.enter_context(self.nc.sbuf_tensor(nm, list(shape), dt)), nm)

    def ps(self, shape, name=None):
        self.n += 1
        nm = "%s%d" % (name or "p", self.n)
        t = Tl(self.es.enter_context(self.nc.psum_tensor(nm, list(shape), F32)), nm)
        t.psum = True
        return t

    def mm(self, out, lhsT, rhs, start=True, stop=True):
        rg = (lhsT.p0, lhsT.pn, out.p0, out.pn)
        last = getattr(self, "last_rg", (0, 128, 0, 128))
        tiled = lambda g: g[1] < 128 or g[3] < 128
        if rg != last and (tiled(rg) or tiled(last)) and getattr(self, "dummy", None) is not None:
            dps, dl, dr_ = self.dummy
            self.P.add("pe", lambda e: e.matmul(dps.ap, dl.ap, dr_.ap, start=True, stop=True), [dl], [dps])
        self.last_rg = rg
        self.P.add("pe", lambda e: e.matmul(out.ap, lhsT.ap, rhs.ap, start=start, stop=stop),
                   [lhsT, rhs], [out])

    def act(self, out, in_, func, scale=1.0, bias=0.0, accum=None, eng="act"):
        kw = {}
        if accum is not None:
            kw["accum_out"] = accum.ap
        self.P.add(eng, lambda e: e.activation(out.ap, in_.ap, func, bias=_ap(bias), scale=_ap(scale), **kw),
                   [in_, scale, bias], [out] + ([accum] if accum is not None else []))

    def tt(self, out, a, b, op, eng="dve"):
        self.P.add(eng, lambda e: e.tensor_tensor(out.ap, a.ap, b.ap, op), [a, b], [out])

    def ts(self, out, a, s1, s2, op0, op1=None, eng="dve"):
        if op1 is None:
            self.P.add(eng, lambda e: e.tensor_scalar(out.ap, a.ap, _ap(s1), None, op0), [a, s1], [out])
        else:
            self.P.add(eng, lambda e: e.tensor_scalar(out.ap, a.ap, _ap(s1), _ap(s2), op0, op1), [a, s1, s2], [out])

    def stt(self, out, a, s, b, op0, op1):
        self.P.add("dve", lambda e: e.scalar_tensor_tensor(out.ap, a.ap, _ap(s), b.ap, op0, op1), [a, s, b], [out])

    def scan(self, out, d0, d1, init, op0, op1):
        self.P.add("dve", lambda e: e.tensor_tensor_scan(out.ap, d0.ap, d1.ap, init, op0, op1), [d0, d1], [out])

    def copy(self, out, in_, eng="dve"):
        if eng == "act":
            self.P.add("act", lambda e: e.copy(out.ap, in_.ap), [in_], [out])
        else:
            self.P.add(eng, lambda e: e.tensor_copy(out.ap, in_.ap), [in_], [out])

    def memset(self, out, val, eng="dve"):
        self.P.add(eng, lambda e: e.memset(out.ap, val), [], [out])

    def sig(self, out, in_, nbias=0.0, e_out=None):
        e = e_out if e_out is not None else out
        self.act(e, in_, AF.Exp, scale=-1.0, bias=nbias)
        self.ts(out, e, 1.0, None, ALU.add)
        self.recip(out, out)

    def rsqrt(self, out, in_, scale, eps):
        self.act(out, in_, AF.Ln, scale=scale, bias=eps)
        self.act(out, out, AF.Exp, scale=-0.5)

    def recip(self, out, in_):
        self.P.add("dve", lambda e: e.reciprocal(out.ap, in_.ap), [in_], [out])

    def dma(self, out, in_):
        self.P.add("sp", lambda e: e.dma_start(out=out.ap, in_=in_.ap), [in_], [out])


class Rot:
    def __init__(self, items):
        self.items, self.i = items, 0

    def nxt(self):
        self.i += 1
        return self.items[(self.i - 1) % len(self.items)]


EV_COLS = dict(aq=0, ai=512, af_f=1024, af_b=1536, a_gate=2048, bq=2560, bk=2816, bv=3072, bl_f=3584, bl_b=3600,
               b_gate=3616, zf=4128, zb=4384)
WC = 4640


def build(NSEG, L, layers=(0, 1), LDT=BF16, use_pool=True):
    nc = bass.Bass("TRN2", target_bir_lowering=False)
    es = ExitStack()
    kb = K(nc, es)
    TT_ = NSEG * L
    NT = TT_ // 128
    TPS = L // 128

    def din(name, shape):
        return Tl(nc.dram_tensor(name, list(shape), F32, kind="ExternalInput").ap(), name)

    x_d = nc.dram_tensor("x", [TT_, D], F32, kind="ExternalInput").ap()
    y_d = nc.dram_tensor("y", [TT_, D], F32, kind="ExternalOutput").ap()
    x1_d = nc.dram_tensor("x1s", [TT_, D], F32, kind="Internal").ap()
    ob_d = nc.dram_tensor("obs", [TT_, D], F32, kind="Internal").ap()
    xT = [Tl(x_d, "x%d" % i) for i in range(NT)]
    yT = [Tl(y_d, "y%d" % i) for i in range(NT)]
    x1T = [Tl(x1_d, "x1%d" % i) for i in range(NT)]
    obT = [Tl(ob_d, "ob%d" % i) for i in range(NT)]
    rows = lambda n: slice(n * 128, (n + 1) * 128)

    cT_d = din("cT", [128, 8 * NSEG])
    flg_d = din("flags", [128, 2 * NSEG])
    cst_d = din("consts", [128, 4 * 128])
    ev = {}
    for nm, shp in [("ev_w_mod", [D, 3 * D]), ("ev_b_modT", [128, 24]), ("ev_b_mod", [1, 3 * D]), ("ev_w_in", [D, 4128]),
                    ("ev_w_blT", [16, 2 * D]), ("gla_w_gk", [16, 512]), ("gla_b_gkT", [128, 4]), ("lblT", [128, 16]),
                    ("ev_rows", [1, 3 * D]), ("ev_w_out", [D, D])]:
        ev[nm] = din(nm, shp)

    od = {}
    for nm, shp in [("od_w_mod", [D, 3 * D]), ("od_b_modT", [128, 24]), ("od_b_mod", [1, 3 * D]), ("od_w_in", [D, 4064]),
                    ("od_rows", [1, 3 * D]), ("od_w_out", [D, D]), ("consts2", [128, 6 * 512 + 256]), ("mixT", [128, 2, 19]),
                    ("vecT", [128, 4, 8]), ("w_w2", [32, 2, 512]), ("a_w2", [32, 512]), ("sink", [1, 8]),
                    ("ropeT", [128, 2, TT_])]:
        od[nm] = din(nm, shp)

    with es:
        kb.P.setup(es)
        cst = kb.sb([128, 512], F32, "cst")
        kb.dma(cst[:], cst_d[:, :])
        ident, Jm, mask, ones = cst[:, 0:128], cst[:, 128:256], cst[:, 256:384], cst[:, 384:512]
        flg = kb.sb([128, 2 * NSEG], F32, "flg")
        kb.dma(flg[:], flg_d[:, :])
        cT = kb.sb([128, 8 * NSEG], F32, "cT")
        kb.dma(cT[:], cT_d[:, :])
        scT = kb.sb([128, 8 * NSEG], F32, "scT")
        kb.sig(scT[:], cT[:])
        kb.tt(scT[:], scT[:], cT[:], ALU.mult)
        zer = kb.sb([128, 128], F32, "zer")
        kb.memset(zer[:], 0.0)

        PD = Rot([kb.ps([128, 1024], "pd") for _ in range(2)])
        PS = Rot([kb.ps([128, 512], "ps") for _ in range(3)])
        psd = kb.ps([128, 512], "psd")
        identg = kb.sb([128, 128], BF16, "identg")
        kb.copy(identg[:], ident)
        kb.dummy = (psd[:, 0:2], identg[:], identg[:, 0:2])

        WH = {}
        Wout = kb.sb([128, 8, D], BF16, "Wout")
        modT = kb.sb([128, 24 * NSEG], F32, "modT")
        bmT = kb.sb([128, 24], F32, "bmT")
        g1b = kb.sb([128, NSEG, D], BF16, "g1b")

        tmpR = {}

        TMPN = [2]

        def tmp(nm, shape=(128, 128), n=None, dt=F32):
            n = n or TMPN[0]
            if nm not in tmpR:
                tmpR[nm] = Rot([kb.sb(shape, dt, nm) for _ in range(n)])
            return tmpR[nm].nxt()

        def load_weights(w_in_d, ncols, w_out_d, w_mod_d, b_modT_d, b_mod_d, rows_d, extra=None, ro=0):
            WH["ro"] = ro
            ses = ExitStack()
            old_es, old_tmp = kb.es, dict(tmpR)
            kb.es = ses
            stg = Rot([kb.sb([128, 2048], F32, "stg") for _ in range(2)])
            brow = kb.sb([1, 3 * D], F32, "brow")
            rrow = kb.sb([1, 3 * D], F32, "rrow")
            for k in range(8):
                for c0 in range(0, ncols, 2048):
                    c1 = min(ncols, c0 + 2048)
                    s = stg.nxt()
                    kb.dma(s[:, 0:c1 - c0], w_in_d[k * 128:(k + 1) * 128, c0:c1])
                    kb.copy(WH["Win"][:, k, c0:c1], s[:, 0:c1 - c0], eng="pool" if (k % 2) else "dve")
                s = stg.nxt()
                kb.dma(s[:, 0:D], w_out_d[k * 128:(k + 1) * 128, :])
                kb.copy(Wout[:, k, :], s[:, 0:D], eng="act")
            kb.dma(bmT[:], b_modT_d[:, :])
            kb.dma(brow[:], b_mod_d[:, :])
            kb.dma(rrow[:], rows_d[:, :])
            pm = PD.nxt()
            for jb in range(12):
                wm = stg.nxt()
                for k in range(8):
                    kb.dma(wm[:, k * 256:(k + 1) * 256], w_mod_d[k * 128:(k + 1) * 128, jb * 256:(jb + 1) * 256])
                for jj in range(2):
                    j = jb * 2 + jj
                    for k in range(8):
                        kb.mm(pm[:, j * NSEG:(j + 1) * NSEG], wm[:, k * 256 + jj * 128:k * 256 + (jj + 1) * 128],
                              scT[:, k * NSEG:(k + 1) * NSEG], start=(k == 0), stop=(k == 7))
            for j in range(24):
                kb.act(modT[:, j * NSEG:(j + 1) * NSEG], pm[:, j * NSEG:(j + 1) * NSEG], AF.Identity,
                       bias=bmT[:, j:j + 1], scale=1.0)
            kb.ts(modT[:, 8 * NSEG:16 * NSEG], modT[:, 8 * NSEG:16 * NSEG], 1.0, None, ALU.add)
            for s_ in range(NSEG):
                pg = PD.nxt()
                for q4 in range(4):
                    wm = stg.nxt()
                    for k in range(8):
                        kb.dma(wm[:, k * 256:(k + 1) * 256], w_mod_d[k * 128:(k + 1) * 128, 2 * D + q4 * 256:2 * D + (q4 + 1) * 256])
                    for k in range(8):
                        scB = tmp("scB")
                        kb.act(scB[:], zer[:], AF.Identity, bias=scT[:, k * NSEG + s_:k * NSEG + s_ + 1], scale=1.0)
                        kb.mm(pg[:, q4 * 256:(q4 + 1) * 256], scB[:], wm[:, k * 256:(k + 1) * 256], start=(k == 0), stop=False)
                    kb.mm(pg[:, q4 * 256:(q4 + 1) * 256], ones[0:1, :], brow[0:1, 2 * D + q4 * 256:2 * D + (q4 + 1) * 256],
                          start=False, stop=True)
                kb.ts(g1b[:, s_, :], pg[:], 1.0, None, ALU.add)
            for r in range(ro, 3):
                pg = PD.nxt()
                for hf in range(2):
                    kb.mm(pg[:, hf * 512:(hf + 1) * 512], ones[0:1, :], rrow[0:1, r * D + hf * 512:r * D + (hf + 1) * 512])
                kb.copy(WH["rowsb"][:, r - ro, :], pg[:], eng="act")
            if extra is not None:
                extra()
            kb.P.emit()
            ses.close()
            kb.es = old_es
            tmpR.clear()
            tmpR.update(old_tmp)

        NB = 2
        xtR = Rot([kb.sb([128, D], F32, "xt") for _ in range(2)])
        hTR = Rot([kb.sb([128, 8, 128], BF16, "hT") for _ in range(NB)])
        from contextlib import contextmanager

        @contextmanager
        def layer_scope():
            ses = ExitStack()
            old_es, old_tmp = kb.es, dict(tmpR)
            kb.es = ses
            try:
                yield
            finally:
                kb.P.emit()
                ses.close()
                kb.es = old_es
                tmpR.clear()
                tmpR.update(old_tmp)

        EVT = {}

        def gla_unit(hT, dr, flip, psO, ocol, V, vcol, qsrc, ksrc, zsrc, S, sidx, heads, kind):
            lbT, bgk = EVT["lbT"], EVT["bgk"]
            def proj(col):
                p = PS.nxt()
                for k in range(8):
                    kb.mm(p[:, 0:128], WH["Win"][:, k, col:col + 128], hT[:, k, :], start=(k == 0), stop=(k == 7))
                return p
            pq = proj(qsrc)
            q = tmp("q")
            sn = tmp("sn")
            lf = tmp("lf")
            if kind == "A":
                h = sidx
                kb.sig(q[:], pq[:, 0:128])
                kb.tt(q[:], q[:], pq[:, 0:128], ALU.mult)
                pf = proj(zsrc)
                sg = tmp("sg")
                ee = tmp("ee")
                kb.sig(sg[:], pf[:, 0:128], e_out=ee[:])
                kb.act(lf[:], sg[:], AF.Ln, scale=lbT[:, dr * 8 + 4 + h:dr * 8 + 5 + h], bias=lbT[:, dr * 8 + h:dr * 8 + h + 1])
                kb.stt(sn[:], ee[:], lbT[:, dr * 8 + 4 + h:dr * 8 + 5 + h], sg[:], ALU.mult, ALU.mult)
                esc = 1.0
                sop = ALU.add
            else:
                p_ = sidx
                kb.act(q[:], pq[:, 0:128], AF.Identity, scale=0.125)
                pk = proj(ksrc)
                kb.copy(sn[:], pk[:, 0:128], eng="act")
                pf = proj(zsrc)
                sg = tmp("sg")
                kb.act(sg[:], pf[:, 0:128], AF.Exp, scale=-1.0, bias=EVT["nbgk"][:, dr * 2 + p_:dr * 2 + p_ + 1])
                kb.act(lf[:], sg[:], AF.Ln, bias=1.0)
                esc = 1.0 / 16.0
                sop = ALU.subtract
            B = tmp("B")
            kb.scan(B[:], ones, lf[:], 0.0, ALU.mult, sop)
            nb = tmp("nb", (128, 4))
            kb.ts(nb[:, 0:1], B[:, 63:64], -esc, None, ALU.mult)
            kb.ts(nb[:, 1:2], B[:, 63:64], esc, None, ALU.mult)
            kb.ts(nb[:, 2:3], B[:, 127:128], esc, None, ALU.mult)
            E1, E1n, E2, E3 = tmp("E1"), tmp("E1n"), tmp("E2"), tmp("E3")
            kb.act(E1[:], B[:], AF.Exp, scale=esc, bias=nb[:, 0:1])
            kb.act(E1n[:], B[:], AF.Exp, scale=-esc, bias=nb[:, 1:2])
            kb.act(E2[:], B[:], AF.Exp, scale=esc)
            kb.act(E3[:], B[:], AF.Exp, scale=-esc, bias=nb[:, 2:3])
            qd, qi, kd, ke = tmp("qd"), tmp("qi"), tmp("kd"), tmp("ke")
            kb.tt(qd[:], q[:], E1[:], ALU.mult, eng="pool")
            qi_o = qi[:, ::-1] if flip else qi[:]
            kb.tt(qi_o, q[:], E2[:], ALU.mult)
            kb.tt(kd[:], sn[:], E1n[:], ALU.mult, eng="pool")
            kb.tt(ke[:], sn[:], E3[:], ALU.mult)
            nh = len(heads)
            dk = 128 // nh

            def partB():
                pT = PS.nxt()
                kb.mm(pT[:, 0:128], ke[:], ident)
                keT = tmp("keT")
                kb.copy(keT[:], pT[:, 0:128], eng="act")
                for hi, hd in enumerate(heads):
                    pr = slice(hi * dk, (hi + 1) * dk)
                    pA = PS.nxt()
                    kb.mm(pA[:, 0:128], kd[pr, :], qd[pr, :])
                    A = tmp("A")
                    A_o = A[:, ::-1] if flip else A[:]
                    kb.tt(A_o, pA[:, 0:128], mask, ALU.mult)
                    oc = slice(ocol + hd * 128, ocol + (hd + 1) * 128)
                    vc = slice(vcol + hd * 128, vcol + (hd + 1) * 128)
                    Sh = S[pr, hd, :]
                    kb.mm(psO[:, oc], A[:], V[:, vc], start=True, stop=False)
                    kb.mm(psO[:, oc], qi[pr, :], Sh, start=False, stop=True)
                    pS_ = PS.nxt()
                    kb.mm(pS_[pr, 0:128], keT[:, pr], V[:, vc])
                    kb.stt(Sh, Sh, E2[pr, 127:128], pS_[pr, 0:128], ALU.mult, ALU.add)
            return partB

        def even_sweep(dr):
            S_A, S_B, oacc, VR = EVT["S_A"], EVT["S_B"], EVT["oacc"], EVT["VR"]
            flip = dr == 1
            order = range(NT - 1, -1, -1) if flip else range(NT)
            xsrc = xT
            kb.memset(S_A[:], 0.0)
            kb.memset(S_B[:], 0.0)
            PDs = PD.items
            order = list(order)

            def stageA(n):
                seg = n // TPS
                xt = xtR.nxt()
                kb.dma(xt[:], xsrc[n][rows(n), :])
                pX = PDs[1]
                for k in range(8):
                    kb.mm(pX[:, k * 128:(k + 1) * 128], xt[:, k * 128:(k + 1) * 128], Jm if flip else ident)
                hT = hTR.nxt()
                for k in range(8):
                    kb.act(hT[:, k, :], pX[:, k * 128:(k + 1) * 128], AF.Identity,
                           scale=modT[:, (8 + k) * NSEG + seg:(8 + k) * NSEG + seg + 1],
                           bias=modT[:, k * NSEG + seg:k * NSEG + seg + 1])
                V = VR.nxt()
                pV = PDs[1]
                for hf, col in enumerate((EV_COLS["ai"], EV_COLS["bv"])):
                    for k in range(8):
                        kb.mm(pV[:, hf * 512:(hf + 1) * 512], hT[:, k, :], WH["Win"][:, k, col:col + 512],
                              start=(k == 0), stop=(k == 7))
                kb.copy(V[:], pV[:], eng="act")
                return dict(xt=xt, hT=hT, V=V, seg=seg)

            nxtA = stageA(order[0])
            for i_n, n in enumerate(order):
                cur = nxtA
                seg, xt, hT, V = cur["seg"], cur["xt"], cur["hT"], cur["V"]
                first = (n % TPS == (TPS - 1 if flip else 0))
                if first:
                    fc = flg[:, (NSEG if flip else 0) + seg:(NSEG if flip else 0) + seg + 1]
                    kb.ts(S_A[:], S_A[:], fc, None, ALU.mult)
                    kb.ts(S_B[:], S_B[:], fc, None, ALU.mult)
                psO = PDs[0]
                pend = None
                for u in range(6):
                    if u < 4:
                        h = u
                        b_ = gla_unit(hT, dr, flip, psO, 0, V, 0, EV_COLS["aq"] + h * 128, None,
                                      EV_COLS["af_b" if flip else "af_f"] + h * 128, S_A, h, [h], "A")
                    else:
                        p_ = u - 4
                        b_ = gla_unit(hT, dr, flip, psO, 512, V, 512, EV_COLS["bq"] + p_ * 128, EV_COLS["bk"] + p_ * 128,
                                      EV_COLS["zb" if flip else "zf"] + p_ * 128, S_B, p_, [2 * p_, 2 * p_ + 1], "B")
                    if pend is not None:
                        pend()
                    pend = b_
                    if u == 3 and i_n + 1 < len(order):
                        nxtA = stageA(order[i_n + 1])
                pend()
                o = oacc.nxt()
                if flip:
                    kb.copy(o[:], psO[:], eng="act")
                    kb.dma(obT[n][rows(n), :], o[:])
                    continue
                ob = tmp("ob", (128, D), 1)
                kb.dma(ob[:], obT[n][rows(n), :])
                kb.tt(o[:], psO[:], ob[:], ALU.add)
                ss = tmp("ss", (128, 8))
                kb.act(ob[:], o[:], AF.Square)
                ob3 = v3(ob[:], 8)
                kb.P.add("dve", lambda e, ss=ss, ob3=ob3: e.tensor_reduce(ss.t[:, 0:8], ob3.ap, mybir.AxisListType.X, ALU.add),
                         [ob3], [ss[:]])
                rs = tmp("rs", (128, 8))
                kb.rsqrt(rs[:], ss[:], 1.0 / 128.0, 1e-6)
                pG = PDs[1]
                for hf, col in enumerate((EV_COLS["a_gate"], EV_COLS["b_gate"])):
                    for k in range(8):
                        kb.mm(pG[:, hf * 512:(hf + 1) * 512], hT[:, k, :], WH["Win"][:, k, col:col + 512],
                              start=(k == 0), stop=(k == 7))
                sgt = tmp("sgt", (128, D), 1)
                kb.sig(sgt[:], pG[:])
                on = o
                kb.tt(v3(on[:], 8), v3(o[:], 8), bc(rs[:, 0:8], (128, 8, 128)), ALU.mult)
                kb.tt(on[:], on[:], WH["rowsb"][:, 0, :], ALU.mult)
                kb.tt(on[:], on[:], sgt[:], ALU.mult, eng="pool")
                kb.tt(on[:], on[:], pG[:], ALU.mult)
                finish(n, seg, on, xt, x1T, PDs)

        def finish(n, seg, on, xt, dstT, PDs):
            pT = PDs[0]
            for k in range(8):
                kb.mm(pT[:, k * 128:(k + 1) * 128], on[:, k * 128:(k + 1) * 128], ident)
            onT = tmp("onT", (128, 8, 128), 1, dt=BF16)
            kb.copy(onT[:, 0:4, :], pT[:, 0:512], eng="act")
            kb.copy(onT[:, 4:8, :], pT[:, 512:1024], eng="dve")
            finish2(n, seg, onT, xt, dstT, PDs[1])

        def finish2(n, seg, onT, xt, dstT, pY=None):
            if pY is None:
                pY = PD.nxt()
            for hf in range(2):
                for k in range(8):
                    kb.mm(pY[:, hf * 512:(hf + 1) * 512], onT[:, k, :], Wout[:, k, hf * 512:(hf + 1) * 512],
                          start=(k == 0), stop=(k == 7))
            kb.tt(pY[:], pY[:], g1b[:, seg, :], ALU.mult)
            r = xt
            kb.stt(r[:], xt[:], ALPHA, pY[:], ALU.mult, ALU.add)
            st = tmp("st", (128, 12))
            kb.P.add("dve", lambda e: e.bn_stats(st.t[:, 0:6], r.t[:, 0:512]), [r[:]], [st[:]])
            kb.P.add("dve", lambda e: e.bn_stats(st.t[:, 6:12], r.t[:, 512:1024]), [r[:]], [st[:]])
            mv = tmp("mv", (128, 4))
            kb.P.add("dve", lambda e: e.bn_aggr(mv.t[:, 0:2], st.t[:, 0:12]), [st[:]], [mv[:]])
            kb.rsqrt(mv[:, 3:4], mv[:, 1:2], 1.0, 1e-5)
            yo = r
            ro = WH.get("ro", 0)
            kb.ts(yo[:], r[:], mv[:, 0:1], mv[:, 3:4], ALU.subtract, ALU.mult)
            kb.tt(yo[:], yo[:], WH["rowsb"][:, 1 - ro, :], ALU.mult, eng="pool")
            kb.tt(yo[:], yo[:], WH["rowsb"][:, 2 - ro, :], ALU.add)
            kb.dma(dstT[n][rows(n), :], yo[:])

        if 0 in layers:
          with layer_scope():
            WH["Win"] = kb.sb([128, 8, WC], BF16, "Win")
            WH["rowsb"] = kb.sb([128, 3, D], F32, "rowsb")
            S_A = EVT["S_A"] = kb.sb([128, 4, 128], F32, "S_A")
            S_B = EVT["S_B"] = kb.sb([128, 4, 128], F32, "S_B")
            lbT = EVT["lbT"] = kb.sb([128, 16], F32, "lbT")
            bgk = EVT["bgk"] = kb.sb([128, 4], F32, "bgk")
            EVT["oacc"] = Rot([kb.sb([128, D], F32, "oacc") for _ in range(1)])
            EVT["VR"] = Rot([kb.sb([128, D], F32, "V") for _ in range(2)])
            def ev_extra():
                wbl = kb.sb([16, 2 * D], F32, "wbl")
                w2 = kb.sb([16, 512], F32, "w2")
                kb.dma(wbl[:], ev["ev_w_blT"][:, :])
                kb.dma(w2[:], ev["gla_w_gk"][:, :])
                for dr in range(2):
                    for k in range(8):
                        p = PS.nxt()
                        kb.mm(p[:, 0:256], wbl[:, dr * D + k * 128:dr * D + (k + 1) * 128], w2[:, dr * 256:(dr + 1) * 256])
                        c0 = EV_COLS["zf"] + dr * 256
                        kb.copy(WH["Win"][:, k, c0:c0 + 256], p[:, 0:256], eng="act")
            load_weights(ev["ev_w_in"], 4128, ev["ev_w_out"], ev["ev_w_mod"], ev["ev_b_modT"], ev["ev_b_mod"], ev["ev_rows"], ev_extra)
            lbl = kb.sb([128, 16], F32, "lbl")
            kb.dma(lbl[:], ev["lblT"][:, :])
            for dr in range(2):
                dlt = tmp("dlt", (128, 4))
                kb.tt(dlt[:], lbl[:, dr * 8:dr * 8 + 4], lbl[:, dr * 8 + 4:dr * 8 + 8], ALU.subtract)
                kb.sig(lbT[:, dr * 8:dr * 8 + 4], dlt[:])
                kb.ts(lbT[:, dr * 8 + 4:dr * 8 + 8], lbT[:, dr * 8:dr * 8 + 4], -1.0, 1.0, ALU.mult, ALU.add)
            kb.dma(bgk[:], ev["gla_b_gkT"][:, :])
            EVT["nbgk"] = kb.sb([128, 4], F32, "nbgk")
            kb.ts(EVT["nbgk"][:], bgk[:], -1.0, None, ALU.mult)
            even_sweep(1)
            even_sweep(0)
        if 1 in layers:
          with layer_scope():
            WH["Win"] = kb.sb([128, 8, 4064], BF16, "Win")
            WH["rowsb"] = kb.sb([128, 2, D], F32, "rowsb")
            OD = dict(cq=0, ck=512, cv=640, cg=768, r=1280, k=1792, v=2304, wl_f=2816, wl_b=2848, al=2880, g=2912,
                      rq=3424, rk=3936)
            src = x1T if 0 in layers else xT
            c2 = kb.sb([128, 6 * 512], BF16, "c2")
            c2f = kb.sb([128, 256], F32, "c2f")
            kb.dma(c2f[:], od["consts2"][:, 3072:3328])
            m_su, m_iu, m_sl, eye8 = c2[:, 0:512], c2[:, 512:1024], c2[:, 1024:1536], c2[:, 1536:2048]
            m_ge4, m_le4 = c2[:, 2048:2560], c2[:, 2560:3072]
            oblk, cmk = c2f[:, 0:128], c2f[:, 128:256]
            mixT = kb.sb([128, 3, 19], F32, "mixT")
            kb.dma(mixT[:, 0:2, :], od["mixT"][:, :, :])
            vecT = kb.sb([128, 4, 8], F32, "vecT")
            kb.dma(vecT[:], od["vecT"][:, :, :])
            nw0 = kb.sb([128, 4, 3], F32, "nw0")
            sinkE = kb.sb([128, 8], F32, "sinkE")
            H = kb.sb([128, 4, 64], F32, "H")
            Hb = kb.sb([128, 4, 64], LDT, "Hb")
            identb = kb.sb([128, 128], LDT, "identb")
            oblkb_t = kb.sb([128, 128], LDT, "oblkb")
            oblkb = oblkb_t[:]
            aw2b = kb.sb([32, 512], LDT, "aw2b")
            ww2b = kb.sb([32, 2, 512], LDT, "ww2b")
            zrR = Rot([kb.sb([128, 19, 130], BF16, "zr") for _ in range(3)])
            kTR = Rot([kb.sb([128, 128], BF16, "kT") for _ in range(3)])
            v65R = Rot([kb.sb([128, 2, 65], BF16, "v65") for _ in range(3)])
            hTo = hTR
            xto = xtR
            ropR = Rot([kb.sb([128, 2, 128], F32, "rop") for _ in range(2)])
            obT_d = nc.dram_tensor("obT", [512, TT_], F32, kind="Internal").ap()
            obTT = [Tl(obT_d, "obT%d" % i) for i in range(NT)]
            rope_d = od["ropeT"]

            def od_extra():
                for q6 in range(3):
                    cs_ = kb.sb([128, 1024], F32, "c2s")
                    kb.dma(cs_[:], od["consts2"][:, q6 * 1024:(q6 + 1) * 1024])
                    kb.copy(c2[:, q6 * 1024:(q6 + 1) * 1024], cs_[:])
                kb.copy(identb[:], ident)
                kb.copy(oblkb_t[:], oblk)
                ww2 = kb.sb([32, 2, 512], F32, "ww2")
                kb.dma(ww2[:], od["w_w2"][:, :, :])
                aw2 = kb.sb([32, 512], F32, "aw2")
                kb.dma(aw2[:], od["a_w2"][:, :])
                kb.copy(ww2b[:], ww2[:])
                kb.copy(aw2b[:], aw2[:])
                srow = kb.sb([1, 8], F32, "srow")
                kb.dma(srow[:], od["sink"][:, :])
                p = PS.nxt()
                kb.mm(p[:, 0:8], ones[0:1, :], srow[0:1, :])
                kb.act(sinkE[:], p[:, 0:8], AF.Exp)
                kb.ts(nw0[:, :, 0:2], vecT[:, :, 6:8], -1.0, None, ALU.mult)
                kb.ts(nw0[:, :, 2:3], vecT[:, :, 0:1], -1.0, None, ALU.mult)
                kb.tt(mixT[:, 2, :], mixT[:, 0, :], mixT[:, 1, :], ALU.add)
                kb.ts(mixT[:, 2, :], mixT[:, 2, :], -1.0, 1.0, ALU.mult, ALU.add)
            load_weights(od["od_w_in"], 4064, od["od_w_out"], od["od_w_mod"], od["od_b_modT"], od["od_b_mod"], od["od_rows"], od_extra, ro=1)

            tr = lambda cb: slice(64 * cb, 64 * cb + 64)
            hc = lambda h: slice(64 * h, 64 * h + 64)
            pc = lambda p, cb: slice((p * 2 + cb) * 64, (p * 2 + cb + 1) * 64)
            T5 = lambda nm, dt=F32: tmp(nm, (128, 512), 1, dt=dt)
            LD = LDT
            PL = "pool" if use_pool else "dve"
            TMPN[0] = 1

            def projF(hT, col, m=128):
                p = PS.nxt()
                for k in range(8):
                    kb.mm(p[0:m, 0:128], WH["Win"][:, k, col:col + m], hT[:, k, :], start=(k == 0), stop=(k == 7))
                return p

            def stage1(n, flip, dirn, need_g, attn):
                seg = n // TPS
                xt = xto.nxt()
                kb.dma(xt[:], src[n][rows(n), :])
                pX = PD.nxt()
                for k in range(8):
                    kb.mm(pX[:, k * 128:(k + 1) * 128], xt[:, k * 128:(k + 1) * 128], Jm if flip else ident)
                hT = hTo.nxt()
                for k in range(8):
                    kb.act(hT[:, k, :], pX[:, k * 128:(k + 1) * 128], AF.Identity,
                           scale=modT[:, (8 + k) * NSEG + seg:(8 + k) * NSEG + seg + 1],
                           bias=modT[:, k * NSEG + seg:k * NSEG + seg + 1])
                zr = zrR.nxt()
                groups = [("r", 0), ("k", 4), ("v", 8)] + ([("g", 12)] if need_g else [])
                for gi, (nm, j0) in enumerate(groups):
                    for j in range(4):
                        p = projF(hT, OD[nm] + j * 128)
                        kb.copy(zr[:, j0 + j, 1:129], p[:, 0:128], eng="act" if (j % 2) else "dve")
                wj = 17 if dirn == 1 else 16
                p = projF(hT, OD["wl_b" if dirn == 1 else "wl_f"], 32)
                kb.copy(zr[0:32, wj, 1:129], p[0:32, 0:128], eng="act")
                p = projF(hT, OD["al"], 32)
                kb.copy(zr[0:32, 18, 1:129], p[0:32, 0:128], eng="dve")
                st = dict(n=n, seg=seg, xt=xt, hT=hT, zr=zr)
                if attn:
                    rop = ropR.nxt()
                    kb.dma(rop[:], rope_d[:, :, n * 128:(n + 1) * 128])
                    st["rop"] = rop
                    pk, pkr = projF(hT, OD["ck"]), projF(hT, OD["rk"])
                    t1, t2 = tmp("rt1"), tmp("rt2")
                    kb.tt(t1[:], pk[:, 0:128], rop[:, 0, :], ALU.mult)
                    kb.tt(t2[:], pkr[:, 0:128], rop[:, 1, :], ALU.mult)
                    kT = kTR.nxt()
                    kb.tt(kT[:], t1[:], t2[:], ALU.add)
                    pv = PS.nxt()
                    for k in range(8):
                        kb.mm(pv[:, 0:128], hT[:, k, :], WH["Win"][:, k, OD["cv"]:OD["cv"] + 128], start=(k == 0), stop=(k == 7))
                    v65 = v65R.nxt()
                    kb.memset(v65[:, :, 64:65], 1.0)
                    for g in range(2):
                        kb.copy(v65[:, g, 0:64], pv[:, g * 64:(g + 1) * 64], eng="act")
                    st["kT"], st["v65"] = kT, v65
                return st

            def halo(cur, prv, nxt, flip):
                zr = cur["zr"]
                for nb, dst, srccol in ((prv, 0, 128), (nxt, 129, 1)):
                    if nb is None:
                        kb.memset(zr[:, :, dst:dst + 1], 0.0)
                        continue
                    if nb["seg"] == cur["seg"]:
                        kb.copy(zr[:, :, dst:dst + 1], nb["zr"][:, :, srccol:srccol + 1])
                    else:
                        sg = max(nb["seg"], cur["seg"])
                        kb.ts(zr[:, :, dst:dst + 1], nb["zr"][:, :, srccol:srccol + 1], flg[:, sg:sg + 1], None, ALU.mult)

            def rwkv_tile(cur, flip, dirn, final):
                zr = cur["zr"]
                seg = cur["seg"]
                n = cur["n"]
                mp, mn = (1, 0) if flip else (0, 1)
                zm = tmp("zm", (128, 19, 128), 1, dt=BF16)
                wj = 17 if dirn == 1 else 16
                js = list(range(12)) + (list(range(12, 16)) if final else []) + [wj, 18]
                for j in js:
                    m_ = 32 if j >= 16 else 128
                    kb.act(zm[0:m_, j, :], zr[0:m_, j, 1:129], AF.Identity, scale=mixT[0:m_, 2, j:j + 1])
                    kb.stt(zm[0:m_, j, :], zr[0:m_, j, 0:128], mixT[0:m_, mp, j:j + 1], zm[0:m_, j, :], ALU.mult, ALU.add)
                    kb.stt(zm[0:m_, j, :], zr[0:m_, j, 2:130], mixT[0:m_, mn, j:j + 1], zm[0:m_, j, :], ALU.mult, ALU.add)
                th = tmp("th", (32, 128), 1, dt=LD)
                thf = tmp("thf", (32, 128), 1)
                kb.act(thf[:], zm[0:32, wj, :], AF.Exp, scale=-2.0)
                kb.ts(thf[:], thf[:], 1.0, None, ALU.add)
                kb.recip(thf[:], thf[:])
                kb.ts(th[:], thf[:], 2.0, -1.0, ALU.mult, ALU.add)
                F = {nm: T5("F" + nm, LD) for nm in ("rd", "kd", "bd", "kkd")}
                gam = tmp("gam", (128, 4, 2), 1)
                Tm = {nm: T5("T" + nm, LD) for nm in ("V", "KK", "KE", "NBE")}
                bon = T5("bon") if final else None
                tp = lambda nm, **kw: tmp(nm, n=2, **kw)

                def pair_gen(p):
                    r_, k_, vT = zm[:, p, :], zm[:, 4 + p, :], zm[:, 8 + p, :]
                    alv = zm[0:32, 18, :]
                    vc = lambda i: vecT[:, p, i:i + 1]
                    pa = PS.nxt()
                    kb.mm(pa[:, 0:128], aw2b[:, p * 128:(p + 1) * 128], alv)
                    a = tp("a")
                    kb.sig(a[:], pa[:, 0:128], nbias=nw0[:, p, 2:3])
                    yield
                    pw = PS.nxt()
                    kb.mm(pw[:, 0:128], ww2b[:, dirn, p * 128:(p + 1) * 128], th[:])
                    e1 = tp("e1")
                    kb.act(e1[:], pw[:, 0:128], AF.Exp, scale=-1.0, bias=nw0[:, p, dirn:dirn + 1])
                    yield
                    kb.act(e1[:], e1[:], AF.Ln, bias=1.0)
                    ew = tp("ew")
                    kb.act(ew[:], e1[:], AF.Exp, scale=-1.0, bias=-0.5)
                    c_ = tp("c_")
                    kb.scan(c_[:], cmk, ew[:], 0.0, ALU.mult, ALU.subtract)
                    yield
                    eCt, eN, eX, eE = tp("eC"), tp("eN"), tp("eX"), tp("eE")
                    eC = eCt[:]
                    kb.act(eC, c_[:], AF.Exp)
                    kb.copy(gam[:, p, 0:1], eCt[:, 63:64])
                    kb.copy(gam[:, p, 1:2], eCt[:, 127:128])
                    kb.act(eN[:], c_[:], AF.Exp, scale=-1.0)
                    cx = tp("cx")
                    kb.tt(cx[:], c_[:], ew[:], ALU.add)
                    yield
                    kb.act(eX[:], cx[:], AF.Exp)
                    for cb in range(2):
                        kb.act(eE[:, tr(cb)], c_[:, tr(cb)], AF.Exp, scale=-1.0, bias=c_[:, 64 * cb + 63:64 * cb + 64])
                    kk0 = tp("kk0")
                    kb.ts(kk0[:], k_, vc(1), None, ALU.mult)
                    sq = tp("sq", dt=LD)
                    kb.act(sq[:], kk0[:], AF.Square)
                    yield
                    pss = PS.nxt()
                    kb.mm(pss[:, 0:128], oblkb, sq[:])
                    nr = tp("nr")
                    kb.ts(nr[:], pss[:, 0:128], 1e-24, None, ALU.max)
                    yield
                    kb.act(nr[:], nr[:], AF.Ln)
                    kb.act(nr[:], nr[:], AF.Exp, scale=-0.5)
                    kk = tp("kk")
                    kb.tt(kk[:], kk0[:], nr[:], ALU.mult)
                    t1 = tp("t1")
                    kb.ts(t1[:], a[:], 1.0, vc(2), ALU.subtract, ALU.mult)
                    km = tp("km")
                    kb.stt(km[:], t1[:], 1.0, k_, ALU.add, ALU.mult)
                    yield
                    b_ = tp("b_")
                    kb.tt(b_[:], kk[:], a[:], ALU.mult, eng=PL)
                    P_ = slice(p * 128, (p + 1) * 128)
                    kb.tt(F["rd"][:, P_], r_, eC, ALU.mult, eng=PL)
                    kb.tt(F["kd"][:, P_], km[:], eN[:], ALU.mult)
                    kb.tt(F["kkd"][:, P_], kk[:], eX[:], ALU.mult)
                    yield
                    kb.tt(F["bd"][:, P_], b_[:], eN[:], ALU.mult, eng=PL)
                    kef = tp("kef", dt=LD)
                    kb.tt(kef[:], km[:], eE[:], ALU.mult, eng=PL)
                    nbf = tp("nbf", dt=LD)
                    kb.stt(nbf[:], b_[:], -1.0, eE[:], ALU.mult, ALU.mult)
                    yield
                    if final:
                        rk = tp("rk", dt=LD)
                        kb.stt(rk[:], r_, vc(3), km[:], ALU.mult, ALU.mult)
                        pb = PS.nxt()
                        kb.mm(pb[:, 0:128], oblkb, rk[:])
                        kb.tt(bon[:, P_], pb[:, 0:128], vT, ALU.mult)
                        yield
                    pt = PS.nxt()
                    for i_, srcv in enumerate((vT, F["kkd"][:, P_], kef[:], nbf[:])):
                        kb.mm(pt[:, i_ * 128:(i_ + 1) * 128], srcv, identb[:])
                    for i_, dst in enumerate(("V", "KK", "KE", "NBE")):
                        kb.copy(Tm[dst][:, P_], pt[:, i_ * 128:(i_ + 1) * 128], eng="act" if i_ % 2 else "dve")

                for pp in ((0, 1), (2, 3)):
                    gens = [pair_gen(p) for p in pp]
                    while gens:
                        for g_ in list(gens):
                            try:
                                next(g_)
                            except StopIteration:
                                gens.remove(g_)
                rd, kd, bd, kkd = F["rd"], F["kd"], F["bd"], F["kkd"]
                V, KK, KE, NBE = Tm["V"], Tm["KK"], Tm["KE"], Tm["NBE"]
                frs = lambda hp: slice(64 * hp, 64 * hp + 64)

                def newt(nm):
                    return tmp(nm, (128, 512), 5, dt=LD) if nm == "S5" else T5(nm, LD)

                def scores(lhs, rhs, msk, nm, neg=False):
                    ps = PS.nxt()
                    for hp in range(2):
                        for cb in range(2):
                            for p in range(4):
                                h = 2 * p + hp
                                kb.mm(ps[tr(cb), hc(h)], lhs[frs(hp), pc(p, cb)], rhs[frs(hp), pc(p, cb)])
                    o_ = newt(nm)
                    if neg:
                        kb.stt(o_[:], ps[:], -1.0, msk, ALU.mult, ALU.mult)
                    else:
                        kb.tt(o_[:], ps[:], msk, ALU.mult)
                    return o_

                def bprod(A, B, nm, add=None, eng="act"):
                    ps = PS.nxt()
                    for cb in range(2):
                        for h in range(8):
                            kb.mm(ps[tr(cb), hc(h)], A[tr(cb), hc(h)], B[tr(cb), hc(h)])
                    o_ = newt(nm)
                    if add is None:
                        kb.copy(o_[:], ps[:], eng=eng)
                    else:
                        kb.tt(o_[:], ps[:], add[:], ALU.add)
                    return o_

                def fprod(pairs):
                    ps = PS.nxt()
                    for cb in range(2):
                        for hp in range(2):
                            for p in range(4):
                                h = 2 * p + hp
                                for i, (A, B) in enumerate(pairs):
                                    kb.mm(ps[frs(hp), pc(p, cb)], A[tr(cb), hc(h)], B[tr(cb), hc(h)], start=(i == 0), stop=(i == len(pairs) - 1))
                    return ps

                LkT = scores(kd, kkd, m_su, "LkT")
                MkT = scores(kd, rd, m_iu, "MkT")
                Nn = scores(bd, kkd, m_su, "S5")
                nMbT = scores(bd, rd, m_iu, "nMbT", neg=True)
                Ll = scores(kkd, bd, m_sl, "S5")
                Y = tmp("S5", (128, 512), 5, dt=LD)
                kb.tt(Y[:], eye8, Nn[:], ALU.subtract)
                Lp, Np = Ll, Nn
                for lvl in range(5):
                    L2 = bprod(Np, Lp, "S5", eng="act")
                    if lvl < 4:
                        N2 = bprod(Lp, Np, "S5", eng="dve")
                    Y = bprod(L2, Y, "S5", add=Y)
                    Lp, Np = L2, (N2 if lvl < 4 else None)
                TT = Y
                Zs = bprod(LkT, V, "Fkkd")
                Ws = bprod(TT, KK, "Fkd", eng="dve")
                U0 = bprod(TT, Zs, "Fbd")
                pP = fprod([(Ws, NBE)])
                PTs = T5("PTs", LD)
                kb.copy(PTs[:], pP[:], eng="act")
                pQ = fprod([(KE, V), (NBE, U0)])
                Qs = T5("LkT", LD)
                kb.copy(Qs[:], pQ[:], eng="dve")
                pR = fprod([(Ws, nMbT)])
                RpT = T5("RpT", LD)
                kb.tt(RpT[:], pR[:], rd[:], ALU.add)
                pOd = PD.nxt()
                pO0, pO1 = pOd[:, 0:512], pOd[:, 512:1024]
                for cb in range(2):
                    for hp in range(2):
                        for p in range(4):
                            h = 2 * p + hp
                            kb.mm(pO0[frs(hp), pc(p, cb)], V[tr(cb), hc(h)], MkT[tr(cb), hc(h)], start=True, stop=False)
                            kb.mm(pO0[frs(hp), pc(p, cb)], U0[tr(cb), hc(h)], nMbT[tr(cb), hc(h)], start=False, stop=True)
                first = (n % TPS == (TPS - 1 if flip else 0))
                if first:
                    kb.ts(H[:], H[:], flg[:, (NSEG if flip else 0) + seg:(NSEG if flip else 0) + seg + 1], None, ALU.mult)
                    kb.copy(Hb[:], H[:], eng="act")
                for cb in range(2):
                    for hp in range(2):
                        for p in range(4):
                            kb.mm(pO1[frs(hp), pc(p, cb)], Hb[frs(hp), p, :], RpT[frs(hp), pc(p, cb)])
                    pH = PS.nxt()
                    for hp in range(2):
                        for p in range(4):
                            kb.mm(pH[frs(hp), p * 64:(p + 1) * 64], PTs[frs(hp), pc(p, cb)], Hb[frs(hp), p, :], start=True, stop=False)
                            kb.mm(pH[frs(hp), p * 64:(p + 1) * 64], identb[frs(hp), frs(hp)], Qs[frs(hp), pc(p, cb)], start=False, stop=True)
                    for p in range(4):
                        kb.stt(H[:, p, :], H[:, p, :], gam[:, p, cb:cb + 1], pH[:, p * 64:(p + 1) * 64], ALU.mult, ALU.add)
                    kb.copy(Hb[:], H[:], eng="act")
                return pOd, zm, bon

            def odd_bwd():
                kb.memset(H[:], 0.0)
                kb.memset(Hb[:], 0.0)
                order = list(range(NT - 1, -1, -1))
                sts = {}
                for i, n in enumerate(order + [None]):
                    if n is not None:
                        sts[n] = stage1(n, True, 1, False, False)
                    if i == 0:
                        continue
                    m = order[i - 1]
                    prv = sts.get(order[i - 2]) if i >= 2 else None
                    halo(sts[m], prv, sts.get(n) if n is not None else None, True)
                    pOd, _, _ = rwkv_tile(sts[m], True, 1, False)
                    o_ = T5("obl")
                    o2 = T5("o2f")
                    kb.copy(o2[:], pOd[:, 0:512], eng="act")
                    for p in range(4):
                        kb.tt(o_[:, p * 128:(p + 1) * 128][:, ::-1], pOd[:, 512 + p * 128:512 + (p + 1) * 128], o2[:, p * 128:(p + 1) * 128], ALU.add)
                        kb.dma(obTT[m][p * 128:(p + 1) * 128, m * 128:(m + 1) * 128], o_[:, p * 128:(p + 1) * 128])
                    if i >= 2:
                        del sts[order[i - 2]]

            def attn_tile(cur, prv, nxt):
                hT, rop, seg = cur["hT"], cur["rop"], cur["seg"]
                qr = T5("qrb", BF16)
                for r in range(4):
                    pq, pqr = projF(hT, OD["cq"] + r * 128), projF(hT, OD["rq"] + r * 128)
                    t1, t2 = tmp("rt1"), tmp("rt2")
                    kb.tt(t1[:], pq[:, 0:128], rop[:, 0, :], ALU.mult)
                    kb.tt(t2[:], pqr[:, 0:128], rop[:, 1, :], ALU.mult)
                    kb.tt(qr[:, r * 128:(r + 1) * 128], t1[:], t2[:], ALU.add)
                pOa = PD.nxt()
                for g in range(2):
                    gr = slice(64 * g, 64 * g + 64)
                    Pts = []
                    for nb, msk in ((prv, m_ge4), (cur, None), (nxt, m_le4)):
                        if nb is None:
                            continue
                        pS_ = PS.nxt()
                        kb.mm(pS_[:], nb["kT"][gr, :], qr[gr, :])
                        Pt = tmp("Pt", (128, 512), 3, dt=BF16)
                        kb.act(Pt[:], pS_[:], AF.Exp, scale=0.125)
                        if msk is not None:
                            if nb["seg"] == seg:
                                kb.tt(Pt[:], Pt[:], msk, ALU.mult)
                            else:
                                sg = max(nb["seg"], seg)
                                kb.tt(Pt[:], Pt[:], msk, ALU.mult)
                                kb.ts(Pt[:], Pt[:], flg[:, sg:sg + 1], None, ALU.mult)
                        Pts.append((Pt, nb["v65"]))
                    for r in range(4):
                        for i, (Pt, v65) in enumerate(Pts):
                            kb.mm(pOa[:, g * 512 + r * 65:g * 512 + (r + 1) * 65], Pt[:, r * 128:(r + 1) * 128], v65[:, g, :],
                                  start=(i == 0), stop=(i == len(Pts) - 1))
                den = tmp("den", (128, 8))
                for g in range(2):
                    kb.tt(den[:, 4 * g:4 * g + 4], pOa[:, slice(g * 512 + 64, g * 512 + 64 + 3 * 65 + 1, 65)], sinkE[:, 4 * g:4 * g + 4], ALU.add)
                kb.recip(den[:], den[:])
                co = T5("cof")
                for g in range(2):
                    pv4 = pOa[:, g * 512:g * 512 + 260]
                    src = Vw(pv4.tile, pv4.ap.rearrange("p (a b) -> p a b", a=4)[:, :, 0:64])
                    kb.tt(v3(co[:, g * 256:(g + 1) * 256], 4), src, bc(den[:, 4 * g:4 * g + 4], (128, 4, 64)), ALU.mult)
                pG = PS.nxt()
                for k in range(8):
                    kb.mm(pG[:], hT[:, k, :], WH["Win"][:, k, OD["cg"]:OD["cg"] + 512], start=(k == 0), stop=(k == 7))
                sg_ = T5("sgf")
                kb.sig(sg_[:], pG[:])
                kb.tt(co[:], co[:], sg_[:], ALU.mult, eng="pool")
                cob = T5("cob", BF16)
                kb.tt(cob[:], co[:], pG[:], ALU.mult)
                return cob

            def odd_fwd():
                kb.memset(H[:], 0.0)
                kb.memset(Hb[:], 0.0)
                order = list(range(NT))
                sts = {}
                for i, n in enumerate(order + [None]):
                    if n is not None:
                        sts[n] = stage1(n, False, 0, True, True)
                    if i == 0:
                        continue
                    m = order[i - 1]
                    cur = sts[m]
                    prv = sts.get(order[i - 2]) if i >= 2 else None
                    nxt = sts.get(n) if n is not None else None
                    halo(cur, prv, nxt, False)
                    pOd, zm, bon = rwkv_tile(cur, False, 0, True)
                    ob = T5("obl")
                    for p in range(4):
                        kb.dma(ob[:, p * 128:(p + 1) * 128], obTT[m][p * 128:(p + 1) * 128, m * 128:(m + 1) * 128])
                    od_ = ob
                    kb.tt(od_[:], pOd[:, 0:512], ob[:], ALU.add)
                    kb.tt(od_[:], pOd[:, 512:1024], od_[:], ALU.add)
                    onT = tmp("onT", (128, 8, 128), 1, dt=BF16)
                    P4 = lambda p: slice(p * 128, (p + 1) * 128)
                    pm_ = PS.nxt()
                    for p in range(4):
                        kb.mm(pm_[:, P4(p)], oblk, od_[:, P4(p)])
                    cen = T5("cen5")
                    kb.stt(cen[:], pm_[:], -1.0 / 64.0, od_[:], ALU.mult, ALU.add)
                    sq5 = T5("sq5")
                    kb.act(sq5[:], cen[:], AF.Square)
                    pv_ = PS.nxt()
                    for p in range(4):
                        kb.mm(pv_[:, P4(p)], oblk, sq5[:, P4(p)])
                    rs5 = T5("rs5")
                    kb.rsqrt(rs5[:], pv_[:], 1.0 / 64.0, 64e-5)
                    kb.tt(cen[:], cen[:], rs5[:], ALU.mult)
                    for p in range(4):
                        kb.ts(cen[:, P4(p)], cen[:, P4(p)], vecT[:, p, 4:5], vecT[:, p, 5:6], ALU.mult, ALU.add)
                    kb.tt(cen[:], cen[:], bon[:], ALU.add, eng=PL)
                    zg = zm[:, 12:16, :]
                    kb.sig(v3(sq5[:], 4), zg)
                    kb.tt(cen[:], cen[:], sq5[:], ALU.mult)
                    kb.tt(onT[:, 4:8, :], v3(cen[:], 4), zg, ALU.mult)
                    co = attn_tile(cur, prv, nxt)
                    pT = PS.nxt()
                    for r in range(4):
                        kb.mm(pT[:, r * 128:(r + 1) * 128], co[:, r * 128:(r + 1) * 128], identb[:])
                    kb.copy(onT[:, 0:4, :], pT[:], eng="act")
                    xr_ = xto.nxt()
                    kb.dma(xr_[:], src[m][rows(m), :])
                    finish2(m, cur["seg"], onT, xr_, yT)
                    if i >= 2:
                        del sts[order[i - 2]]

            odd_bwd()
            odd_fwd()
        if 1 not in layers:
            for n in range(NT):
                t_ = xtR.nxt()
                kb.dma(t_[:], x1T[n][rows(n), :])
                kb.dma(yT[n][rows(n), :], t_[:])
        kb.P.emit()
    return nc


def consts():
    c = np.zeros((128, 512), np.float32)
    c[:, 0:128] = np.eye(128)
    c[:, 128:256] = np.eye(128)[::-1]
    j = np.arange(128)[:, None]
    i = np.arange(128)[None, :]
    c[:, 256:384] = (j <= i)
    c[:, 384:512] = 1.0
    return c


def shared_inputs(w):
    f = lambda a: np.ascontiguousarray(a, dtype=np.float32)
    m = {}
    m["consts"] = consts()
    m["ev_w_mod"] = f(w["ev_w_mod"][0])
    m["ev_b_modT"] = f(w["ev_b_mod"][0].reshape(24, 128).T)
    m["ev_b_mod"] = f(w["ev_b_mod"][0].reshape(1, -1))
    m["ev_w_in"] = f(w["ev_w_in"][0])
    wi = w["ev_w_in"][0]
    m["ev_w_blT"] = f(np.concatenate([wi[:, 3584:3600].T, wi[:, 3600:3616].T], axis=1))
    m["gla_w_gk"] = f(np.concatenate([w["gla_w_gk"][0, 0], w["gla_w_gk"][0, 1]], axis=1))
    m["gla_b_gkT"] = f(w["gla_b_gk"][0].reshape(2, 2, 128).transpose(2, 0, 1).reshape(128, 4))
    m["ev_rows"] = f(np.concatenate([w["hgrn_norm"][0], w["gla_norm"][0], w["ev_ln_g"][0], w["ev_ln_b"][0]]).reshape(1, -1))
    m["ev_w_out"] = f(w["ev_w_out"][0])
    m["lblT"] = f(w["hgrn_lb_logits"].reshape(2, 2, 4, 128).transpose(3, 0, 1, 2).reshape(128, 16))
    return m


def consts2():
    c = np.zeros((128, 6 * 512 + 256), np.float32)
    r = (np.arange(128) % 64)[:, None]
    q = (np.arange(512) % 64)[None, :]
    c[:, 0:512] = r < q
    c[:, 512:1024] = r <= q
    c[:, 1024:1536] = r > q
    c[:, 1536:2048] = r == q
    j = np.arange(128)[:, None]
    i = (np.arange(512) % 128)[None, :]
    c[:, 2048:2560] = j >= i
    c[:, 2560:3072] = j <= i
    a = np.arange(128)
    c[:, 3072:3200] = (a[:, None] // 64) == (a[None, :] // 64)
    cm = np.ones((128, 128), np.float32)
    cm[:, 0] = 0.0
    cm[:, 64] = 0.0
    c[:, 3200:3328] = cm
    return c


def rope_table(pos):
    inv = (10000.0 ** (-np.arange(0, 64, 2, dtype=np.float32) / np.float32(64))).astype(np.float32)
    ang = (pos.astype(np.float32)[:, None] * inv[None, :]).astype(np.float32)
    cos, sin = np.cos(ang).astype(np.float32), np.sin(ang).astype(np.float32)
    cc = np.concatenate([cos, cos], axis=1).T
    ss = np.concatenate([-sin, sin], axis=1).T
    t = np.stack([np.concatenate([cc, cc], 0), np.concatenate([ss, ss], 0)], axis=1)
    return np.ascontiguousarray(t, dtype=np.float32)


def od_shared(w):
    f = lambda a: np.ascontiguousarray(a, dtype=np.float32)
    m = {}
    wi = w["od_w_in"][0]
    qcols = np.concatenate([np.concatenate([np.arange(r * 64, r * 64 + 64), np.arange((4 + r) * 64, (4 + r) * 64 + 64)])
                            for r in range(4)])
    swap = lambda cols: np.concatenate([np.concatenate([cols[i * 64 + 32:i * 64 + 64], cols[i * 64:i * 64 + 32]])
                                        for i in range(len(cols) // 64)])
    kcols = np.arange(512, 640)
    order = np.concatenate([qcols, np.arange(512, 3424), swap(qcols), swap(kcols)])
    m["od_w_in"] = f(wi[:, order])
    m["od_w_mod"] = f(w["od_w_mod"][0])
    m["od_b_modT"] = f(w["od_b_mod"][0].reshape(24, 128).T)
    m["od_b_mod"] = f(w["od_b_mod"][0].reshape(1, -1))
    m["od_rows"] = f(np.concatenate([np.zeros(D, np.float32), w["od_ln_g"][0], w["od_ln_b"][0]]).reshape(1, -1))
    m["od_w_out"] = f(w["od_w_out"][0])
    m["consts2"] = consts2()
    mix = w["rwkv_mix"][0]
    mt = np.zeros((128, 2, 19), np.float32)
    starts = [0, 128, 256, 384, 512, 640, 768, 896, 1024, 1152, 1280, 1408, 1632, 1760, 1888, 2016, 1536, 1568, 1600]
    for j, st in enumerate(starts):
        n_ = 32 if j >= 16 else 128
        mt[0:n_, :, j] = mix[:, st:st + n_].T
    m["mixT"] = mt
    vecs = [w["rwkv_a0"][0], w["rwkv_k_k"][0], w["rwkv_k_a"][0], w["rwkv_r_k"][0], w["rwkv_ln_w"][0], w["rwkv_ln_b"][0],
            w["rwkv_w0"][0, 0], w["rwkv_w0"][0, 1]]
    m["vecT"] = f(np.stack([v.reshape(4, 128) for v in vecs], axis=-1).transpose(1, 0, 2))
    m["w_w2"] = f(w["rwkv_w_w2"][0].transpose(1, 0, 2))
    m["a_w2"] = f(w["rwkv_a_w2"][0])
    m["sink"] = f(w["swa_sink"][0].reshape(1, 8))
    return m


_NC_CACHE = {}


def kernel(**inputs):
    NSEG, L = 4, 4096
    w = {k: np.asarray(v) for k, v in inputs.items()}
    xp, xs, cp, cs = w["x_prompt"], w["x_sample"], w["c_prompt"], w["c_sample"]
    shared = shared_inputs(w)
    shared.update(od_shared(w))
    plan = []
    for b in range(2):
        plan.append([("p", b, k) for k in range(4)])
    samp = [[0, 1, 2], [3, 4, 5], [6, 7, 8], [9, 10, 11], [12, 13], [14, 15]]
    for lst in samp:
        plan.append([("s", i, 0) for i in lst] + [("s", lst[0], 0)] * (4 - len(lst)))
    in_maps = []
    for core in range(8):
        m = dict(shared)
        xs_, cs_, pos = [], [], []
        fl = np.zeros((128, 2 * NSEG), np.float32)
        for s_, (kind, b, k) in enumerate(plan[core]):
            if kind == "p":
                xs_.append(xp[b, k * L:(k + 1) * L])
                cs_.append(cp[b])
                pos.append(np.arange(k * L, (k + 1) * L, dtype=np.float32))
                if k > 0:
                    fl[:, s_] = 1.0
                if k < 3:
                    fl[:, NSEG + s_] = 1.0
            else:
                xs_.append(xs[b])
                cs_.append(cs[b])
                pos.append(np.arange(L, dtype=np.float32))
        m["x"] = np.ascontiguousarray(np.concatenate(xs_, axis=0), dtype=np.float32)
        cc = np.stack(cs_, axis=0)
        m["cT"] = np.ascontiguousarray(cc.reshape(NSEG, 8, 128).transpose(2, 1, 0).reshape(128, 8 * NSEG), dtype=np.float32)
        m["flags"] = fl
        m["ropeT"] = rope_table(np.concatenate(pos))
        in_maps.append(m)
    if "nc" not in _NC_CACHE:
        _NC_CACHE["nc"] = build(NSEG, L)
    res = run_bass_kernel_spmd(_NC_CACHE["nc"], in_maps, core_ids=list(range(8)))
    yp = np.zeros_like(xp)
    ys = np.zeros_like(xs)
    for core in range(8):
        y = res.results[core]["y"].reshape(NSEG, L, D)
        seen = set()
        for s_, (kind, b, k) in enumerate(plan[core]):
            if kind == "p":
                yp[b, k * L:(k + 1) * L] = y[s_]
            elif b not in seen:
                ys[b] = y[s_]
                seen.add(b)
    return (yp, ys)
```

```python
import numpy as np
from contextlib import ExitStack
import concourse.bass as bass
import concourse.mybir as mybir
from concourse.bass_utils import run_bass_kernel_spmd
from concourse.alu_op_type import AluOpType as ALU

F32 = mybir.dt.float32
BF16 = mybir.dt.bfloat16
AF = mybir.ActivationFunctionType
D = 1024
ALPHA = 4 ** 0.25
NDMA = 24


class Tl:
    def __init__(self, t, name):
        self.t, self.name, self.w, self.r = t, name, {}, {}

    def __getitem__(self, idx):
        v = Vw(self, self.t[idx])
        i0 = idx[0] if isinstance(idx, tuple) else idx
        if isinstance(i0, slice) and i0.start is not None:
            v.p0, v.pn = i0.start, (i0.stop - i0.start)
        return v


class Vw:
    def __init__(self, tile, ap):
        self.tile, self.ap = tile, ap
        self.p0, self.pn = 0, 128

    def __getitem__(self, idx):
        v = Vw(self.tile, self.ap[idx])
        v.p0, v.pn = self.p0, self.pn
        i0 = idx[0] if isinstance(idx, tuple) else idx
        if isinstance(i0, slice) and i0.start is not None:
            v.p0, v.pn = self.p0 + i0.start, (i0.stop - i0.start)
        return v


class Op:
    __slots__ = ("eng", "fn", "deps", "sig", "sigval", "idx", "dma_k")

    def __init__(self, eng, fn, idx):
        self.eng, self.fn, self.idx = eng, fn, idx
        self.deps, self.sig, self.sigval, self.dma_k = set(), False, 0, -1


class Prog:
    ENGS = ("pe", "act", "dve", "pool")

    def __init__(self, nc):
        self.nc, self.ops, self.ndma = nc, [], 0

    def add(self, eng, fn, reads=(), writes=()):
        idx = len(self.ops)
        op = Op(eng, fn, idx)
        isdma = eng == "sp"
        key = ("dma", idx) if isdma else eng
        raw, oth = set(), set()
        for v in reads:
            if v is None or not isinstance(v, Vw):
                continue
            raw.update(v.tile.w.values())
            if getattr(v.tile, "psum", False):
                oth.update(i for k, i in v.tile.r.items() if k != key)
        for v in writes:
            oth.update(v.tile.w.values())
            oth.update(v.tile.r.values())
        for d in raw:
            p = self.ops[d]
            if p.eng == eng and eng == "pe":
                continue
            op.deps.add(d)
        for d in oth:
            p = self.ops[d]
            if p.eng == eng and not isdma:
                continue
            op.deps.add(d)
        for v in reads:
            if v is None or not isinstance(v, Vw):
                continue
            v.tile.r[key] = idx
        for v in writes:
            tl = v.tile
            tl.r = {}
            if isdma:
                tl.w = {k: i for k, i in tl.w.items() if not isinstance(k, tuple)}
            tl.w[key] = idx
        if isdma:
            op.dma_k = self.ndma
            self.ndma += 1
        self.ops.append(op)
        return op

    def setup(self, es):
        nc = self.nc
        self.sems = {e: es.enter_context(nc.semaphore("s_" + e)) for e in self.ENGS}
        self.dsem = [es.enter_context(nc.semaphore("d%d" % i)) for i in range(NDMA)]
        self.cnt = {e: 0 for e in self.ENGS}
        self.waited = {e: {} for e in self.ENGS + ("sp",)}
        self.done = 0

    def emit(self):
        nc = self.nc
        ops, sems, dsem = self.ops, self.sems, self.dsem
        lo = self.done
        for op in ops[lo:]:
            op.deps = {d for d in op.deps if d >= lo}
            for d in op.deps:
                ops[d].sig = True
        for op in ops[lo:]:
            if op.eng != "sp" and op.sig:
                self.cnt[op.eng] += 1
                op.sigval = self.cnt[op.eng]

        def need(op):
            req = {}
            for d in op.deps:
                p = ops[d]
                if p.eng == "sp":
                    s, v = dsem[p.dma_k % NDMA], 16 * (p.dma_k // NDMA + 1)
                else:
                    s, v = sems[p.eng], p.sigval
                k = id(s)
                if k not in req or req[k][1] < v:
                    req[k] = (s, v)
            return req

        def run(engname, e):
            waited = self.waited[engname]
            for op in ops[lo:]:
                if op.eng != engname:
                    continue
                req = need(op)
                if engname == "sp" and op.dma_k >= NDMA:
                    s = dsem[op.dma_k % NDMA]
                    v = 16 * (op.dma_k // NDMA)
                    if id(s) not in req or req[id(s)][1] < v:
                        req[id(s)] = (s, v)
                for k, (s, v) in req.items():
                    if waited.get(k, 0) < v:
                        e.wait_ge(s, v)
                        waited[k] = v
                ins = op.fn(e)
                if engname == "sp":
                    ins.then_inc(dsem[op.dma_k % NDMA], 16)
                elif op.sig:
                    ins.then_inc(sems[engname], 1)
            if engname == "sp":
                for i in range(min(NDMA, self.ndma)):
                    last = ((self.ndma - 1 - i) // NDMA) * NDMA + i
                    v = 16 * (last // NDMA + 1)
                    if waited.get(id(dsem[i]), 0) < v:
                        e.wait_ge(dsem[i], v)
                        waited[id(dsem[i])] = v

        with nc.Block() as block:
            @block.sync
            def _(e):
                run("sp", e)

            @block.tensor
            def _(e):
                run("pe", e)

            @block.scalar
            def _(e):
                run("act", e)

            @block.vector
            def _(e):
                run("dve", e)

            @block.gpsimd
            def _(e):
                run("pool", e)
        self.done = len(ops)


def v3(vw, a):
    return Vw(vw.tile, vw.ap.rearrange("p (a b) -> p a b", a=a))


def bc(vw, shape):
    return Vw(vw.tile, vw.ap[:, :, None].broadcast_to(list(shape)))


def _ap(v):
    return v.ap if isinstance(v, Vw) else v


class K:
    def __init__(self, nc, es):
        self.nc, self.es, self.P = nc, es, Prog(nc)
        self.n = 0

    def sb(self, shape, dt=F32, name=None):
        self.n += 1
        nm = "%s%d" % (name or "t", self.n)
        return Tl(self.es.enter_context(self.nc.sbuf_tensor(nm, list(shape), dt)), nm)

    def ps(self, shape, name=None):
        self.n += 1
        nm = "%s%d" % (name or "p", self.n)
        t = Tl(self.es.enter_context(self.nc.psum_tensor(nm, list(shape), F32)), nm)
        t.psum = True
        return t

    def mm(self, out, lhsT, rhs, start=True, stop=True):
        rg = (lhsT.p0, lhsT.pn, out.p0, out.pn)
        last = getattr(self, "last_rg", (0, 128, 0, 128))
        tiled = lambda g: g[1] < 128 or g[3] < 128
        if rg != last and (tiled(rg) or tiled(last)) and getattr(self, "dummy", None) is not None:
            dps, dl, dr_ = self.dummy
            self.P.add("pe", lambda e: e.matmul(dps.ap, dl.ap, dr_.ap, start=True, stop=True), [dl], [dps])
        self.last_rg = rg
        self.P.add("pe", lambda e: e.matmul(out.ap, lhsT.ap, rhs.ap, start=start, stop=stop),
                   [lhsT, rhs], [out])

    def act(self, out, in_, func, scale=1.0, bias=0.0, accum=None, eng="act"):
        kw = {}
        if accum is not None:
            kw["accum_out"] = accum.ap
        self.P.add(eng, lambda e: e.activation(out.ap, in_.ap, func, bias=_ap(bias), scale=_ap(scale), **kw),
                   [in_, scale, bias], [out] + ([accum] if accum is not None else []))

    def tt(self, out, a, b, op, eng="dve"):
        self.P.add(eng, lambda e: e.tensor_tensor(out.ap, a.ap, b.ap, op), [a, b], [out])

    def ts(self, out, a, s1, s2, op0, op1=None, eng="dve"):
        if op1 is None:
            self.P.add(eng, lambda e: e.tensor_scalar(out.ap, a.ap, _ap(s1), None, op0), [a, s1], [out])
        else:
            self.P.add(eng, lambda e: e.tensor_scalar(out.ap, a.ap, _ap(s1), _ap(s2), op0, op1), [a, s1, s2], [out])

    def stt(self, out, a, s, b, op0, op1):
        self.P.add("dve", lambda e: e.scalar_tensor_tensor(out.ap, a.ap, _ap(s), b.ap, op0, op1), [a, s, b], [out])

    def scan(self, out, d0, d1, init, op0, op1):
        self.P.add("dve", lambda e: e.tensor_tensor_scan(out.ap, d0.ap, d1.ap, init, op0, op1), [d0, d1], [out])

    def copy(self, out, in_, eng="dve"):
        if eng == "act":
            self.P.add("act", lambda e: e.copy(out.ap, in_.ap), [in_], [out])
        else:
            self.P.add(eng, lambda e: e.tensor_copy(out.ap, in_.ap), [in_], [out])

    def memset(self, out, val, eng="dve"):
        self.P.add(eng, lambda e: e.memset(out.ap, val), [], [out])

    def sig(self, out, in_, nbias=0.0, e_out=None):
        e = e_out if e_out is not None else out
        self.act(e, in_, AF.Exp, scale=-1.0, bias=nbias)
        self.ts(out, e, 1.0, None, ALU.add)
        self.recip(out, out)

    def rsqrt(self, out, in_, scale, eps):
        self.act(out, in_, AF.Ln, scale=scale, bias=eps)
        self.act(out, out, AF.Exp, scale=-0.5)

    def recip(self, out, in_):
        self.P.add("dve", lambda e: e.reciprocal(out.ap, in_.ap), [in_], [out])

    def dma(self, out, in_):
        self.P.add("sp", lambda e: e.dma_start(out=out.ap, in_=in_.ap), [in_], [out])


class Rot:
    def __init__(self, items):
        self.items, self.i = items, 0

    def nxt(self):
        self.i += 1
        return self.items[(self.i - 1) % len(self.items)]


EV_COLS = dict(aq=0, ai=512, af_f=1024, af_b=1536, a_gate=2048, bq=2560, bk=2816, bv=3072, bl_f=3584, bl_b=3600,
               b_gate=3616, zf=4128, zb=4384)
WC = 4640


def build(NSEG, L, layers=(0, 1), LDT=BF16, use_pool=True):
    nc = bass.Bass("TRN2", target_bir_lowering=False)
    es = ExitStack()
    kb = K(nc, es)
    TT_ = NSEG * L
    NT = TT_ // 128
    TPS = L // 128

    def din(name, shape):
        return Tl(nc.dram_tensor(name, list(shape), F32, kind="ExternalInput").ap(), name)

    x_d = nc.dram_tensor("x", [TT_, D], F32, kind="ExternalInput").ap()
    y_d = nc.dram_tensor("y", [TT_, D], F32, kind="ExternalOutput").ap()
    x1_d = nc.dram_tensor("x1s", [TT_, D], F32, kind="Internal").ap()
    ob_d = nc.dram_tensor("obs", [TT_, D], F32, kind="Internal").ap()
    xT = [Tl(x_d, "x%d" % i) for i in range(NT)]
    yT = [Tl(y_d, "y%d" % i) for i in range(NT)]
    x1T = [Tl(x1_d, "x1%d" % i) for i in range(NT)]
    obT = [Tl(ob_d, "ob%d" % i) for i in range(NT)]
    rows = lambda n: slice(n * 128, (n + 1) * 128)

    cT_d = din("cT", [128, 8 * NSEG])
    flg_d = din("flags", [128, 2 * NSEG])
    cst_d = din("consts", [128, 4 * 128])
    ev = {}
    for nm, shp in [("ev_w_mod", [D, 3 * D]), ("ev_b_modT", [128, 24]), ("ev_b_mod", [1, 3 * D]), ("ev_w_in", [D, 4128]),
                    ("ev_w_blT", [16, 2 * D]), ("gla_w_gk", [16, 512]), ("gla_b_gkT", [128, 4]), ("lblT", [128, 16]),
                    ("ev_rows", [1, 3 * D]), ("ev_w_out", [D, D])]:
        ev[nm] = din(nm, shp)

    od = {}
    for nm, shp in [("od_w_mod", [D, 3 * D]), ("od_b_modT", [128, 24]), ("od_b_mod", [1, 3 * D]), ("od_w_in", [D, 4064]),
                    ("od_rows", [1, 3 * D]), ("od_w_out", [D, D]), ("consts2", [128, 6 * 512 + 256]), ("mixT", [128, 2, 19]),
                    ("vecT", [128, 4, 8]), ("w_w2", [32, 2, 512]), ("a_w2", [32, 512]), ("sink", [1, 8]),
                    ("ropeT", [128, 2, TT_])]:
        od[nm] = din(nm, shp)

    with es:
        kb.P.setup(es)
        cst = kb.sb([128, 512], F32, "cst")
        kb.dma(cst[:], cst_d[:, :])
        ident, Jm, mask, ones = cst[:, 0:128], cst[:, 128:256], cst[:, 256:384], cst[:, 384:512]
        flg = kb.sb([128, 2 * NSEG], F32, "flg")
        kb.dma(flg[:], flg_d[:, :])
        cT = kb.sb([128, 8 * NSEG], F32, "cT")
        kb.dma(cT[:], cT_d[:, :])
        scT = kb.sb([128, 8 * NSEG], F32, "scT")
        kb.sig(scT[:], cT[:])
        kb.tt(scT[:], scT[:], cT[:], ALU.mult)
        zer = kb.sb([128, 128], F32, "zer")
        kb.memset(zer[:], 0.0)

        PD = Rot([kb.ps([128, 1024], "pd") for _ in range(2)])
        PS = Rot([kb.ps([128, 512], "ps") for _ in range(3)])
        psd = kb.ps([128, 512], "psd")
        identg = kb.sb([128, 128], BF16, "identg")
        kb.copy(identg[:], ident)
        kb.dummy = (psd[:, 0:2], identg[:], identg[:, 0:2])

        WH = {}
        Wout = kb.sb([128, 8, D], BF16, "Wout")
        modT = kb.sb([128, 24 * NSEG], F32, "modT")
        bmT = kb.sb([128, 24], F32, "bmT")
        g1b = kb.sb([128, NSEG, D], BF16, "g1b")

        tmpR = {}

        TMPN = [2]

        def tmp(nm, shape=(128, 128), n=None, dt=F32):
            n = n or TMPN[0]
            if nm not in tmpR:
                tmpR[nm] = Rot([kb.sb(shape, dt, nm) for _ in range(n)])
            return tmpR[nm].nxt()

        def load_weights(w_in_d, ncols, w_out_d, w_mod_d, b_modT_d, b_mod_d, rows_d, extra=None, ro=0):
            WH["ro"] = ro
            ses = ExitStack()
            old_es, old_tmp = kb.es, dict(tmpR)
            kb.es = ses
            stg = Rot([kb.sb([128, 2048], F32, "stg") for _ in range(2)])
            brow = kb.sb([1, 3 * D], F32, "brow")
            rrow = kb.sb([1, 3 * D], F32, "rrow")
            for k in range(8):
                for c0 in range(0, ncols, 2048):
                    c1 = min(ncols, c0 + 2048)
                    s = stg.nxt()
                    kb.dma(s[:, 0:c1 - c0], w_in_d[k * 128:(k + 1) * 128, c0:c1])
                    kb.copy(WH["Win"][:, k, c0:c1], s[:, 0:c1 - c0], eng="pool" if (k % 2) else "dve")
                s = stg.nxt()
                kb.dma(s[:, 0:D], w_out_d[k * 128:(k + 1) * 128, :])
                kb.copy(Wout[:, k, :], s[:, 0:D], eng="act")
            kb.dma(bmT[:], b_modT_d[:, :])
            kb.dma(brow[:], b_mod_d[:, :])
            kb.dma(rrow[:], rows_d[:, :])
            pm = PD.nxt()
            for jb in range(12):
                wm = stg.nxt()
                for k in range(8):
                    kb.dma(wm[:, k * 256:(k + 1) * 256], w_mod_d[k * 128:(k + 1) * 128, jb * 256:(jb + 1) * 256])
                for jj in range(2):
                    j = jb * 2 + jj
                    for k in range(8):
                        kb.mm(pm[:, j * NSEG:(j + 1) * NSEG], wm[:, k * 256 + jj * 128:k * 256 + (jj + 1) * 128],
                              scT[:, k * NSEG:(k + 1) * NSEG], start=(k == 0), stop=(k == 7))
            for j in range(24):
                kb.act(modT[:, j * NSEG:(j + 1) * NSEG], pm[:, j * NSEG:(j + 1) * NSEG], AF.Identity,
                       bias=bmT[:, j:j + 1], scale=1.0)
            kb.ts(modT[:, 8 * NSEG:16 * NSEG], modT[:, 8 * NSEG:16 * NSEG], 1.0, None, ALU.add)
            for s_ in range(NSEG):
                pg = PD.nxt()
                for q4 in range(4):
                    wm = stg.nxt()
                    for k in range(8):
                        kb.dma(wm[:, k * 256:(k + 1) * 256], w_mod_d[k * 128:(k + 1) * 128, 2 * D + q4 * 256:2 * D + (q4 + 1) * 256])
                    for k in range(8):
                        scB = tmp("scB")
                        kb.act(scB[:], zer[:], AF.Identity, bias=scT[:, k * NSEG + s_:k * NSEG + s_ + 1], scale=1.0)
                        kb.mm(pg[:, q4 * 256:(q4 + 1) * 256], scB[:], wm[:, k * 256:(k + 1) * 256], start=(k == 0), stop=False)
                    kb.mm(pg[:, q4 * 256:(q4 + 1) * 256], ones[0:1, :], brow[0:1, 2 * D + q4 * 256:2 * D + (q4 + 1) * 256],
                          start=False, stop=True)
                kb.ts(g1b[:, s_, :], pg[:], 1.0, None, ALU.add)
            for r in range(ro, 3):
                pg = PD.nxt()
                for hf in range(2):
                    kb.mm(pg[:, hf * 512:(hf + 1) * 512], ones[0:1, :], rrow[0:1, r * D + hf * 512:r * D + (hf + 1) * 512])
                kb.copy(WH["rowsb"][:, r - ro, :], pg[:], eng="act")
            if extra is not None:
                extra()
            kb.P.emit()
            ses.close()
            kb.es = old_es
            tmpR.clear()
            tmpR.update(old_tmp)

        NB = 2
        xtR = Rot([kb.sb([128, D], F32, "xt") for _ in range(2)])
        hTR = Rot([kb.sb([128, 8, 128], BF16, "hT") for _ in range(NB)])
        from contextlib import contextmanager

        @contextmanager
        def layer_scope():
            ses = ExitStack()
            old_es, old_tmp = kb.es, dict(tmpR)
            kb.es = ses
            try:
                yield
            finally:
                kb.P.emit()
                ses.close()
                kb.es = old_es
                tmpR.clear()
                tmpR.update(old_tmp)

        EVT = {}

        def gla_unit(hT, dr, flip, psO, ocol, V, vcol, qsrc, ksrc, zsrc, S, sidx, heads, kind):
            lbT, bgk = EVT["lbT"], EVT["bgk"]
            def proj(col):
                p = PS.nxt()
                for k in range(8):
                    kb.mm(p[:, 0:128], WH["Win"][:, k, col:col + 128], hT[:, k, :], start=(k == 0), stop=(k == 7))
                return p
            pq = proj(qsrc)
            q = tmp("q")
            sn = tmp("sn")
            lf = tmp("lf")
            if kind == "A":
                h = sidx
                kb.sig(q[:], pq[:, 0:128])
                kb.tt(q[:], q[:], pq[:, 0:128], ALU.mult)
                yield
                pf = proj(zsrc)
                sg = tmp("sg")
                ee = tmp("ee")
                kb.sig(sg[:], pf[:, 0:128], e_out=ee[:])
                yield
                kb.act(lf[:], sg[:], AF.Ln, scale=lbT[:, dr * 8 + 4 + h:dr * 8 + 5 + h], bias=lbT[:, dr * 8 + h:dr * 8 + h + 1])
                kb.stt(sn[:], ee[:], lbT[:, dr * 8 + 4 + h:dr * 8 + 5 + h], sg[:], ALU.mult, ALU.mult)
                esc = 1.0
                sop = ALU.add
            else:
                p_ = sidx
                kb.act(q[:], pq[:, 0:128], AF.Identity, scale=0.125)
                yield
                pk = proj(ksrc)
                kb.copy(sn[:], pk[:, 0:128], eng="act")
                yield
                pf = proj(zsrc)
                sg = tmp("sg")
                kb.act(sg[:], pf[:, 0:128], AF.Exp, scale=-1.0, bias=EVT["nbgk"][:, dr * 2 + p_:dr * 2 + p_ + 1])
                yield
                kb.act(lf[:], sg[:], AF.Ln, bias=1.0)
                esc = 1.0 / 16.0
                sop = ALU.subtract
            yield
            B = tmp("B")
            kb.scan(B[:], ones, lf[:], 0.0, ALU.mult, sop)
            nb = tmp("nb", (128, 4))
            kb.ts(nb[:, 0:1], B[:, 63:64], -esc, None, ALU.mult)
            kb.ts(nb[:, 1:2], B[:, 63:64], esc, None, ALU.mult)
            kb.ts(nb[:, 2:3], B[:, 127:128], esc, None, ALU.mult)
            yield
            E1, E1n, E2, E3 = tmp("E1"), tmp("E1n"), tmp("E2"), tmp("E3")
            kb.act(E1[:], B[:], AF.Exp, scale=esc, bias=nb[:, 0:1])
            kb.act(E1n[:], B[:], AF.Exp, scale=-esc, bias=nb[:, 1:2])
            yield
            kb.act(E2[:], B[:], AF.Exp, scale=esc)
            kb.act(E3[:], B[:], AF.Exp, scale=-esc, bias=nb[:, 2:3])
            qd, qi, kd, ke = tmp("qd", dt=BF16), tmp("qi", dt=BF16), tmp("kd", dt=BF16), tmp("ke", dt=BF16)
            kb.tt(qd[:], q[:], E1[:], ALU.mult, eng="pool")
            kb.tt(kd[:], sn[:], E1n[:], ALU.mult, eng="pool")
            yield
            qi_o = qi[:, ::-1] if flip else qi[:]
            kb.tt(qi_o, q[:], E2[:], ALU.mult)
            kb.tt(ke[:], sn[:], E3[:], ALU.mult)
            nh = len(heads)
            dk = 128 // nh
            yield
            pT = PS.nxt()
            kb.mm(pT[:, 0:128], ke[:], identg[:])
            keT = tmp("keT", dt=BF16)
            kb.copy(keT[:], pT[:, 0:128], eng="act")
            for hi, hd in enumerate(heads):
                yield
                pr = slice(hi * dk, (hi + 1) * dk)
                pA = PS.nxt()
                kb.mm(pA[:, 0:128], kd[pr, :], qd[pr, :])
                A = tmp("A", dt=BF16)
                A_o = A[:, ::-1] if flip else A[:]
                kb.tt(A_o, pA[:, 0:128], mask, ALU.mult)
                oc = slice(ocol + hd * 128, ocol + (hd + 1) * 128)
                vc = slice(vcol + hd * 128, vcol + (hd + 1) * 128)
                Sh = S[pr, hd, :]
                Shb = EVT["Sb"][id(S)][pr, hd, :]
                kb.mm(psO[:, oc], A[:], V[:, vc], start=True, stop=False)
                kb.mm(psO[:, oc], qi[pr, :], Shb, start=False, stop=True)
                yield
                pS_ = PS.nxt()
                kb.mm(pS_[pr, 0:128], keT[:, pr], V[:, vc])
                kb.stt(Sh, Sh, E2[pr, 127:128], pS_[pr, 0:128], ALU.mult, ALU.add)
                kb.copy(Shb, Sh, eng="act")

        def even_sweep(dr):
            S_A, S_B, oacc, VR = EVT["S_A"], EVT["S_B"], EVT["oacc"], EVT["VR"]
            flip = dr == 1
            order = range(NT - 1, -1, -1) if flip else range(NT)
            xsrc = xT
            kb.memset(S_A[:], 0.0)
            kb.memset(S_B[:], 0.0)
            for sb_ in EVT["Sb"].values():
                kb.memset(sb_[:], 0.0)
            PDs = PD.items
            order = list(order)

            def stageA(n):
                seg = n // TPS
                xt = xtR.nxt()
                kb.dma(xt[:], xsrc[n][rows(n), :])
                pX = PDs[1]
                for k in range(8):
                    kb.mm(pX[:, k * 128:(k + 1) * 128], xt[:, k * 128:(k + 1) * 128], Jm if flip else ident)
                hT = hTR.nxt()
                for k in range(8):
                    kb.act(hT[:, k, :], pX[:, k * 128:(k + 1) * 128], AF.Identity,
                           scale=modT[:, (8 + k) * NSEG + seg:(8 + k) * NSEG + seg + 1],
                           bias=modT[:, k * NSEG + seg:k * NSEG + seg + 1])
                V = VR.nxt()
                pV = PDs[1]
                for hf, col in enumerate((EV_COLS["ai"], EV_COLS["bv"])):
                    for k in range(8):
                        kb.mm(pV[:, hf * 512:(hf + 1) * 512], hT[:, k, :], WH["Win"][:, k, col:col + 512],
                              start=(k == 0), stop=(k == 7))
                kb.copy(V[:], pV[:], eng="act")
                return dict(xt=xt, hT=hT, V=V, seg=seg)

            nxtA = stageA(order[0])
            for i_n, n in enumerate(order):
                cur = nxtA
                seg, xt, hT, V = cur["seg"], cur["xt"], cur["hT"], cur["V"]
                first = (n % TPS == (TPS - 1 if flip else 0))
                if first:
                    fc = flg[:, (NSEG if flip else 0) + seg:(NSEG if flip else 0) + seg + 1]
                    kb.ts(S_A[:], S_A[:], fc, None, ALU.mult)
                    kb.ts(S_B[:], S_B[:], fc, None, ALU.mult)
                    kb.copy(EVT["Sb"][id(S_A)][:], S_A[:], eng="act")
                    kb.copy(EVT["Sb"][id(S_B)][:], S_B[:], eng="act")
                psO = PDs[0]
                def mk(u):
                    if u < 4:
                        return gla_unit(hT, dr, flip, psO, 0, V, 0, EV_COLS["aq"] + u * 128, None,
                                        EV_COLS["af_b" if flip else "af_f"] + u * 128, S_A, u, [u], "A")
                    p_ = u - 4
                    return gla_unit(hT, dr, flip, psO, 512, V, 512, EV_COLS["bq"] + p_ * 128, EV_COLS["bk"] + p_ * 128,
                                    EV_COLS["zb" if flip else "zf"] + p_ * 128, S_B, p_, [2 * p_, 2 * p_ + 1], "B")
                for pi, pair in enumerate(((0, 1, 2, 3), (4, 5))):
                    gens = [mk(u) for u in pair]
                    while gens:
                        for g_ in list(gens):
                            try:
                                next(g_)
                            except StopIteration:
                                gens.remove(g_)
                    if pi == 0 and i_n + 1 < len(order):
                        nxtA = stageA(order[i_n + 1])
                o = oacc.nxt()
                if flip:
                    kb.copy(o[:], psO[:], eng="act")
                    kb.dma(obT[n][rows(n), :], o[:])
                    continue
                ob = tmp("ob", (128, D), 1)
                kb.dma(ob[:], obT[n][rows(n), :])
                kb.tt(o[:], psO[:], ob[:], ALU.add)
                ss = tmp("ss", (128, 8))
                kb.act(ob[:], o[:], AF.Square)
                ob3 = v3(ob[:], 8)
                kb.P.add("dve", lambda e, ss=ss, ob3=ob3: e.tensor_reduce(ss.t[:, 0:8], ob3.ap, mybir.AxisListType.X, ALU.add),
                         [ob3], [ss[:]])
                rs = tmp("rs", (128, 8))
                kb.rsqrt(rs[:], ss[:], 1.0 / 128.0, 1e-6)
                pG = PDs[1]
                for hf, col in enumerate((EV_COLS["a_gate"], EV_COLS["b_gate"])):
                    for k in range(8):
                        kb.mm(pG[:, hf * 512:(hf + 1) * 512], hT[:, k, :], WH["Win"][:, k, col:col + 512],
                              start=(k == 0), stop=(k == 7))
                sgt = tmp("sgt", (128, D), 1)
                kb.sig(sgt[:], pG[:])
                on = o
                kb.tt(v3(on[:], 8), v3(o[:], 8), bc(rs[:, 0:8], (128, 8, 128)), ALU.mult)
                kb.tt(on[:], on[:], WH["rowsb"][:, 0, :], ALU.mult)
                kb.tt(on[:], on[:], sgt[:], ALU.mult, eng="pool")
                kb.tt(on[:], on[:], pG[:], ALU.mult)
                finish(n, seg, on, xt, x1T, PDs)

        def finish(n, seg, on, xt, dstT, PDs):
            pT = PDs[0]
            for k in range(8):
                kb.mm(pT[:, k * 128:(k + 1) * 128], on[:, k * 128:(k + 1) * 128], ident)
            onT = tmp("onT", (128, 8, 128), 1, dt=BF16)
            kb.copy(onT[:, 0:4, :], pT[:, 0:512], eng="act")
            kb.copy(onT[:, 4:8, :], pT[:, 512:1024], eng="dve")
            finish2(n, seg, onT, xt, dstT, PDs[1])

        def finish2(n, seg, onT, xt, dstT, pY=None):
            if pY is None:
                pY = PD.nxt()
            for hf in range(2):
                for k in range(8):
                    kb.mm(pY[:, hf * 512:(hf + 1) * 512], onT[:, k, :], Wout[:, k, hf * 512:(hf + 1) * 512],
                          start=(k == 0), stop=(k == 7))
            kb.tt(pY[:], pY[:], g1b[:, seg, :], ALU.mult)
            r = xt
            kb.stt(r[:], xt[:], ALPHA, pY[:], ALU.mult, ALU.add)
            st = tmp("st", (128, 12))
            kb.P.add("dve", lambda e: e.bn_stats(st.t[:, 0:6], r.t[:, 0:512]), [r[:]], [st[:]])
            kb.P.add("dve", lambda e: e.bn_stats(st.t[:, 6:12], r.t[:, 512:1024]), [r[:]], [st[:]])
            mv = tmp("mv", (128, 4))
            kb.P.add("dve", lambda e: e.bn_aggr(mv.t[:, 0:2], st.t[:, 0:12]), [st[:]], [mv[:]])
            kb.rsqrt(mv[:, 3:4], mv[:, 1:2], 1.0, 1e-5)
            yo = r
            ro = WH.get("ro", 0)
            kb.ts(yo[:], r[:], mv[:, 0:1], mv[:, 3:4], ALU.subtract, ALU.mult)
            kb.tt(yo[:], yo[:], WH["rowsb"][:, 1 - ro, :], ALU.mult, eng="pool")
            kb.tt(yo[:], yo[:], WH["rowsb"][:, 2 - ro, :], ALU.add)
            kb.dma(dstT[n][rows(n), :], yo[:])

        if 0 in layers:
          with layer_scope():
            WH["Win"] = kb.sb([128, 8, WC], BF16, "Win")
            WH["rowsb"] = kb.sb([128, 3, D], F32, "rowsb")
            S_A = EVT["S_A"] = kb.sb([128, 4, 128], F32, "S_A")
            S_B = EVT["S_B"] = kb.sb([128, 4, 128], F32, "S_B")
            lbT = EVT["lbT"] = kb.sb([128, 16], F32, "lbT")
            bgk = EVT["bgk"] = kb.sb([128, 4], F32, "bgk")
            EVT["oacc"] = Rot([kb.sb([128, D], F32, "oacc") for _ in range(1)])
            EVT["VR"] = Rot([kb.sb([128, D], BF16, "V") for _ in range(2)])
            S_Ab = kb.sb([128, 4, 128], BF16, "S_Ab")
            S_Bb = kb.sb([128, 4, 128], BF16, "S_Bb")
            EVT["Sb"] = {id(S_A): S_Ab, id(S_B): S_Bb}
            def ev_extra():
                wbl = kb.sb([16, 2 * D], F32, "wbl")
                w2 = kb.sb([16, 512], F32, "w2")
                kb.dma(wbl[:], ev["ev_w_blT"][:, :])
                kb.dma(w2[:], ev["gla_w_gk"][:, :])
                for dr in range(2):
                    for k in range(8):
                        p = PS.nxt()
                        kb.mm(p[:, 0:256], wbl[:, dr * D + k * 128:dr * D + (k + 1) * 128], w2[:, dr * 256:(dr + 1) * 256])
                        c0 = EV_COLS["zf"] + dr * 256
                        kb.copy(WH["Win"][:, k, c0:c0 + 256], p[:, 0:256], eng="act")
            load_weights(ev["ev_w_in"], 4128, ev["ev_w_out"], ev["ev_w_mod"], ev["ev_b_modT"], ev["ev_b_mod"], ev["ev_rows"], ev_extra)
            lbl = kb.sb([128, 16], F32, "lbl")
            kb.dma(lbl[:], ev["lblT"][:, :])
            for dr in range(2):
                dlt = tmp("dlt", (128, 4))
                kb.tt(dlt[:], lbl[:, dr * 8:dr * 8 + 4], lbl[:, dr * 8 + 4:dr * 8 + 8], ALU.subtract)
                kb.sig(lbT[:, dr * 8:dr * 8 + 4], dlt[:])
                kb.ts(lbT[:, dr * 8 + 4:dr * 8 + 8], lbT[:, dr * 8:dr * 8 + 4], -1.0, 1.0, ALU.mult, ALU.add)
            kb.dma(bgk[:], ev["gla_b_gkT"][:, :])
            EVT["nbgk"] = kb.sb([128, 4], F32, "nbgk")
            kb.ts(EVT["nbgk"][:], bgk[:], -1.0, None, ALU.mult)
            TMPN[0] = 4
            even_sweep(1)
            even_sweep(0)
        if 1 in layers:
          with layer_scope():
            WH["Win"] = kb.sb([128, 8, 4064], BF16, "Win")
            WH["rowsb"] = kb.sb([128, 2, D], F32, "rowsb")
            OD = dict(cq=0, ck=512, cv=640, cg=768, r=1280, k=1792, v=2304, wl_f=2816, wl_b=2848, al=2880, g=2912,
                      rq=3424, rk=3936)
            src = x1T if 0 in layers else xT
            c2 = kb.sb([128, 6 * 512], BF16, "c2")
            c2f = kb.sb([128, 256], F32, "c2f")
            kb.dma(c2f[:], od["consts2"][:, 3072:3328])
            m_su, m_iu, m_sl, eye8 = c2[:, 0:512], c2[:, 512:1024], c2[:, 1024:1536], c2[:, 1536:2048]
            m_ge4, m_le4 = c2[:, 2048:2560], c2[:, 2560:3072]
            oblk, cmk = c2f[:, 0:128], c2f[:, 128:256]
            mixT = kb.sb([128, 3, 19], F32, "mixT")
            kb.dma(mixT[:, 0:2, :], od["mixT"][:, :, :])
            vecT = kb.sb([128, 4, 8], F32, "vecT")
            kb.dma(vecT[:], od["vecT"][:, :, :])
            nw0 = kb.sb([128, 4, 3], F32, "nw0")
            sinkE = kb.sb([128, 8], F32, "sinkE")
            H = kb.sb([128, 4, 64], F32, "H")
            Hb = kb.sb([128, 4, 64], LDT, "Hb")
            identb = kb.sb([128, 128], LDT, "identb")
            oblkb_t = kb.sb([128, 128], LDT, "oblkb")
            oblkb = oblkb_t[:]
            aw2b = kb.sb([32, 512], LDT, "aw2b")
            ww2b = kb.sb([32, 2, 512], LDT, "ww2b")
            zrR = Rot([kb.sb([128, 19, 130], BF16, "zr") for _ in range(3)])
            kTR = Rot([kb.sb([128, 128], BF16, "kT") for _ in range(3)])
            v65R = Rot([kb.sb([128, 2, 65], BF16, "v65") for _ in range(3)])
            hTo = hTR
            xto = xtR
            ropR = Rot([kb.sb([128, 2, 128], F32, "rop") for _ in range(2)])
            obT_d = nc.dram_tensor("obT", [512, TT_], F32, kind="Internal").ap()
            obTT = [Tl(obT_d, "obT%d" % i) for i in range(NT)]
            rope_d = od["ropeT"]

            def od_extra():
                for q6 in range(3):
                    cs_ = kb.sb([128, 1024], F32, "c2s")
                    kb.dma(cs_[:], od["consts2"][:, q6 * 1024:(q6 + 1) * 1024])
                    kb.copy(c2[:, q6 * 1024:(q6 + 1) * 1024], cs_[:])
                kb.copy(identb[:], ident)
                kb.copy(oblkb_t[:], oblk)
                ww2 = kb.sb([32, 2, 512], F32, "ww2")
                kb.dma(ww2[:], od["w_w2"][:, :, :])
                aw2 = kb.sb([32, 512], F32, "aw2")
                kb.dma(aw2[:], od["a_w2"][:, :])
                kb.copy(ww2b[:], ww2[:])
                kb.copy(aw2b[:], aw2[:])
                srow = kb.sb([1, 8], F32, "srow")
                kb.dma(srow[:], od["sink"][:, :])
                p = PS.nxt()
                kb.mm(p[:, 0:8], ones[0:1, :], srow[0:1, :])
                kb.act(sinkE[:], p[:, 0:8], AF.Exp)
                kb.ts(nw0[:, :, 0:2], vecT[:, :, 6:8], -1.0, None, ALU.mult)
                kb.ts(nw0[:, :, 2:3], vecT[:, :, 0:1], -1.0, None, ALU.mult)
                kb.tt(mixT[:, 2, :], mixT[:, 0, :], mixT[:, 1, :], ALU.add)
                kb.ts(mixT[:, 2, :], mixT[:, 2, :], -1.0, 1.0, ALU.mult, ALU.add)
            load_weights(od["od_w_in"], 4064, od["od_w_out"], od["od_w_mod"], od["od_b_modT"], od["od_b_mod"], od["od_rows"], od_extra, ro=1)

            tr = lambda cb: slice(64 * cb, 64 * cb + 64)
            hc = lambda h: slice(64 * h, 64 * h + 64)
            pc = lambda p, cb: slice((p * 2 + cb) * 64, (p * 2 + cb + 1) * 64)
            T5 = lambda nm, dt=F32: tmp(nm, (128, 512), 1, dt=dt)
            LD = LDT
            PL = "pool" if use_pool else "dve"
            TMPN[0] = 1

            def projF(hT, col, m=128):
                p = PS.nxt()
                for k in range(8):
                    kb.mm(p[0:m, 0:128], WH["Win"][:, k, col:col + m], hT[:, k, :], start=(k == 0), stop=(k == 7))
                return p

            def stage1(n, flip, dirn, need_g, attn):
                seg = n // TPS
                xt = xto.nxt()
                kb.dma(xt[:], src[n][rows(n), :])
                pX = PD.nxt()
                for k in range(8):
                    kb.mm(pX[:, k * 128:(k + 1) * 128], xt[:, k * 128:(k + 1) * 128], Jm if flip else ident)
                hT = hTo.nxt()
                for k in range(8):
                    kb.act(hT[:, k, :], pX[:, k * 128:(k + 1) * 128], AF.Identity,
                           scale=modT[:, (8 + k) * NSEG + seg:(8 + k) * NSEG + seg + 1],
                           bias=modT[:, k * NSEG + seg:k * NSEG + seg + 1])
                zr = zrR.nxt()
                groups = [("r", 0), ("k", 4), ("v", 8)] + ([("g", 12)] if need_g else [])
                for gi, (nm, j0) in enumerate(groups):
                    for j in range(4):
                        p = projF(hT, OD[nm] + j * 128)
                        kb.copy(zr[:, j0 + j, 1:129], p[:, 0:128], eng="act" if (j % 2) else "dve")
                wj = 17 if dirn == 1 else 16
                p = projF(hT, OD["wl_b" if dirn == 1 else "wl_f"], 32)
                kb.copy(zr[0:32, wj, 1:129], p[0:32, 0:128], eng="act")
                p = projF(hT, OD["al"], 32)
                kb.copy(zr[0:32, 18, 1:129], p[0:32, 0:128], eng="dve")
                st = dict(n=n, seg=seg, xt=xt, hT=hT, zr=zr)
                if attn:
                    rop = ropR.nxt()
                    kb.dma(rop[:], rope_d[:, :, n * 128:(n + 1) * 128])
                    st["rop"] = rop
                    pk, pkr = projF(hT, OD["ck"]), projF(hT, OD["rk"])
                    t1, t2 = tmp("rt1"), tmp("rt2")
                    kb.tt(t1[:], pk[:, 0:128], rop[:, 0, :], ALU.mult)
                    kb.tt(t2[:], pkr[:, 0:128], rop[:, 1, :], ALU.mult)
                    kT = kTR.nxt()
                    kb.tt(kT[:], t1[:], t2[:], ALU.add)
                    pv = PS.nxt()
                    for k in range(8):
                        kb.mm(pv[:, 0:128], hT[:, k, :], WH["Win"][:, k, OD["cv"]:OD["cv"] + 128], start=(k == 0), stop=(k == 7))
                    v65 = v65R.nxt()
                    kb.memset(v65[:, :, 64:65], 1.0)
                    for g in range(2):
                        kb.copy(v65[:, g, 0:64], pv[:, g * 64:(g + 1) * 64], eng="act")
                    st["kT"], st["v65"] = kT, v65
                return st

            def halo(cur, prv, nxt, flip):
                zr = cur["zr"]
                for nb, dst, srccol in ((prv, 0, 128), (nxt, 129, 1)):
                    if nb is None:
                        kb.memset(zr[:, :, dst:dst + 1], 0.0)
                        continue
                    if nb["seg"] == cur["seg"]:
                        kb.copy(zr[:, :, dst:dst + 1], nb["zr"][:, :, srccol:srccol + 1])
                    else:
                        sg = max(nb["seg"], cur["seg"])
                        kb.ts(zr[:, :, dst:dst + 1], nb["zr"][:, :, srccol:srccol + 1], flg[:, sg:sg + 1], None, ALU.mult)

            def rwkv_tile(cur, flip, dirn, final):
                zr = cur["zr"]
                seg = cur["seg"]
                n = cur["n"]
                mp, mn = (1, 0) if flip else (0, 1)
                zm = tmp("zm", (128, 19, 128), 1, dt=BF16)
                wj = 17 if dirn == 1 else 16
                js = list(range(12)) + (list(range(12, 16)) if final else []) + [wj, 18]
                for j in js:
                    m_ = 32 if j >= 16 else 128
                    kb.act(zm[0:m_, j, :], zr[0:m_, j, 1:129], AF.Identity, scale=mixT[0:m_, 2, j:j + 1])
                    kb.stt(zm[0:m_, j, :], zr[0:m_, j, 0:128], mixT[0:m_, mp, j:j + 1], zm[0:m_, j, :], ALU.mult, ALU.add)
                    kb.stt(zm[0:m_, j, :], zr[0:m_, j, 2:130], mixT[0:m_, mn, j:j + 1], zm[0:m_, j, :], ALU.mult, ALU.add)
                th = tmp("th", (32, 128), 1, dt=LD)
                thf = tmp("thf", (32, 128), 1)
                kb.act(thf[:], zm[0:32, wj, :], AF.Exp, scale=-2.0)
                kb.ts(thf[:], thf[:], 1.0, None, ALU.add)
                kb.recip(thf[:], thf[:])
                kb.ts(th[:], thf[:], 2.0, -1.0, ALU.mult, ALU.add)
                F = {nm: T5("F" + nm, LD) for nm in ("rd", "kd", "bd", "kkd")}
                gam = tmp("gam", (128, 4, 2), 1)
                Tm = {nm: T5("T" + nm, LD) for nm in ("V", "KK", "KE", "NBE")}
                bon = T5("bon") if final else None
                tp = lambda nm, **kw: tmp(nm, n=2, **kw)

                def pair_gen(p):
                    r_, k_, vT = zm[:, p, :], zm[:, 4 + p, :], zm[:, 8 + p, :]
                    alv = zm[0:32, 18, :]
                    vc = lambda i: vecT[:, p, i:i + 1]
                    pa = PS.nxt()
                    kb.mm(pa[:, 0:128], aw2b[:, p * 128:(p + 1) * 128], alv)
                    a = tp("a")
                    kb.sig(a[:], pa[:, 0:128], nbias=nw0[:, p, 2:3])
                    yield
                    pw = PS.nxt()
                    kb.mm(pw[:, 0:128], ww2b[:, dirn, p * 128:(p + 1) * 128], th[:])
                    e1 = tp("e1")
                    kb.act(e1[:], pw[:, 0:128], AF.Exp, scale=-1.0, bias=nw0[:, p, dirn:dirn + 1])
                    yield
                    kb.act(e1[:], e1[:], AF.Ln, bias=1.0)
                    ew = tp("ew")
                    kb.act(ew[:], e1[:], AF.Exp, scale=-1.0, bias=-0.5)
                    c_ = tp("c_")
                    kb.scan(c_[:], cmk, ew[:], 0.0, ALU.mult, ALU.subtract)
                    yield
                    eCt, eN, eX, eE = tp("eC"), tp("eN"), tp("eX"), tp("eE")
                    eC = eCt[:]
                    kb.act(eC, c_[:], AF.Exp)
                    kb.copy(gam[:, p, 0:1], eCt[:, 63:64])
                    kb.copy(gam[:, p, 1:2], eCt[:, 127:128])
                    kb.act(eN[:], c_[:], AF.Exp, scale=-1.0)
                    cx = tp("cx")
                    kb.tt(cx[:], c_[:], ew[:], ALU.add)
                    yield
                    kb.act(eX[:], cx[:], AF.Exp)
                    for cb in range(2):
                        kb.act(eE[:, tr(cb)], c_[:, tr(cb)], AF.Exp, scale=-1.0, bias=c_[:, 64 * cb + 63:64 * cb + 64])
                    kk0 = tp("kk0")
                    kb.ts(kk0[:], k_, vc(1), None, ALU.mult)
                    sq = tp("sq", dt=LD)
                    kb.act(sq[:], kk0[:], AF.Square)
                    yield
                    pss = PS.nxt()
                    kb.mm(pss[:, 0:128], oblkb, sq[:])
                    nr = tp("nr")
                    kb.ts(nr[:], pss[:, 0:128], 1e-24, None, ALU.max)
                    yield
                    kb.act(nr[:], nr[:], AF.Ln)
                    kb.act(nr[:], nr[:], AF.Exp, scale=-0.5)
                    kk = tp("kk")
                    kb.tt(kk[:], kk0[:], nr[:], ALU.mult)
                    t1 = tp("t1")
                    kb.ts(t1[:], a[:], 1.0, vc(2), ALU.subtract, ALU.mult)
                    km = tp("km")
                    kb.stt(km[:], t1[:], 1.0, k_, ALU.add, ALU.mult)
                    yield
                    b_ = tp("b_")
                    kb.tt(b_[:], kk[:], a[:], ALU.mult, eng=PL)
                    P_ = slice(p * 128, (p + 1) * 128)
                    kb.tt(F["rd"][:, P_], r_, eC, ALU.mult, eng=PL)
                    kb.tt(F["kd"][:, P_], km[:], eN[:], ALU.mult)
                    kb.tt(F["kkd"][:, P_], kk[:], eX[:], ALU.mult)
                    yield
                    kb.tt(F["bd"][:, P_], b_[:], eN[:], ALU.mult, eng=PL)
                    kef = tp("kef", dt=LD)
                    kb.tt(kef[:], km[:], eE[:], ALU.mult, eng=PL)
                    nbf = tp("nbf", dt=LD)
                    kb.stt(nbf[:], b_[:], -1.0, eE[:], ALU.mult, ALU.mult)
                    yield
                    if final:
                        rk = tp("rk", dt=LD)
                        kb.stt(rk[:], r_, vc(3), km[:], ALU.mult, ALU.mult)
                        pb = PS.nxt()
                        kb.mm(pb[:, 0:128], oblkb, rk[:])
                        kb.tt(bon[:, P_], pb[:, 0:128], vT, ALU.mult)
                        yield
                    pt = PS.nxt()
                    for i_, srcv in enumerate((vT, F["kkd"][:, P_], kef[:], nbf[:])):
                        kb.mm(pt[:, i_ * 128:(i_ + 1) * 128], srcv, identb[:])
                    for i_, dst in enumerate(("V", "KK", "KE", "NBE")):
                        kb.copy(Tm[dst][:, P_], pt[:, i_ * 128:(i_ + 1) * 128], eng="act" if i_ % 2 else "dve")

                for pp in ((0, 1), (2, 3)):
                    gens = [pair_gen(p) for p in pp]
                    while gens:
                        for g_ in list(gens):
                            try:
                                next(g_)
                            except StopIteration:
                                gens.remove(g_)
                rd, kd, bd, kkd = F["rd"], F["kd"], F["bd"], F["kkd"]
                V, KK, KE, NBE = Tm["V"], Tm["KK"], Tm["KE"], Tm["NBE"]
                frs = lambda hp: slice(64 * hp, 64 * hp + 64)

                def newt(nm):
                    return tmp(nm, (128, 512), 5, dt=LD) if nm == "S5" else T5(nm, LD)

                def scores(lhs, rhs, msk, nm, neg=False):
                    ps = PS.nxt()
                    for hp in range(2):
                        for cb in range(2):
                            for p in range(4):
                                h = 2 * p + hp
                                kb.mm(ps[tr(cb), hc(h)], lhs[frs(hp), pc(p, cb)], rhs[frs(hp), pc(p, cb)])
                    o_ = newt(nm)
                    if neg:
                        kb.stt(o_[:], ps[:], -1.0, msk, ALU.mult, ALU.mult)
                    else:
                        kb.tt(o_[:], ps[:], msk, ALU.mult)
                    return o_

                def bprod(A, B, nm, add=None, eng="act"):
                    ps = PS.nxt()
                    for cb in range(2):
                        for h in range(8):
                            kb.mm(ps[tr(cb), hc(h)], A[tr(cb), hc(h)], B[tr(cb), hc(h)])
                    o_ = newt(nm)
                    if add is None:
                        kb.copy(o_[:], ps[:], eng=eng)
                    else:
                        kb.tt(o_[:], ps[:], add[:], ALU.add)
                    return o_

                def fprod(pairs):
                    ps = PS.nxt()
                    for cb in range(2):
                        for hp in range(2):
                            for p in range(4):
                                h = 2 * p + hp
                                for i, (A, B) in enumerate(pairs):
                                    kb.mm(ps[frs(hp), pc(p, cb)], A[tr(cb), hc(h)], B[tr(cb), hc(h)], start=(i == 0), stop=(i == len(pairs) - 1))
                    return ps

                LkT = scores(kd, kkd, m_su, "LkT")
                MkT = scores(kd, rd, m_iu, "MkT")
                Nn = scores(bd, kkd, m_su, "S5")
                nMbT = scores(bd, rd, m_iu, "nMbT", neg=True)
                Ll = scores(kkd, bd, m_sl, "S5")
                Y = tmp("S5", (128, 512), 5, dt=LD)
                kb.tt(Y[:], eye8, Nn[:], ALU.subtract)
                Lp, Np = Ll, Nn
                for lvl in range(5):
                    L2 = bprod(Np, Lp, "S5", eng="act")
                    if lvl < 4:
                        N2 = bprod(Lp, Np, "S5", eng="dve")
                    Y = bprod(L2, Y, "S5", add=Y)
                    Lp, Np = L2, (N2 if lvl < 4 else None)
                TT = Y
                Zs = bprod(LkT, V, "Fkkd")
                Ws = bprod(TT, KK, "Fkd", eng="dve")
                U0 = bprod(TT, Zs, "Fbd")
                pP = fprod([(Ws, NBE)])
                PTs = T5("PTs", LD)
                kb.copy(PTs[:], pP[:], eng="act")
                pQ = fprod([(KE, V), (NBE, U0)])
                Qs = T5("LkT", LD)
                kb.copy(Qs[:], pQ[:], eng="dve")
                pR = fprod([(Ws, nMbT)])
                RpT = T5("RpT", LD)
                kb.tt(RpT[:], pR[:], rd[:], ALU.add)
                pOd = PD.nxt()
                pO0, pO1 = pOd[:, 0:512], pOd[:, 512:1024]
                for cb in range(2):
                    for hp in range(2):
                        for p in range(4):
                            h = 2 * p + hp
                            kb.mm(pO0[frs(hp), pc(p, cb)], V[tr(cb), hc(h)], MkT[tr(cb), hc(h)], start=True, stop=False)
                            kb.mm(pO0[frs(hp), pc(p, cb)], U0[tr(cb), hc(h)], nMbT[tr(cb), hc(h)], start=False, stop=True)
                first = (n % TPS == (TPS - 1 if flip else 0))
                if first:
                    kb.ts(H[:], H[:], flg[:, (NSEG if flip else 0) + seg:(NSEG if flip else 0) + seg + 1], None, ALU.mult)
                    kb.copy(Hb[:], H[:], eng="act")
                for cb in range(2):
                    for hp in range(2):
                        for p in range(4):
                            kb.mm(pO1[frs(hp), pc(p, cb)], Hb[frs(hp), p, :], RpT[frs(hp), pc(p, cb)])
                    pH = PS.nxt()
                    for hp in range(2):
                        for p in range(4):
                            kb.mm(pH[frs(hp), p * 64:(p + 1) * 64], PTs[frs(hp), pc(p, cb)], Hb[frs(hp), p, :], start=True, stop=False)
                            kb.mm(pH[frs(hp), p * 64:(p + 1) * 64], identb[frs(hp), frs(hp)], Qs[frs(hp), pc(p, cb)], start=False, stop=True)
                    for p in range(4):
                        kb.stt(H[:, p, :], H[:, p, :], gam[:, p, cb:cb + 1], pH[:, p * 64:(p + 1) * 64], ALU.mult, ALU.add)
                    kb.copy(Hb[:], H[:], eng="act")
                return pOd, zm, bon

            def odd_bwd():
                kb.memset(H[:], 0.0)
                kb.memset(Hb[:], 0.0)
                order = list(range(NT - 1, -1, -1))
                sts = {}
                for i, n in enumerate(order + [None]):
                    if n is not None:
                        sts[n] = stage1(n, True, 1, False, False)
                    if i == 0:
                        continue
                    m = order[i - 1]
                    prv = sts.get(order[i - 2]) if i >= 2 else None
                    halo(sts[m], prv, sts.get(n) if n is not None else None, True)
                    pOd, _, _ = rwkv_tile(sts[m], True, 1, False)
                    o_ = T5("obl")
                    o2 = T5("o2f")
                    kb.copy(o2[:], pOd[:, 0:512], eng="act")
                    for p in range(4):
                        kb.tt(o_[:, p * 128:(p + 1) * 128][:, ::-1], pOd[:, 512 + p * 128:512 + (p + 1) * 128], o2[:, p * 128:(p + 1) * 128], ALU.add)
                        kb.dma(obTT[m][p * 128:(p + 1) * 128, m * 128:(m + 1) * 128], o_[:, p * 128:(p + 1) * 128])
                    if i >= 2:
                        del sts[order[i - 2]]

            def attn_tile(cur, prv, nxt):
                hT, rop, seg = cur["hT"], cur["rop"], cur["seg"]
                qr = T5("qrb", BF16)
                for r in range(4):
                    pq, pqr = projF(hT, OD["cq"] + r * 128), projF(hT, OD["rq"] + r * 128)
                    t1, t2 = tmp("rt1"), tmp("rt2")
                    kb.tt(t1[:], pq[:, 0:128], rop[:, 0, :], ALU.mult)
                    kb.tt(t2[:], pqr[:, 0:128], rop[:, 1, :], ALU.mult)
                    kb.tt(qr[:, r * 128:(r + 1) * 128], t1[:], t2[:], ALU.add)
                pOa = PD.nxt()
                for g in range(2):
                    gr = slice(64 * g, 64 * g + 64)
                    Pts = []
                    for nb, msk in ((prv, m_ge4), (cur, None), (nxt, m_le4)):
                        if nb is None:
                            continue
                        pS_ = PS.nxt()
                        kb.mm(pS_[:], nb["kT"][gr, :], qr[gr, :])
                        Pt = tmp("Pt", (128, 512), 3, dt=BF16)
                        kb.act(Pt[:], pS_[:], AF.Exp, scale=0.125)
                        if msk is not None:
                            if nb["seg"] == seg:
                                kb.tt(Pt[:], Pt[:], msk, ALU.mult)
                            else:
                                sg = max(nb["seg"], seg)
                                kb.tt(Pt[:], Pt[:], msk, ALU.mult)
                                kb.ts(Pt[:], Pt[:], flg[:, sg:sg + 1], None, ALU.mult)
                        Pts.append((Pt, nb["v65"]))
                    for r in range(4):
                        for i, (Pt, v65) in enumerate(Pts):
                            kb.mm(pOa[:, g * 512 + r * 65:g * 512 + (r + 1) * 65], Pt[:, r * 128:(r + 1) * 128], v65[:, g, :],
                                  start=(i == 0), stop=(i == len(Pts) - 1))
                den = tmp("den", (128, 8))
                for g in range(2):
                    kb.tt(den[:, 4 * g:4 * g + 4], pOa[:, slice(g * 512 + 64, g * 512 + 64 + 3 * 65 + 1, 65)], sinkE[:, 4 * g:4 * g + 4], ALU.add)
                kb.recip(den[:], den[:])
                co = T5("cof")
                for g in range(2):
                    pv4 = pOa[:, g * 512:g * 512 + 260]
                    src = Vw(pv4.tile, pv4.ap.rearrange("p (a b) -> p a b", a=4)[:, :, 0:64])
                    kb.tt(v3(co[:, g * 256:(g + 1) * 256], 4), src, bc(den[:, 4 * g:4 * g + 4], (128, 4, 64)), ALU.mult)
                pG = PS.nxt()
                for k in range(8):
                    kb.mm(pG[:], hT[:, k, :], WH["Win"][:, k, OD["cg"]:OD["cg"] + 512], start=(k == 0), stop=(k == 7))
                sg_ = T5("sgf")
                kb.sig(sg_[:], pG[:])
                kb.tt(co[:], co[:], sg_[:], ALU.mult, eng="pool")
                cob = T5("cob", BF16)
                kb.tt(cob[:], co[:], pG[:], ALU.mult)
                return cob

            def odd_fwd():
                kb.memset(H[:], 0.0)
                kb.memset(Hb[:], 0.0)
                order = list(range(NT))
                sts = {}
                for i, n in enumerate(order + [None]):
                    if n is not None:
                        sts[n] = stage1(n, False, 0, True, True)
                    if i == 0:
                        continue
                    m = order[i - 1]
                    cur = sts[m]
                    prv = sts.get(order[i - 2]) if i >= 2 else None
                    nxt = sts.get(n) if n is not None else None
                    halo(cur, prv, nxt, False)
                    pOd, zm, bon = rwkv_tile(cur, False, 0, True)
                    ob = T5("obl")
                    for p in range(4):
                        kb.dma(ob[:, p * 128:(p + 1) * 128], obTT[m][p * 128:(p + 1) * 128, m * 128:(m + 1) * 128])
                    od_ = ob
                    kb.tt(od_[:], pOd[:, 0:512], ob[:], ALU.add)
                    kb.tt(od_[:], pOd[:, 512:1024], od_[:], ALU.add)
                    onT = tmp("onT", (128, 8, 128), 1, dt=BF16)
                    P4 = lambda p: slice(p * 128, (p + 1) * 128)
                    pm_ = PS.nxt()
                    for p in range(4):
                        kb.mm(pm_[:, P4(p)], oblk, od_[:, P4(p)])
                    cen = T5("cen5")
                    kb.stt(cen[:], pm_[:], -1.0 / 64.0, od_[:], ALU.mult, ALU.add)
                    sq5 = T5("sq5")
                    kb.act(sq5[:], cen[:], AF.Square)
                    pv_ = PS.nxt()
                    for p in range(4):
                        kb.mm(pv_[:, P4(p)], oblk, sq5[:, P4(p)])
                    rs5 = T5("rs5")
                    kb.rsqrt(rs5[:], pv_[:], 1.0 / 64.0, 64e-5)
                    kb.tt(cen[:], cen[:], rs5[:], ALU.mult)
                    for p in range(4):
                        kb.ts(cen[:, P4(p)], cen[:, P4(p)], vecT[:, p, 4:5], vecT[:, p, 5:6], ALU.mult, ALU.add)
                    kb.tt(cen[:], cen[:], bon[:], ALU.add, eng=PL)
                    zg = zm[:, 12:16, :]
                    kb.sig(v3(sq5[:], 4), zg)
                    kb.tt(cen[:], cen[:], sq5[:], ALU.mult)
                    kb.tt(onT[:, 4:8, :], v3(cen[:], 4), zg, ALU.mult)
                    co = attn_tile(cur, prv, nxt)
                    pT = PS.nxt()
                    for r in range(4):
                        kb.mm(pT[:, r * 128:(r + 1) * 128], co[:, r * 128:(r + 1) * 128], identb[:])
                    kb.copy(onT[:, 0:4, :], pT[:], eng="act")
                    xr_ = xto.nxt()
                    kb.dma(xr_[:], src[m][rows(m), :])
                    finish2(m, cur["seg"], onT, xr_, yT)
                    if i >= 2:
                        del sts[order[i - 2]]

            odd_bwd()
            odd_fwd()
        if 1 not in layers:
            for n in range(NT):
                t_ = xtR.nxt()
                kb.dma(t_[:], x1T[n][rows(n), :])
                kb.dma(yT[n][rows(n), :], t_[:])
        kb.P.emit()
    return nc


def consts():
    c = np.zeros((128, 512), np.float32)
    c[:, 0:128] = np.eye(128)
    c[:, 128:256] = np.eye(128)[::-1]
    j = np.arange(128)[:, None]
    i = np.arange(128)[None, :]
    c[:, 256:384] = (j <= i)
    c[:, 384:512] = 1.0
    return c


def shared_inputs(w):
    f = lambda a: np.ascontiguousarray(a, dtype=np.float32)
    m = {}
    m["consts"] = consts()
    m["ev_w_mod"] = f(w["ev_w_mod"][0])
    m["ev_b_modT"] = f(w["ev_b_mod"][0].reshape(24, 128).T)
    m["ev_b_mod"] = f(w["ev_b_mod"][0].reshape(1, -1))
    m["ev_w_in"] = f(w["ev_w_in"][0])
    wi = w["ev_w_in"][0]
    m["ev_w_blT"] = f(np.concatenate([wi[:, 3584:3600].T, wi[:, 3600:3616].T], axis=1))
    m["gla_w_gk"] = f(np.concatenate([w["gla_w_gk"][0, 0], w["gla_w_gk"][0, 1]], axis=1))
    m["gla_b_gkT"] = f(w["gla_b_gk"][0].reshape(2, 2, 128).transpose(2, 0, 1).reshape(128, 4))
    m["ev_rows"] = f(np.concatenate([w["hgrn_norm"][0], w["gla_norm"][0], w["ev_ln_g"][0], w["ev_ln_b"][0]]).reshape(1, -1))
    m["ev_w_out"] = f(w["ev_w_out"][0])
    m["lblT"] = f(w["hgrn_lb_logits"].reshape(2, 2, 4, 128).transpose(3, 0, 1, 2).reshape(128, 16))
    return m


def consts2():
    c = np.zeros((128, 6 * 512 + 256), np.float32)
    r = (np.arange(128) % 64)[:, None]
    q = (np.arange(512) % 64)[None, :]
    c[:, 0:512] = r < q
    c[:, 512:1024] = r <= q
    c[:, 1024:1536] = r > q
    c[:, 1536:2048] = r == q
    j = np.arange(128)[:, None]
    i = (np.arange(512) % 128)[None, :]
    c[:, 2048:2560] = j >= i
    c[:, 2560:3072] = j <= i
    a = np.arange(128)
    c[:, 3072:3200] = (a[:, None] // 64) == (a[None, :] // 64)
    cm = np.ones((128, 128), np.float32)
    cm[:, 0] = 0.0
    cm[:, 64] = 0.0
    c[:, 3200:3328] = cm
    return c


def rope_table(pos):
    inv = (10000.0 ** (-np.arange(0, 64, 2, dtype=np.float32) / np.float32(64))).astype(np.float32)
    ang = (pos.astype(np.float32)[:, None] * inv[None, :]).astype(np.float32)
    cos, sin = np.cos(ang).astype(np.float32), np.sin(ang).astype(np.float32)
    cc = np.concatenate([cos, cos], axis=1).T
    ss = np.concatenate([-sin, sin], axis=1).T
    t = np.stack([np.concatenate([cc, cc], 0), np.concatenate([ss, ss], 0)], axis=1)
    return np.ascontiguousarray(t, dtype=np.float32)


def od_shared(w):
    f = lambda a: np.ascontiguousarray(a, dtype=np.float32)
    m = {}
    wi = w["od_w_in"][0]
    qcols = np.concatenate([np.concatenate([np.arange(r * 64, r * 64 + 64), np.arange((4 + r) * 64, (4 + r) * 64 + 64)])
                            for r in range(4)])
    swap = lambda cols: np.concatenate([np.concatenate([cols[i * 64 + 32:i * 64 + 64], cols[i * 64:i * 64 + 32]])
                                        for i in range(len(cols) // 64)])
    kcols = np.arange(512, 640)
    order = np.concatenate([qcols, np.arange(512, 3424), swap(qcols), swap(kcols)])
    m["od_w_in"] = f(wi[:, order])
    m["od_w_mod"] = f(w["od_w_mod"][0])
    m["od_b_modT"] = f(w["od_b_mod"][0].reshape(24, 128).T)
    m["od_b_mod"] = f(w["od_b_mod"][0].reshape(1, -1))
    m["od_rows"] = f(np.concatenate([np.zeros(D, np.float32), w["od_ln_g"][0], w["od_ln_b"][0]]).reshape(1, -1))
    m["od_w_out"] = f(w["od_w_out"][0])
    m["consts2"] = consts2()
    mix = w["rwkv_mix"][0]
    mt = np.zeros((128, 2, 19), np.float32)
    starts = [0, 128, 256, 384, 512, 640, 768, 896, 1024, 1152, 1280, 1408, 1632, 1760, 1888, 2016, 1536, 1568, 1600]
    for j, st in enumerate(starts):
        n_ = 32 if j >= 16 else 128
        mt[0:n_, :, j] = mix[:, st:st + n_].T
    m["mixT"] = mt
    vecs = [w["rwkv_a0"][0], w["rwkv_k_k"][0], w["rwkv_k_a"][0], w["rwkv_r_k"][0], w["rwkv_ln_w"][0], w["rwkv_ln_b"][0],
            w["rwkv_w0"][0, 0], w["rwkv_w0"][0, 1]]
    m["vecT"] = f(np.stack([v.reshape(4, 128) for v in vecs], axis=-1).transpose(1, 0, 2))
    m["w_w2"] = f(w["rwkv_w_w2"][0].transpose(1, 0, 2))
    m["a_w2"] = f(w["rwkv_a_w2"][0])
    m["sink"] = f(w["swa_sink"][0].reshape(1, 8))
    return m


_NC_CACHE = {}


def kernel(**inputs):
    NSEG, L = 4, 4096
    w = {k: np.asarray(v) for k, v in inputs.items()}
    xp, xs, cp, cs = w["x_prompt"], w["x_sample"], w["c_prompt"], w["c_sample"]
    shared = shared_inputs(w)
    shared.update(od_shared(w))
    plan = []
    for b in range(2):
        plan.append([("p", b, k) for k in range(4)])
    samp = [[0, 1, 2], [3, 4, 5], [6, 7, 8], [9, 10, 11], [12, 13], [14, 15]]
    for lst in samp:
        plan.append([("s", i, 0) for i in lst] + [("s", lst[0], 0)] * (4 - len(lst)))
    in_maps = []
    for core in range(8):
        m = dict(shared)
        xs_, cs_, pos = [], [], []
        fl = np.zeros((128, 2 * NSEG), np.float32)
        for s_, (kind, b, k) in enumerate(plan[core]):
            if kind == "p":
                xs_.append(xp[b, k * L:(k + 1) * L])
                cs_.append(cp[b])
                pos.append(np.arange(k * L, (k + 1) * L, dtype=np.float32))
                if k > 0:
                    fl[:, s_] = 1.0
                if k < 3:
                    fl[:, NSEG + s_] = 1.0
            else:
                xs_.append(xs[b])
                cs_.append(cs[b])
                pos.append(np.arange(L, dtype=np.float32))
        m["x"] = np.ascontiguousarray(np.concatenate(xs_, axis=0), dtype=np.float32)
        cc = np.stack(cs_, axis=0)
        m["cT"] = np.ascontiguousarray(cc.reshape(NSEG, 8, 128).transpose(2, 1, 0).reshape(128, 8 * NSEG), dtype=np.float32)
        m["flags"] = fl
        m["ropeT"] = rope_table(np.concatenate(pos))
        in_maps.append(m)
    if "nc" not in _NC_CACHE:
        _NC_CACHE["nc"] = build(NSEG, L)
    res = run_bass_kernel_spmd(_NC_CACHE["nc"], in_maps, core_ids=list(range(8)))
    yp = np.zeros_like(xp)
    ys = np.zeros_like(xs)
    for core in range(8):
        y = res.results[core]["y"].reshape(NSEG, L, D)
        seen = set()
        for s_, (kind, b, k) in enumerate(plan[core]):
            if kind == "p":
                yp[b, k * L:(k + 1) * L] = y[s_]
            elif b not in seen:
                ys[b] = y[s_]
                seen.add(b)
    return (yp, ys)
```

```python
import numpy as np
from contextlib import ExitStack
import concourse.bass as bass
import concourse.mybir as mybir
from concourse.bass_utils import run_bass_kernel_spmd
from concourse.alu_op_type import AluOpType as ALU

F32 = mybir.dt.float32
BF16 = mybir.dt.bfloat16
AF = mybir.ActivationFunctionType
D = 1024
ALPHA = 4 ** 0.25
NDMA = 24


class Tl:
    def __init__(self, t, name):
        self.t, self.name, self.w, self.r = t, name, {}, {}

    def __getitem__(self, idx):
        v = Vw(self, self.t[idx])
        i0 = idx[0] if isinstance(idx, tuple) else idx
        if isinstance(i0, slice) and i0.start is not None:
            v.p0, v.pn = i0.start, (i0.stop - i0.start)
        return v


class Vw:
    def __init__(self, tile, ap):
        self.tile, self.ap = tile, ap
        self.p0, self.pn = 0, 128

    def __getitem__(self, idx):
        v = Vw(self.tile, self.ap[idx])
        v.p0, v.pn = self.p0, self.pn
        i0 = idx[0] if isinstance(idx, tuple) else idx
        if isinstance(i0, slice) and i0.start is not None:
            v.p0, v.pn = self.p0 + i0.start, (i0.stop - i0.start)
        return v


class Op:
    __slots__ = ("eng", "fn", "deps", "sig", "sigval", "idx", "dma_k")

    def __init__(self, eng, fn, idx):
        self.eng, self.fn, self.idx = eng, fn, idx
        self.deps, self.sig, self.sigval, self.dma_k = set(), False, 0, -1


class Prog:
    ENGS = ("pe", "act", "dve", "pool")

    def __init__(self, nc):
        self.nc, self.ops, self.ndma = nc, [], 0

    def add(self, eng, fn, reads=(), writes=()):
        idx = len(self.ops)
        op = Op(eng, fn, idx)
        isdma = eng == "sp"
        key = ("dma", idx) if isdma else eng
        raw, oth = set(), set()
        for v in reads:
            if v is None or not isinstance(v, Vw):
                continue
            raw.update(v.tile.w.values())
            if getattr(v.tile, "psum", False):
                oth.update(i for k, i in v.tile.r.items() if k != key)
        for v in writes:
            oth.update(v.tile.w.values())
            oth.update(v.tile.r.values())
        for d in raw:
            p = self.ops[d]
            if p.eng == eng and eng == "pe":
                continue
            op.deps.add(d)
        for d in oth:
            p = self.ops[d]
            if p.eng == eng and not isdma:
                continue
            op.deps.add(d)
        for v in reads:
            if v is None or not isinstance(v, Vw):
                continue
            v.tile.r[key] = idx
        for v in writes:
            tl = v.tile
            tl.r = {}
            if isdma:
                tl.w = {k: i for k, i in tl.w.items() if not isinstance(k, tuple)}
            tl.w[key] = idx
        if isdma:
            op.dma_k = self.ndma
            self.ndma += 1
        self.ops.append(op)
        return op

    def setup(self, es):
        nc = self.nc
        self.sems = {e: es.enter_context(nc.semaphore("s_" + e)) for e in self.ENGS}
        self.dsem = [es.enter_context(nc.semaphore("d%d" % i)) for i in range(NDMA)]
        self.cnt = {e: 0 for e in self.ENGS}
        self.waited = {e: {} for e in self.ENGS + ("sp",)}
        self.done = 0

    def emit(self):
        nc = self.nc
        ops, sems, dsem = self.ops, self.sems, self.dsem
        lo = self.done
        for op in ops[lo:]:
            op.deps = {d for d in op.deps if d >= lo}
            for d in op.deps:
                ops[d].sig = True
        for op in ops[lo:]:
            if op.eng != "sp" and op.sig:
                self.cnt[op.eng] += 1
                op.sigval = self.cnt[op.eng]

        def need(op):
            req = {}
            for d in op.deps:
                p = ops[d]
                if p.eng == "sp":
                    s, v = dsem[p.dma_k % NDMA], 16 * (p.dma_k // NDMA + 1)
                else:
                    s, v = sems[p.eng], p.sigval
                k = id(s)
                if k not in req or req[k][1] < v:
                    req[k] = (s, v)
            return req

        def run(engname, e):
            waited = self.waited[engname]
            for op in ops[lo:]:
                if op.eng != engname:
                    continue
                req = need(op)
                if engname == "sp" and op.dma_k >= NDMA:
                    s = dsem[op.dma_k % NDMA]
                    v = 16 * (op.dma_k // NDMA)
                    if id(s) not in req or req[id(s)][1] < v:
                        req[id(s)] = (s, v)
                for k, (s, v) in req.items():
                    if waited.get(k, 0) < v:
                        e.wait_ge(s, v)
                        waited[k] = v
                ins = op.fn(e)
                if engname == "sp":
                    ins.then_inc(dsem[op.dma_k % NDMA], 16)
                elif op.sig:
                    ins.then_inc(sems[engname], 1)
            if engname == "sp":
                for i in range(min(NDMA, self.ndma)):
                    last = ((self.ndma - 1 - i) // NDMA) * NDMA + i
                    v = 16 * (last // NDMA + 1)
                    if waited.get(id(dsem[i]), 0) < v:
                        e.wait_ge(dsem[i], v)
                        waited[id(dsem[i])] = v

        with nc.Block() as block:
            @block.sync
            def _(e):
                run("sp", e)

            @block.tensor
            def _(e):
                run("pe", e)

            @block.scalar
            def _(e):
                run("act", e)

            @block.vector
            def _(e):
                run("dve", e)

            @block.gpsimd
            def _(e):
                run("pool", e)
        self.done = len(ops)


def v3(vw, a):
    return Vw(vw.tile, vw.ap.rearrange("p (a b) -> p a b", a=a))


def bc(vw, shape):
    return Vw(vw.tile, vw.ap[:, :, None].broadcast_to(list(shape)))


def _ap(v):
    return v.ap if isinstance(v, Vw) else v


class K:
    def __init__(self, nc, es):
        self.nc, self.es, self.P = nc, es, Prog(nc)
        self.n = 0

    def sb(self, shape, dt=F32, name=None):
        self.n += 1
        nm = "%s%d" % (name or "t", self.n)
        return Tl(self.es.enter_context(self.nc.sbuf_tensor(nm, list(shape), dt)), nm)

    def ps(self, shape, name=None):
        self.n += 1
        nm = "%s%d" % (name or "p", self.n)
        t = Tl(self.es.enter_context(self.nc.psum_tensor(nm, list(shape), F32)), nm)
        t.psum = True
        return t

    def mm(self, out, lhsT, rhs, start=True, stop=True):
        rg = (lhsT.p0, lhsT.pn, out.p0, out.pn)
        last = getattr(self, "last_rg", (0, 128, 0, 128))
        tiled = lambda g: g[1] < 128 or g[3] < 128
        if rg != last and (tiled(rg) or tiled(last)) and getattr(self, "dummy", None) is not None:
            dps, dl, dr_ = self.dummy
            self.P.add("pe", lambda e: e.matmul(dps.ap, dl.ap, dr_.ap, start=True, stop=True), [dl], [dps])
        self.last_rg = rg
        self.P.add("pe", lambda e: e.matmul(out.ap, lhsT.ap, rhs.ap, start=start, stop=stop),
                   [lhsT, rhs], [out])

    def act(self, out, in_, func, scale=1.0, bias=0.0, accum=None, eng="act"):
        kw = {}
        if accum is not None:
            kw["accum_out"] = accum.ap
        self.P.add(eng, lambda e: e.activation(out.ap, in_.ap, func, bias=_ap(bias), scale=_ap(scale), **kw),
                   [in_, scale, bias], [out] + ([accum] if accum is not None else []))

    def tt(self, out, a, b, op, eng="dve"):
        self.P.add(eng, lambda e: e.tensor_tensor(out.ap, a.ap, b.ap, op), [a, b], [out])

    def ts(self, out, a, s1, s2, op0, op1=None, eng="dve"):
        if op1 is None:
            self.P.add(eng, lambda e: e.tensor_scalar(out.ap, a.ap, _ap(s1), None, op0), [a, s1], [out])
        else:
            self.P.add(eng, lambda e: e.tensor_scalar(out.ap, a.ap, _ap(s1), _ap(s2), op0, op1), [a, s1, s2], [out])

    def stt(self, out, a, s, b, op0, op1):
        self.P.add("dve", lambda e: e.scalar_tensor_tensor(out.ap, a.ap, _ap(s), b.ap, op0, op1), [a, s, b], [out])

    def scan(self, out, d0, d1, init, op0, op1):
        self.P.add("dve", lambda e: e.tensor_tensor_scan(out.ap, d0.ap, d1.ap, init, op0, op1), [d0, d1], [out])

    def copy(self, out, in_, eng="dve"):
        if eng == "act":
            self.P.add("act", lambda e: e.copy(out.ap, in_.ap), [in_], [out])
        else:
            self.P.add(eng, lambda e: e.tensor_copy(out.ap, in_.ap), [in_], [out])

    def memset(self, out, val, eng="dve"):
        self.P.add(eng, lambda e: e.memset(out.ap, val), [], [out])

    def sig(self, out, in_, nbias=0.0, e_out=None):
        e = e_out if e_out is not None else out
        self.act(e, in_, AF.Exp, scale=-1.0, bias=nbias)
        self.ts(out, e, 1.0, None, ALU.add)
        self.recip(out, out)

    def rsqrt(self, out, in_, scale, eps):
        self.act(out, in_, AF.Ln, scale=scale, bias=eps)
        self.act(out, out, AF.Exp, scale=-0.5)

    def recip(self, out, in_):
        self.P.add("dve", lambda e: e.reciprocal(out.ap, in_.ap), [in_], [out])

    def dma(self, out, in_):
        self.P.add("sp", lambda e: e.dma_start(out=out.ap, in_=in_.ap), [in_], [out])


class Rot:
    def __init__(self, items):
        self.items, self.i = items, 0

    def nxt(self):
        self.i += 1
        return self.items[(self.i - 1) % len(self.items)]


EV_COLS = dict(aq=0, ai=512, af_f=1024, af_b=1536, a_gate=2048, bq=2560, bk=2816, bv=3072, bl_f=3584, bl_b=3600,
               b_gate=3616, zf=4128, zb=4384)
WC = 4640


def build(NSEG, L, layers=(0, 1), LDT=BF16, use_pool=True):
    nc = bass.Bass("TRN2", target_bir_lowering=False)
    es = ExitStack()
    kb = K(nc, es)
    TT_ = NSEG * L
    NT = TT_ // 128
    TPS = L // 128

    def din(name, shape):
        return Tl(nc.dram_tensor(name, list(shape), F32, kind="ExternalInput").ap(), name)

    x_d = nc.dram_tensor("x", [TT_, D], F32, kind="ExternalInput").ap()
    y_d = nc.dram_tensor("y", [TT_, D], F32, kind="ExternalOutput").ap()
    x1_d = nc.dram_tensor("x1s", [TT_, D], F32, kind="Internal").ap()
    ob_d = nc.dram_tensor("obs", [TT_, D], F32, kind="Internal").ap()
    xT = [Tl(x_d, "x%d" % i) for i in range(NT)]
    yT = [Tl(y_d, "y%d" % i) for i in range(NT)]
    x1T = [Tl(x1_d, "x1%d" % i) for i in range(NT)]
    obT = [Tl(ob_d, "ob%d" % i) for i in range(NT)]
    rows = lambda n: slice(n * 128, (n + 1) * 128)

    cT_d = din("cT", [128, 8 * NSEG])
    flg_d = din("flags", [128, 2 * NSEG])
    cst_d = din("consts", [128, 4 * 128])
    ev = {}
    for nm, shp in [("ev_w_mod", [D, 3 * D]), ("ev_b_modT", [128, 24]), ("ev_b_mod", [1, 3 * D]), ("ev_w_in", [D, 4128]),
                    ("ev_w_blT", [16, 2 * D]), ("gla_w_gk", [16, 512]), ("gla_b_gkT", [128, 4]), ("lblT", [128, 16]),
                    ("ev_rows", [1, 3 * D]), ("ev_w_out", [D, D])]:
        ev[nm] = din(nm, shp)

    od = {}
    for nm, shp in [("od_w_mod", [D, 3 * D]), ("od_b_modT", [128, 24]), ("od_b_mod", [1, 3 * D]), ("od_w_in", [D, 4064]),
                    ("od_rows", [1, 3 * D]), ("od_w_out", [D, D]), ("consts2", [128, 6 * 512 + 256]), ("mixT", [128, 2, 19]),
                    ("vecT", [128, 4, 8]), ("w_w2", [32, 2, 512]), ("a_w2", [32, 512]), ("sink", [1, 8]),
                    ("ropeT", [128, 2, TT_])]:
        od[nm] = din(nm, shp)

    with es:
        kb.P.setup(es)
        cst = kb.sb([128, 512], F32, "cst")
        kb.dma(cst[:], cst_d[:, :])
        ident, Jm, mask, ones = cst[:, 0:128], cst[:, 128:256], cst[:, 256:384], cst[:, 384:512]
        flg = kb.sb([128, 2 * NSEG], F32, "flg")
        kb.dma(flg[:], flg_d[:, :])
        cT = kb.sb([128, 8 * NSEG], F32, "cT")
        kb.dma(cT[:], cT_d[:, :])
        scT = kb.sb([128, 8 * NSEG], F32, "scT")
        kb.sig(scT[:], cT[:])
        kb.tt(scT[:], scT[:], cT[:], ALU.mult)
        zer = kb.sb([128, 128], F32, "zer")
        kb.memset(zer[:], 0.0)

        PD = Rot([kb.ps([128, 1024], "pd") for _ in range(2)])
        PS = Rot([kb.ps([128, 512], "ps") for _ in range(3)])
        psd = kb.ps([128, 512], "psd")
        identg = kb.sb([128, 128], BF16, "identg")
        kb.copy(identg[:], ident)
        kb.dummy = (psd[:, 0:2], identg[:], identg[:, 0:2])

        WH = {}
        Wout = kb.sb([128, 8, D], BF16, "Wout")
        modT = kb.sb([128, 24 * NSEG], F32, "modT")
        bmT = kb.sb([128, 24], F32, "bmT")
        g1b = kb.sb([128, NSEG, D], BF16, "g1b")

        tmpR = {}

        TMPN = [2]

        def tmp(nm, shape=(128, 128), n=None, dt=F32):
            n = n or TMPN[0]
            if nm not in tmpR:
                tmpR[nm] = Rot([kb.sb(shape, dt, nm) for _ in range(n)])
            return tmpR[nm].nxt()

        def load_weights(w_in_d, ncols, w_out_d, w_mod_d, b_modT_d, b_mod_d, rows_d, extra=None, ro=0):
            WH["ro"] = ro
            ses = ExitStack()
            old_es, old_tmp = kb.es, dict(tmpR)
            kb.es = ses
            stg = Rot([kb.sb([128, 2048], F32, "stg") for _ in range(2)])
            brow = kb.sb([1, 3 * D], F32, "brow")
            rrow = kb.sb([1, 3 * D], F32, "rrow")
            for k in range(8):
                for c0 in range(0, ncols, 2048):
                    c1 = min(ncols, c0 + 2048)
                    s = stg.nxt()
                    kb.dma(s[:, 0:c1 - c0], w_in_d[k * 128:(k + 1) * 128, c0:c1])
                    kb.copy(WH["Win"][:, k, c0:c1], s[:, 0:c1 - c0], eng="pool" if (k % 2) else "dve")
                s = stg.nxt()
                kb.dma(s[:, 0:D], w_out_d[k * 128:(k + 1) * 128, :])
                kb.copy(Wout[:, k, :], s[:, 0:D], eng="act")
            kb.dma(bmT[:], b_modT_d[:, :])
            kb.dma(brow[:], b_mod_d[:, :])
            kb.dma(rrow[:], rows_d[:, :])
            pm = PD.nxt()
            for jb in range(12):
                wm = stg.nxt()
                for k in range(8):
                    kb.dma(wm[:, k * 256:(k + 1) * 256], w_mod_d[k * 128:(k + 1) * 128, jb * 256:(jb + 1) * 256])
                for jj in range(2):
                    j = jb * 2 + jj
                    for k in range(8):
                        kb.mm(pm[:, j * NSEG:(j + 1) * NSEG], wm[:, k * 256 + jj * 128:k * 256 + (jj + 1) * 128],
                              scT[:, k * NSEG:(k + 1) * NSEG], start=(k == 0), stop=(k == 7))
            for j in range(24):
                kb.act(modT[:, j * NSEG:(j + 1) * NSEG], pm[:, j * NSEG:(j + 1) * NSEG], AF.Identity,
                       bias=bmT[:, j:j + 1], scale=1.0)
            kb.ts(modT[:, 8 * NSEG:16 * NSEG], modT[:, 8 * NSEG:16 * NSEG], 1.0, None, ALU.add)
            for s_ in range(NSEG):
                pg = PD.nxt()
                for q4 in range(4):
                    wm = stg.nxt()
                    for k in range(8):
                        kb.dma(wm[:, k * 256:(k + 1) * 256], w_mod_d[k * 128:(k + 1) * 128, 2 * D + q4 * 256:2 * D + (q4 + 1) * 256])
                    for k in range(8):
                        scB = tmp("scB")
                        kb.act(scB[:], zer[:], AF.Identity, bias=scT[:, k * NSEG + s_:k * NSEG + s_ + 1], scale=1.0)
                        kb.mm(pg[:, q4 * 256:(q4 + 1) * 256], scB[:], wm[:, k * 256:(k + 1) * 256], start=(k == 0), stop=False)
                    kb.mm(pg[:, q4 * 256:(q4 + 1) * 256], ones[0:1, :], brow[0:1, 2 * D + q4 * 256:2 * D + (q4 + 1) * 256],
                          start=False, stop=True)
                kb.ts(g1b[:, s_, :], pg[:], 1.0, None, ALU.add)
            for r in range(ro, 3):
                pg = PD.nxt()
                for hf in range(2):
                    kb.mm(pg[:, hf * 512:(hf + 1) * 512], ones[0:1, :], rrow[0:1, r * D + hf * 512:r * D + (hf + 1) * 512])
                kb.copy(WH["rowsb"][:, r - ro, :], pg[:], eng="act")
            if extra is not None:
                extra()
            kb.P.emit()
            ses.close()
            kb.es = old_es
            tmpR.clear()
            tmpR.update(old_tmp)

        NB = 2
        xtR = Rot([kb.sb([128, D], F32, "xt") for _ in range(2)])
        hTR = Rot([kb.sb([128, 8, 128], BF16, "hT") for _ in range(NB)])
        from contextlib import contextmanager

        @contextmanager
        def layer_scope():
            ses = ExitStack()
            old_es, old_tmp = kb.es, dict(tmpR)
            kb.es = ses
            try:
                yield
            finally:
                kb.P.emit()
                ses.close()
                kb.es = old_es
                tmpR.clear()
                tmpR.update(old_tmp)

        EVT = {}

        def gla_unit(hT, dr, flip, psO, ocol, V, vcol, qsrc, ksrc, zsrc, S, sidx, heads, kind):
            lbT, bgk = EVT["lbT"], EVT["bgk"]
            def proj(col):
                p = PS.nxt()
                for k in range(8):
                    kb.mm(p[:, 0:128], WH["Win"][:, k, col:col + 128], hT[:, k, :], start=(k == 0), stop=(k == 7))
                return p
            pq = proj(qsrc)
            q = tmp("q")
            sn = tmp("sn")
            lf = tmp("lf")
            if kind == "A":
                h = sidx
                kb.sig(q[:], pq[:, 0:128])
                kb.tt(q[:], q[:], pq[:, 0:128], ALU.mult)
                yield
                pf = proj(zsrc)
                sg = tmp("sg")
                ee = tmp("ee")
                kb.sig(sg[:], pf[:, 0:128], e_out=ee[:])
                yield
                kb.act(lf[:], sg[:], AF.Ln, scale=lbT[:, dr * 8 + 4 + h:dr * 8 + 5 + h], bias=lbT[:, dr * 8 + h:dr * 8 + h + 1])
                kb.stt(sn[:], ee[:], lbT[:, dr * 8 + 4 + h:dr * 8 + 5 + h], sg[:], ALU.mult, ALU.mult)
                esc = 1.0
                sop = ALU.add
            else:
                p_ = sidx
                kb.act(q[:], pq[:, 0:128], AF.Identity, scale=0.125)
                yield
                pk = proj(ksrc)
                kb.copy(sn[:], pk[:, 0:128], eng="act")
                yield
                pf = proj(zsrc)
                sg = tmp("sg")
                kb.act(sg[:], pf[:, 0:128], AF.Exp, scale=-1.0, bias=EVT["nbgk"][:, dr * 2 + p_:dr * 2 + p_ + 1])
                yield
                kb.act(lf[:], sg[:], AF.Ln, bias=1.0)
                esc = 1.0 / 16.0
                sop = ALU.subtract
            yield
            B = tmp("B")
            kb.scan(B[:], ones, lf[:], 0.0, ALU.mult, sop)
            nb = tmp("nb", (128, 4))
            kb.ts(nb[:, 0:1], B[:, 63:64], -esc, None, ALU.mult)
            kb.ts(nb[:, 1:2], B[:, 63:64], esc, None, ALU.mult)
            kb.ts(nb[:, 2:3], B[:, 127:128], esc, None, ALU.mult)
            yield
            E1, E1n, E2, E3 = tmp("E1"), tmp("E1n"), tmp("E2"), tmp("E3")
            kb.act(E1[:], B[:], AF.Exp, scale=esc, bias=nb[:, 0:1])
            kb.act(E1n[:], B[:], AF.Exp, scale=-esc, bias=nb[:, 1:2])
            yield
            kb.act(E2[:], B[:], AF.Exp, scale=esc)
            kb.act(E3[:], B[:], AF.Exp, scale=-esc, bias=nb[:, 2:3])
            qd, qi, kd, ke = tmp("qd", dt=BF16), tmp("qi", dt=BF16), tmp("kd", dt=BF16), tmp("ke", dt=BF16)
            kb.tt(qd[:], q[:], E1[:], ALU.mult, eng="pool")
            kb.tt(kd[:], sn[:], E1n[:], ALU.mult, eng="pool")
            yield
            qi_o = qi[:, ::-1] if flip else qi[:]
            kb.tt(qi_o, q[:], E2[:], ALU.mult)
            kb.tt(ke[:], sn[:], E3[:], ALU.mult)
            nh = len(heads)
            dk = 128 // nh
            yield
            pT = PS.nxt()
            kb.mm(pT[:, 0:128], ke[:], identg[:])
            keT = tmp("keT", dt=BF16)
            kb.copy(keT[:], pT[:, 0:128], eng="act")
            for hi, hd in enumerate(heads):
                yield
                pr = slice(hi * dk, (hi + 1) * dk)
                pA = PS.nxt()
                kb.mm(pA[:, 0:128], kd[pr, :], qd[pr, :])
                A = tmp("A", dt=BF16)
                A_o = A[:, ::-1] if flip else A[:]
                kb.tt(A_o, pA[:, 0:128], mask, ALU.mult)
                oc = slice(ocol + hd * 128, ocol + (hd + 1) * 128)
                vc = slice(vcol + hd * 128, vcol + (hd + 1) * 128)
                Sh = S[pr, hd, :]
                Shb = EVT["Sb"][id(S)][pr, hd, :]
                kb.mm(psO[:, oc], A[:], V[:, vc], start=True, stop=False)
                kb.mm(psO[:, oc], qi[pr, :], Shb, start=False, stop=True)
                yield
                pS_ = PS.nxt()
                kb.mm(pS_[pr, 0:128], keT[:, pr], V[:, vc])
                kb.stt(Sh, Sh, E2[pr, 127:128], pS_[pr, 0:128], ALU.mult, ALU.add)
                kb.copy(Shb, Sh, eng="act")

        def even_sweep(dr):
            S_A, S_B, oacc, VR = EVT["S_A"], EVT["S_B"], EVT["oacc"], EVT["VR"]
            flip = dr == 1
            order = range(NT - 1, -1, -1) if flip else range(NT)
            xsrc = xT
            kb.memset(S_A[:], 0.0)
            kb.memset(S_B[:], 0.0)
            for sb_ in EVT["Sb"].values():
                kb.memset(sb_[:], 0.0)
            PDs = PD.items
            order = list(order)

            def stageA(n):
                seg = n // TPS
                xt = xtR.nxt()
                kb.dma(xt[:], xsrc[n][rows(n), :])
                pX = PDs[1]
                for k in range(8):
                    kb.mm(pX[:, k * 128:(k + 1) * 128], xt[:, k * 128:(k + 1) * 128], Jm if flip else ident)
                hT = hTR.nxt()
                for k in range(8):
                    kb.act(hT[:, k, :], pX[:, k * 128:(k + 1) * 128], AF.Identity,
                           scale=modT[:, (8 + k) * NSEG + seg:(8 + k) * NSEG + seg + 1],
                           bias=modT[:, k * NSEG + seg:k * NSEG + seg + 1])
                V = VR.nxt()
                pV = PDs[1]
                for hf, col in enumerate((EV_COLS["ai"], EV_COLS["bv"])):
                    for k in range(8):
                        kb.mm(pV[:, hf * 512:(hf + 1) * 512], hT[:, k, :], WH["Win"][:, k, col:col + 512],
                              start=(k == 0), stop=(k == 7))
                kb.copy(V[:], pV[:], eng="act")
                return dict(xt=xt, hT=hT, V=V, seg=seg)

            nxtA = stageA(order[0])
            for i_n, n in enumerate(order):
                cur = nxtA
                seg, xt, hT, V = cur["seg"], cur["xt"], cur["hT"], cur["V"]
                first = (n % TPS == (TPS - 1 if flip else 0))
                if first:
                    fc = flg[:, (NSEG if flip else 0) + seg:(NSEG if flip else 0) + seg + 1]
                    kb.ts(S_A[:], S_A[:], fc, None, ALU.mult)
                    kb.ts(S_B[:], S_B[:], fc, None, ALU.mult)
                    kb.copy(EVT["Sb"][id(S_A)][:], S_A[:], eng="act")
                    kb.copy(EVT["Sb"][id(S_B)][:], S_B[:], eng="act")
                psO = PDs[0]
                def mk(u):
                    if u < 4:
                        return gla_unit(hT, dr, flip, psO, 0, V, 0, EV_COLS["aq"] + u * 128, None,
                                        EV_COLS["af_b" if flip else "af_f"] + u * 128, S_A, u, [u], "A")
                    p_ = u - 4
                    return gla_unit(hT, dr, flip, psO, 512, V, 512, EV_COLS["bq"] + p_ * 128, EV_COLS["bk"] + p_ * 128,
                                    EV_COLS["zb" if flip else "zf"] + p_ * 128, S_B, p_, [2 * p_, 2 * p_ + 1], "B")
                for pi, pair in enumerate(((0, 1, 2, 3), (4, 5))):
                    gens = [mk(u) for u in pair]
                    while gens:
                        for g_ in list(gens):
                            try:
                                next(g_)
                            except StopIteration:
                                gens.remove(g_)
                    if pi == 0 and i_n + 1 < len(order):
                        nxtA = stageA(order[i_n + 1])
                o = oacc.nxt()
                if flip:
                    kb.copy(o[:], psO[:], eng="act")
                    kb.dma(obT[n][rows(n), :], o[:])
                    continue
                ob = tmp("ob", (128, D), 1)
                kb.dma(ob[:], obT[n][rows(n), :])
                kb.tt(o[:], psO[:], ob[:], ALU.add)
                ss = tmp("ss", (128, 8))
                kb.act(ob[:], o[:], AF.Square)
                ob3 = v3(ob[:], 8)
                kb.P.add("dve", lambda e, ss=ss, ob3=ob3: e.tensor_reduce(ss.t[:, 0:8], ob3.ap, mybir.AxisListType.X, ALU.add),
                         [ob3], [ss[:]])
                rs = tmp("rs", (128, 8))
                kb.rsqrt(rs[:], ss[:], 1.0 / 128.0, 1e-6)
                pG = PDs[1]
                for hf, col in enumerate((EV_COLS["a_gate"], EV_COLS["b_gate"])):
                    for k in range(8):
                        kb.mm(pG[:, hf * 512:(hf + 1) * 512], hT[:, k, :], WH["Win"][:, k, col:col + 512],
                              start=(k == 0), stop=(k == 7))
                sgt = tmp("sgt", (128, D), 1)
                kb.sig(sgt[:], pG[:])
                on = o
                kb.tt(v3(on[:], 8), v3(o[:], 8), bc(rs[:, 0:8], (128, 8, 128)), ALU.mult)
                kb.tt(on[:], on[:], WH["rowsb"][:, 0, :], ALU.mult)
                kb.tt(on[:], on[:], sgt[:], ALU.mult, eng="pool")
                kb.tt(on[:], on[:], pG[:], ALU.mult)
                finish(n, seg, on, xt, x1T, PDs)

        def finish(n, seg, on, xt, dstT, PDs):
            pT = PDs[0]
            for k in range(8):
                kb.mm(pT[:, k * 128:(k + 1) * 128], on[:, k * 128:(k + 1) * 128], ident)
            onT = tmp("onT", (128, 8, 128), 1, dt=BF16)
            kb.copy(onT[:, 0:4, :], pT[:, 0:512], eng="act")
            kb.copy(onT[:, 4:8, :], pT[:, 512:1024], eng="dve")
            finish2(n, seg, onT, xt, dstT, PDs[1])

        def finish2(n, seg, onT, xt, dstT, pY=None):
            if pY is None:
                pY = PD.nxt()
            for hf in range(2):
                for k in range(8):
                    kb.mm(pY[:, hf * 512:(hf + 1) * 512], onT[:, k, :], Wout[:, k, hf * 512:(hf + 1) * 512],
                          start=(k == 0), stop=(k == 7))
            kb.tt(pY[:], pY[:], g1b[:, seg, :], ALU.mult)
            r = xt
            kb.stt(r[:], xt[:], ALPHA, pY[:], ALU.mult, ALU.add)
            st = tmp("st", (128, 12))
            kb.P.add("dve", lambda e: e.bn_stats(st.t[:, 0:6], r.t[:, 0:512]), [r[:]], [st[:]])
            kb.P.add("dve", lambda e: e.bn_stats(st.t[:, 6:12], r.t[:, 512:1024]), [r[:]], [st[:]])
            mv = tmp("mv", (128, 4))
            kb.P.add("dve", lambda e: e.bn_aggr(mv.t[:, 0:2], st.t[:, 0:12]), [st[:]], [mv[:]])
            kb.rsqrt(mv[:, 3:4], mv[:, 1:2], 1.0, 1e-5)
            yo = r
            ro = WH.get("ro", 0)
            kb.ts(yo[:], r[:], mv[:, 0:1], mv[:, 3:4], ALU.subtract, ALU.mult)
            kb.tt(yo[:], yo[:], WH["rowsb"][:, 1 - ro, :], ALU.mult, eng="pool")
            kb.tt(yo[:], yo[:], WH["rowsb"][:, 2 - ro, :], ALU.add)
            kb.dma(dstT[n][rows(n), :], yo[:])

        if 0 in layers:
          with layer_scope():
            WH["Win"] = kb.sb([128, 8, WC], BF16, "Win")
            WH["rowsb"] = kb.sb([128, 3, D], F32, "rowsb")
            S_A = EVT["S_A"] = kb.sb([128, 4, 128], F32, "S_A")
            S_B = EVT["S_B"] = kb.sb([128, 4, 128], F32, "S_B")
            lbT = EVT["lbT"] = kb.sb([128, 16], F32, "lbT")
            bgk = EVT["bgk"] = kb.sb([128, 4], F32, "bgk")
            EVT["oacc"] = Rot([kb.sb([128, D], F32, "oacc") for _ in range(1)])
            EVT["VR"] = Rot([kb.sb([128, D], BF16, "V") for _ in range(2)])
            S_Ab = kb.sb([128, 4, 128], BF16, "S_Ab")
            S_Bb = kb.sb([128, 4, 128], BF16, "S_Bb")
            EVT["Sb"] = {id(S_A): S_Ab, id(S_B): S_Bb}
            def ev_extra():
                wbl = kb.sb([16, 2 * D], F32, "wbl")
                w2 = kb.sb([16, 512], F32, "w2")
                kb.dma(wbl[:], ev["ev_w_blT"][:, :])
                kb.dma(w2[:], ev["gla_w_gk"][:, :])
                for dr in range(2):
                    for k in range(8):
                        p = PS.nxt()
                        kb.mm(p[:, 0:256], wbl[:, dr * D + k * 128:dr * D + (k + 1) * 128], w2[:, dr * 256:(dr + 1) * 256])
                        c0 = EV_COLS["zf"] + dr * 256
                        kb.copy(WH["Win"][:, k, c0:c0 + 256], p[:, 0:256], eng="act")
            load_weights(ev["ev_w_in"], 4128, ev["ev_w_out"], ev["ev_w_mod"], ev["ev_b_modT"], ev["ev_b_mod"], ev["ev_rows"], ev_extra)
            lbl = kb.sb([128, 16], F32, "lbl")
            kb.dma(lbl[:], ev["lblT"][:, :])
            for dr in range(2):
                dlt = tmp("dlt", (128, 4))
                kb.tt(dlt[:], lbl[:, dr * 8:dr * 8 + 4], lbl[:, dr * 8 + 4:dr * 8 + 8], ALU.subtract)
                kb.sig(lbT[:, dr * 8:dr * 8 + 4], dlt[:])
                kb.ts(lbT[:, dr * 8 + 4:dr * 8 + 8], lbT[:, dr * 8:dr * 8 + 4], -1.0, 1.0, ALU.mult, ALU.add)
            kb.dma(bgk[:], ev["gla_b_gkT"][:, :])
            EVT["nbgk"] = kb.sb([128, 4], F32, "nbgk")
            kb.ts(EVT["nbgk"][:], bgk[:], -1.0, None, ALU.mult)
            TMPN[0] = 4
            even_sweep(1)
            even_sweep(0)
        if 1 in layers:
          with layer_scope():
            WH["Win"] = kb.sb([128, 8, 4064], BF16, "Win")
            WH["rowsb"] = kb.sb([128, 2, D], F32, "rowsb")
            OD = dict(cq=0, ck=512, cv=640, cg=768, r=1280, k=1792, v=2304, wl_f=2816, wl_b=2848, al=2880, g=2912,
                      rq=3424, rk=3936)
            src = x1T if 0 in layers else xT
            c2 = kb.sb([128, 6 * 512], BF16, "c2")
            c2f = kb.sb([128, 256], F32, "c2f")
            kb.dma(c2f[:], od["consts2"][:, 3072:3328])
            m_su, m_iu, m_sl, eye8 = c2[:, 0:512], c2[:, 512:1024], c2[:, 1024:1536], c2[:, 1536:2048]
            m_ge4, m_le4 = c2[:, 2048:2560], c2[:, 2560:3072]
            oblk, cmk = c2f[:, 0:128], c2f[:, 128:256]
            mixT = kb.sb([128, 3, 19], F32, "mixT")
            kb.dma(mixT[:, 0:2, :], od["mixT"][:, :, :])
            vecT = kb.sb([128, 4, 8], F32, "vecT")
            kb.dma(vecT[:], od["vecT"][:, :, :])
            nw0 = kb.sb([128, 4, 3], F32, "nw0")
            sinkE = kb.sb([128, 8], F32, "sinkE")
            H = kb.sb([128, 4, 64], F32, "H")
            Hb = kb.sb([128, 4, 64], LDT, "Hb")
            identb = kb.sb([128, 128], LDT, "identb")
            oblkb_t = kb.sb([128, 128], LDT, "oblkb")
            oblkb = oblkb_t[:]
            aw2b = kb.sb([32, 512], LDT, "aw2b")
            ww2b = kb.sb([32, 2, 512], LDT, "ww2b")
            zrR = Rot([kb.sb([128, 19, 130], BF16, "zr") for _ in range(3)])
            kTR = Rot([kb.sb([128, 128], BF16, "kT") for _ in range(3)])
            v65R = Rot([kb.sb([128, 2, 65], BF16, "v65") for _ in range(3)])
            hTo = hTR
            xto = xtR
            ropR = Rot([kb.sb([128, 2, 128], F32, "rop") for _ in range(2)])
            obT_d = nc.dram_tensor("obT", [512, TT_], F32, kind="Internal").ap()
            obTT = [Tl(obT_d, "obT%d" % i) for i in range(NT)]
            rope_d = od["ropeT"]

            def od_extra():
                for q6 in range(3):
                    cs_ = kb.sb([128, 1024], F32, "c2s")
                    kb.dma(cs_[:], od["consts2"][:, q6 * 1024:(q6 + 1) * 1024])
                    kb.copy(c2[:, q6 * 1024:(q6 + 1) * 1024], cs_[:])
                kb.copy(identb[:], ident)
                kb.copy(oblkb_t[:], oblk)
                ww2 = kb.sb([32, 2, 512], F32, "ww2")
                kb.dma(ww2[:], od["w_w2"][:, :, :])
                aw2 = kb.sb([32, 512], F32, "aw2")
                kb.dma(aw2[:], od["a_w2"][:, :])
                kb.copy(ww2b[:], ww2[:])
                kb.copy(aw2b[:], aw2[:])
                srow = kb.sb([1, 8], F32, "srow")
                kb.dma(srow[:], od["sink"][:, :])
                p = PS.nxt()
                kb.mm(p[:, 0:8], ones[0:1, :], srow[0:1, :])
                kb.act(sinkE[:], p[:, 0:8], AF.Exp)
                kb.ts(nw0[:, :, 0:2], vecT[:, :, 6:8], -1.0, None, ALU.mult)
                kb.ts(nw0[:, :, 2:3], vecT[:, :, 0:1], -1.0, None, ALU.mult)
                kb.tt(mixT[:, 2, :], mixT[:, 0, :], mixT[:, 1, :], ALU.add)
                kb.ts(mixT[:, 2, :], mixT[:, 2, :], -1.0, 1.0, ALU.mult, ALU.add)
            load_weights(od["od_w_in"], 4064, od["od_w_out"], od["od_w_mod"], od["od_b_modT"], od["od_b_mod"], od["od_rows"], od_extra, ro=1)

            tr = lambda cb: slice(64 * cb, 64 * cb + 64)
            hc = lambda h: slice(64 * h, 64 * h + 64)
            pc = lambda p, cb: slice((p * 2 + cb) * 64, (p * 2 + cb + 1) * 64)
            T5 = lambda nm, dt=F32: tmp(nm, (128, 512), 1, dt=dt)
            LD = LDT
            PL = "pool" if use_pool else "dve"
            TMPN[0] = 1

            def projF(hT, col, m=128):
                p = PS.nxt()
                for k in range(8):
                    kb.mm(p[0:m, 0:128], WH["Win"][:, k, col:col + m], hT[:, k, :], start=(k == 0), stop=(k == 7))
                return p

            def stage1(n, flip, dirn, need_g, attn):
                seg = n // TPS
                xt = xto.nxt()
                kb.dma(xt[:], src[n][rows(n), :])
                pX = PD.nxt()
                for k in range(8):
                    kb.mm(pX[:, k * 128:(k + 1) * 128], xt[:, k * 128:(k + 1) * 128], Jm if flip else ident)
                hT = hTo.nxt()
                for k in range(8):
                    kb.act(hT[:, k, :], pX[:, k * 128:(k + 1) * 128], AF.Identity,
                           scale=modT[:, (8 + k) * NSEG + seg:(8 + k) * NSEG + seg + 1],
                           bias=modT[:, k * NSEG + seg:k * NSEG + seg + 1])
                zr = zrR.nxt()
                groups = [("r", 0), ("k", 4), ("v", 8)] + ([("g", 12)] if need_g else [])
                for gi, (nm, j0) in enumerate(groups):
                    for j in range(4):
                        p = projF(hT, OD[nm] + j * 128)
                        kb.copy(zr[:, j0 + j, 1:129], p[:, 0:128], eng="act" if (j % 2) else "dve")
                wj = 17 if dirn == 1 else 16
                p = projF(hT, OD["wl_b" if dirn == 1 else "wl_f"], 32)
                kb.copy(zr[0:32, wj, 1:129], p[0:32, 0:128], eng="act")
                p = projF(hT, OD["al"], 32)
                kb.copy(zr[0:32, 18, 1:129], p[0:32, 0:128], eng="dve")
                st = dict(n=n, seg=seg, xt=xt, hT=hT, zr=zr)
                if attn:
                    rop = ropR.nxt()
                    kb.dma(rop[:], rope_d[:, :, n * 128:(n + 1) * 128])
                    st["rop"] = rop
                    pk, pkr = projF(hT, OD["ck"]), projF(hT, OD["rk"])
                    t1, t2 = tmp("rt1"), tmp("rt2")
                    kb.tt(t1[:], pk[:, 0:128], rop[:, 0, :], ALU.mult)
                    kb.tt(t2[:], pkr[:, 0:128], rop[:, 1, :], ALU.mult)
                    kT = kTR.nxt()
                    kb.tt(kT[:], t1[:], t2[:], ALU.add)
                    pv = PS.nxt()
                    for k in range(8):
                        kb.mm(pv[:, 0:128], hT[:, k, :], WH["Win"][:, k, OD["cv"]:OD["cv"] + 128], start=(k == 0), stop=(k == 7))
                    v65 = v65R.nxt()
                    kb.memset(v65[:, :, 64:65], 1.0)
                    for g in range(2):
                        kb.copy(v65[:, g, 0:64], pv[:, g * 64:(g + 1) * 64], eng="act")
                    st["kT"], st["v65"] = kT, v65
                return st

            def halo(cur, prv, nxt, flip):
                zr = cur["zr"]
                for nb, dst, srccol in ((prv, 0, 128), (nxt, 129, 1)):
                    if nb is None:
                        kb.memset(zr[:, :, dst:dst + 1], 0.0)
                        continue
                    if nb["seg"] == cur["seg"]:
                        kb.copy(zr[:, :, dst:dst + 1], nb["zr"][:, :, srccol:srccol + 1])
                    else:
                        sg = max(nb["seg"], cur["seg"])
                        kb.ts(zr[:, :, dst:dst + 1], nb["zr"][:, :, srccol:srccol + 1], flg[:, sg:sg + 1], None, ALU.mult)

            def rwkv_tile(cur, flip, dirn, final):
                zr = cur["zr"]
                seg = cur["seg"]
                n = cur["n"]
                mp, mn = (1, 0) if flip else (0, 1)
                zm = tmp("zm", (128, 19, 128), 1, dt=BF16)
                wj = 17 if dirn == 1 else 16
                js = list(range(12)) + (list(range(12, 16)) if final else []) + [wj, 18]
                for j in js:
                    m_ = 32 if j >= 16 else 128
                    t0 = tmp("tsa", (128, 128), 2)
                    t1_ = tmp("tsb", (128, 128), 2)
                    kb.act(t0[0:m_, :], zr[0:m_, j, 1:129], AF.Identity, scale=mixT[0:m_, 2, j:j + 1])
                    kb.act(t1_[0:m_, :], zr[0:m_, j, 0:128], AF.Identity, scale=mixT[0:m_, mp, j:j + 1])
                    kb.tt(t0[0:m_, :], t0[0:m_, :], t1_[0:m_, :], ALU.add, eng=PL)
                    kb.stt(zm[0:m_, j, :], zr[0:m_, j, 2:130], mixT[0:m_, mn, j:j + 1], t0[0:m_, :], ALU.mult, ALU.add)
                th = tmp("th", (32, 128), 1, dt=LD)
                thf = tmp("thf", (32, 128), 1)
                kb.act(thf[:], zm[0:32, wj, :], AF.Exp, scale=-2.0)
                kb.ts(thf[:], thf[:], 1.0, None, ALU.add)
                kb.recip(thf[:], thf[:])
                kb.ts(th[:], thf[:], 2.0, -1.0, ALU.mult, ALU.add)
                F = {nm: T5("F" + nm, LD) for nm in ("rd", "kd", "bd", "kkd")}
                gam = tmp("gam", (128, 4, 2), 1)
                Tm = {nm: T5("T" + nm, LD) for nm in ("V", "KK", "KE", "NBE")}
                bon = T5("bon") if final else None
                tp = lambda nm, **kw: tmp(nm, n=2, **kw)

                def pair_gen(p):
                    r_, k_, vT = zm[:, p, :], zm[:, 4 + p, :], zm[:, 8 + p, :]
                    alv = zm[0:32, 18, :]
                    vc = lambda i: vecT[:, p, i:i + 1]
                    pa = PS.nxt()
                    kb.mm(pa[:, 0:128], aw2b[:, p * 128:(p + 1) * 128], alv)
                    a = tp("a")
                    kb.sig(a[:], pa[:, 0:128], nbias=nw0[:, p, 2:3])
                    yield
                    pw = PS.nxt()
                    kb.mm(pw[:, 0:128], ww2b[:, dirn, p * 128:(p + 1) * 128], th[:])
                    e1 = tp("e1")
                    kb.act(e1[:], pw[:, 0:128], AF.Exp, scale=-1.0, bias=nw0[:, p, dirn:dirn + 1])
                    yield
                    kb.act(e1[:], e1[:], AF.Ln, bias=1.0)
                    ew = tp("ew")
                    kb.act(ew[:], e1[:], AF.Exp, scale=-1.0, bias=-0.5)
                    c_ = tp("c_")
                    kb.scan(c_[:], cmk, ew[:], 0.0, ALU.mult, ALU.subtract)
                    yield
                    eCt, eN, eX, eE = tp("eC"), tp("eN"), tp("eX"), tp("eE")
                    eC = eCt[:]
                    kb.act(eC, c_[:], AF.Exp)
                    kb.copy(gam[:, p, 0:1], eCt[:, 63:64], eng="act")
                    kb.copy(gam[:, p, 1:2], eCt[:, 127:128], eng="act")
                    kb.act(eN[:], c_[:], AF.Exp, scale=-1.0)
                    cx = tp("cx")
                    kb.tt(cx[:], c_[:], ew[:], ALU.add, eng=PL)
                    yield
                    kb.act(eX[:], cx[:], AF.Exp)
                    for cb in range(2):
                        kb.act(eE[:, tr(cb)], c_[:, tr(cb)], AF.Exp, scale=-1.0, bias=c_[:, 64 * cb + 63:64 * cb + 64])
                    kk0 = tp("kk0")
                    kb.ts(kk0[:], k_, vc(1), None, ALU.mult)
                    sq = tp("sq", dt=LD)
                    kb.act(sq[:], kk0[:], AF.Square)
                    yield
                    pss = PS.nxt()
                    kb.mm(pss[:, 0:128], oblkb, sq[:])
                    nr = tp("nr")
                    kb.ts(nr[:], pss[:, 0:128], 1e-24, None, ALU.max)
                    yield
                    kb.act(nr[:], nr[:], AF.Ln)
                    kb.act(nr[:], nr[:], AF.Exp, scale=-0.5)
                    kk = tp("kk")
                    kb.tt(kk[:], kk0[:], nr[:], ALU.mult, eng=PL)
                    t1 = tp("t1")
                    kb.ts(t1[:], a[:], 1.0, vc(2), ALU.subtract, ALU.mult)
                    km = tp("km")
                    kb.stt(km[:], t1[:], 1.0, k_, ALU.add, ALU.mult)
                    yield
                    b_ = tp("b_")
                    kb.tt(b_[:], kk[:], a[:], ALU.mult, eng=PL)
                    P_ = slice(p * 128, (p + 1) * 128)
                    kb.tt(F["rd"][:, P_], r_, eC, ALU.mult, eng=PL)
                    kb.tt(F["kd"][:, P_], km[:], eN[:], ALU.mult)
                    kb.tt(F["kkd"][:, P_], kk[:], eX[:], ALU.mult, eng=PL)
                    yield
                    kb.tt(F["bd"][:, P_], b_[:], eN[:], ALU.mult, eng=PL)
                    kef = tp("kef", dt=LD)
                    kb.tt(kef[:], km[:], eE[:], ALU.mult, eng=PL)
                    nbf = tp("nbf", dt=LD)
                    kb.stt(nbf[:], b_[:], -1.0, eE[:], ALU.mult, ALU.mult)
                    yield
                    if final:
                        rk = tp("rk", dt=LD)
                        kb.stt(rk[:], r_, vc(3), km[:], ALU.mult, ALU.mult)
                        pb = PS.nxt()
                        kb.mm(pb[:, 0:128], oblkb, rk[:])
                        kb.tt(bon[:, P_], pb[:, 0:128], vT, ALU.mult)
                        yield
                    pt = PS.nxt()
                    for i_, srcv in enumerate((vT, F["kkd"][:, P_], kef[:], nbf[:])):
                        kb.mm(pt[:, i_ * 128:(i_ + 1) * 128], srcv, identb[:])
                    for i_, dst in enumerate(("V", "KK", "KE", "NBE")):
                        kb.copy(Tm[dst][:, P_], pt[:, i_ * 128:(i_ + 1) * 128], eng="act" if i_ % 2 else "dve")

                for pp in ((0, 1), (2, 3)):
                    gens = [pair_gen(p) for p in pp]
                    while gens:
                        for g_ in list(gens):
                            try:
                                next(g_)
                            except StopIteration:
                                gens.remove(g_)
                rd, kd, bd, kkd = F["rd"], F["kd"], F["bd"], F["kkd"]
                V, KK, KE, NBE = Tm["V"], Tm["KK"], Tm["KE"], Tm["NBE"]
                frs = lambda hp: slice(64 * hp, 64 * hp + 64)

                def newt(nm):
                    return tmp(nm, (128, 512), 5, dt=LD) if nm == "S5" else T5(nm, LD)

                def scores(lhs, rhs, msk, nm, neg=False):
                    ps = PS.nxt()
                    for hp in range(2):
                        for cb in range(2):
                            for p in range(4):
                                h = 2 * p + hp
                                kb.mm(ps[tr(cb), hc(h)], lhs[frs(hp), pc(p, cb)], rhs[frs(hp), pc(p, cb)])
                    o_ = newt(nm)
                    if neg:
                        kb.stt(o_[:], ps[:], -1.0, msk, ALU.mult, ALU.mult)
                    else:
                        kb.tt(o_[:], ps[:], msk, ALU.mult)
                    return o_

                def bprod(A, B, nm, add=None, eng="act"):
                    ps = PS.nxt()
                    for cb in range(2):
                        for h in range(8):
                            kb.mm(ps[tr(cb), hc(h)], A[tr(cb), hc(h)], B[tr(cb), hc(h)])
                    o_ = newt(nm)
                    if add is None:
                        kb.copy(o_[:], ps[:], eng=eng)
                    else:
                        kb.tt(o_[:], ps[:], add[:], ALU.add)
                    return o_

                def fprod(pairs):
                    ps = PS.nxt()
                    for cb in range(2):
                        for hp in range(2):
                            for p in range(4):
                                h = 2 * p + hp
                                for i, (A, B) in enumerate(pairs):
                                    kb.mm(ps[frs(hp), pc(p, cb)], A[tr(cb), hc(h)], B[tr(cb), hc(h)], start=(i == 0), stop=(i == len(pairs) - 1))
                    return ps

                LkT = scores(kd, kkd, m_su, "LkT")
                MkT = scores(kd, rd, m_iu, "MkT")
                Nn = scores(bd, kkd, m_su, "S5")
                nMbT = scores(bd, rd, m_iu, "nMbT", neg=True)
                Ll = scores(kkd, bd, m_sl, "S5")
                Y = tmp("S5", (128, 512), 5, dt=LD)
                kb.tt(Y[:], eye8, Nn[:], ALU.subtract)
                Lp, Np = Ll, Nn
                for lvl in range(5):
                    L2 = bprod(Np, Lp, "S5", eng="act")
                    if lvl < 4:
                        N2 = bprod(Lp, Np, "S5", eng="dve")
                    Y = bprod(L2, Y, "S5", add=Y)
                    Lp, Np = L2, (N2 if lvl < 4 else None)
                TT = Y
                Zs = bprod(LkT, V, "Fkkd")
                Ws = bprod(TT, KK, "Fkd", eng="dve")
                U0 = bprod(TT, Zs, "Fbd")
                pP = fprod([(Ws, NBE)])
                PTs = T5("PTs", LD)
                kb.copy(PTs[:], pP[:], eng="act")
                pQ = fprod([(KE, V), (NBE, U0)])
                Qs = T5("LkT", LD)
                kb.copy(Qs[:], pQ[:], eng="dve")
                pR = fprod([(Ws, nMbT)])
                RpT = T5("RpT", LD)
                kb.tt(RpT[:], pR[:], rd[:], ALU.add)
                pOd = PD.nxt()
                pO0, pO1 = pOd[:, 0:512], pOd[:, 512:1024]
                for cb in range(2):
                    for hp in range(2):
                        for p in range(4):
                            h = 2 * p + hp
                            kb.mm(pO0[frs(hp), pc(p, cb)], V[tr(cb), hc(h)], MkT[tr(cb), hc(h)], start=True, stop=False)
                            kb.mm(pO0[frs(hp), pc(p, cb)], U0[tr(cb), hc(h)], nMbT[tr(cb), hc(h)], start=False, stop=True)
                first = (n % TPS == (TPS - 1 if flip else 0))
                if first:
                    kb.ts(H[:], H[:], flg[:, (NSEG if flip else 0) + seg:(NSEG if flip else 0) + seg + 1], None, ALU.mult)
                    kb.copy(Hb[:], H[:], eng="act")
                for cb in range(2):
                    for hp in range(2):
                        for p in range(4):
                            kb.mm(pO1[frs(hp), pc(p, cb)], Hb[frs(hp), p, :], RpT[frs(hp), pc(p, cb)])
                    pH = PS.nxt()
                    for hp in range(2):
                        for p in range(4):
                            kb.mm(pH[frs(hp), p * 64:(p + 1) * 64], PTs[frs(hp), pc(p, cb)], Hb[frs(hp), p, :], start=True, stop=False)
                            kb.mm(pH[frs(hp), p * 64:(p + 1) * 64], identb[frs(hp), frs(hp)], Qs[frs(hp), pc(p, cb)], start=False, stop=True)
                    for p in range(4):
                        kb.stt(H[:, p, :], H[:, p, :], gam[:, p, cb:cb + 1], pH[:, p * 64:(p + 1) * 64], ALU.mult, ALU.add)
                    kb.copy(Hb[:], H[:], eng="act")
                return pOd, zm, bon

            def odd_bwd():
                kb.memset(H[:], 0.0)
                kb.memset(Hb[:], 0.0)
                order = list(range(NT - 1, -1, -1))
                sts = {}
                for i, n in enumerate(order + [None]):
                    if n is not None:
                        sts[n] = stage1(n, True, 1, False, False)
                    if i == 0:
                        continue
                    m = order[i - 1]
                    prv = sts.get(order[i - 2]) if i >= 2 else None
                    halo(sts[m], prv, sts.get(n) if n is not None else None, True)
                    pOd, _, _ = rwkv_tile(sts[m], True, 1, False)
                    o_ = T5("obl")
                    o2 = T5("o2f")
                    kb.copy(o2[:], pOd[:, 0:512], eng="act")
                    for p in range(4):
                        kb.tt(o_[:, p * 128:(p + 1) * 128][:, ::-1], pOd[:, 512 + p * 128:512 + (p + 1) * 128], o2[:, p * 128:(p + 1) * 128], ALU.add)
                        kb.dma(obTT[m][p * 128:(p + 1) * 128, m * 128:(m + 1) * 128], o_[:, p * 128:(p + 1) * 128])
                    if i >= 2:
                        del sts[order[i - 2]]

            def attn_tile(cur, prv, nxt):
                hT, rop, seg = cur["hT"], cur["rop"], cur["seg"]
                qr = T5("qrb", BF16)
                for r in range(4):
                    pq, pqr = projF(hT, OD["cq"] + r * 128), projF(hT, OD["rq"] + r * 128)
                    t1, t2 = tmp("rt1"), tmp("rt2")
                    kb.tt(t1[:], pq[:, 0:128], rop[:, 0, :], ALU.mult)
                    kb.tt(t2[:], pqr[:, 0:128], rop[:, 1, :], ALU.mult)
                    kb.tt(qr[:, r * 128:(r + 1) * 128], t1[:], t2[:], ALU.add)
                pOa = PD.nxt()
                for g in range(2):
                    gr = slice(64 * g, 64 * g + 64)
                    Pts = []
                    for nb, msk in ((prv, m_ge4), (cur, None), (nxt, m_le4)):
                        if nb is None:
                            continue
                        pS_ = PS.nxt()
                        kb.mm(pS_[:], nb["kT"][gr, :], qr[gr, :])
                        Pt = tmp("Pt", (128, 512), 3, dt=BF16)
                        kb.act(Pt[:], pS_[:], AF.Exp, scale=0.125)
                        if msk is not None:
                            if nb["seg"] == seg:
                                kb.tt(Pt[:], Pt[:], msk, ALU.mult)
                            else:
                                sg = max(nb["seg"], seg)
                                kb.tt(Pt[:], Pt[:], msk, ALU.mult)
                                kb.ts(Pt[:], Pt[:], flg[:, sg:sg + 1], None, ALU.mult)
                        Pts.append((Pt, nb["v65"]))
                    for r in range(4):
                        for i, (Pt, v65) in enumerate(Pts):
                            kb.mm(pOa[:, g * 512 + r * 65:g * 512 + (r + 1) * 65], Pt[:, r * 128:(r + 1) * 128], v65[:, g, :],
                                  start=(i == 0), stop=(i == len(Pts) - 1))
                den = tmp("den", (128, 8))
                for g in range(2):
                    kb.tt(den[:, 4 * g:4 * g + 4], pOa[:, slice(g * 512 + 64, g * 512 + 64 + 3 * 65 + 1, 65)], sinkE[:, 4 * g:4 * g + 4], ALU.add)
                kb.recip(den[:], den[:])
                co = T5("sq5")
                for g in range(2):
                    pv4 = pOa[:, g * 512:g * 512 + 260]
                    src = Vw(pv4.tile, pv4.ap.rearrange("p (a b) -> p a b", a=4)[:, :, 0:64])
                    kb.tt(v3(co[:, g * 256:(g + 1) * 256], 4), src, bc(den[:, 4 * g:4 * g + 4], (128, 4, 64)), ALU.mult)
                pG = PS.nxt()
                for k in range(8):
                    kb.mm(pG[:], hT[:, k, :], WH["Win"][:, k, OD["cg"]:OD["cg"] + 512], start=(k == 0), stop=(k == 7))
                sg_ = T5("cen5")
                kb.sig(sg_[:], pG[:])
                kb.tt(co[:], co[:], sg_[:], ALU.mult, eng="pool")
                cob = T5("cob", BF16)
                kb.tt(cob[:], co[:], pG[:], ALU.mult)
                return cob

            def odd_fwd():
                kb.memset(H[:], 0.0)
                kb.memset(Hb[:], 0.0)
                order = list(range(NT))
                sts = {}
                for i, n in enumerate(order + [None]):
                    if n is not None:
                        sts[n] = stage1(n, False, 0, True, True)
                    if i == 0:
                        continue
                    m = order[i - 1]
                    cur = sts[m]
                    prv = sts.get(order[i - 2]) if i >= 2 else None
                    nxt = sts.get(n) if n is not None else None
                    halo(cur, prv, nxt, False)
                    pOd, zm, bon = rwkv_tile(cur, False, 0, True)
                    ob = T5("obl")
                    for p in range(4):
                        kb.dma(ob[:, p * 128:(p + 1) * 128], obTT[m][p * 128:(p + 1) * 128, m * 128:(m + 1) * 128])
                    od_ = ob
                    kb.tt(od_[:], pOd[:, 0:512], ob[:], ALU.add)
                    kb.tt(od_[:], pOd[:, 512:1024], od_[:], ALU.add)
                    onT = tmp("onT", (128, 8, 128), 1, dt=BF16)
                    P4 = lambda p: slice(p * 128, (p + 1) * 128)
                    pm_ = PS.nxt()
                    for p in range(4):
                        kb.mm(pm_[:, P4(p)], oblk, od_[:, P4(p)])
                    cen = T5("cen5")
                    kb.stt(cen[:], pm_[:], -1.0 / 64.0, od_[:], ALU.mult, ALU.add)
                    sq5 = T5("sq5")
                    kb.act(sq5[:], cen[:], AF.Square)
                    pv_ = PS.nxt()
                    for p in range(4):
                        kb.mm(pv_[:, P4(p)], oblk, sq5[:, P4(p)])
                    rs5 = T5("rs5")
                    kb.rsqrt(rs5[:], pv_[:], 1.0 / 64.0, 64e-5)
                    kb.tt(cen[:], cen[:], rs5[:], ALU.mult)
                    for p in range(4):
                        kb.ts(cen[:, P4(p)], cen[:, P4(p)], vecT[:, p, 4:5], vecT[:, p, 5:6], ALU.mult, ALU.add)
                    kb.tt(cen[:], cen[:], bon[:], ALU.add, eng=PL)
                    zg = zm[:, 12:16, :]
                    kb.sig(v3(sq5[:], 4), zg)
                    kb.tt(cen[:], cen[:], sq5[:], ALU.mult)
                    kb.tt(onT[:, 4:8, :], v3(cen[:], 4), zg, ALU.mult)
                    co = attn_tile(cur, prv, nxt)
                    pT = PS.nxt()
                    for r in range(4):
                        kb.mm(pT[:, r * 128:(r + 1) * 128], co[:, r * 128:(r + 1) * 128], identb[:])
                    kb.copy(onT[:, 0:4, :], pT[:], eng="act")
                    xr_ = xto.nxt()
                    kb.dma(xr_[:], src[m][rows(m), :])
                    finish2(m, cur["seg"], onT, xr_, yT)
                    if i >= 2:
                        del sts[order[i - 2]]

            odd_bwd()
            odd_fwd()
        if 1 not in layers:
            for n in range(NT):
                t_ = xtR.nxt()
                kb.dma(t_[:], x1T[n][rows(n), :])
                kb.dma(yT[n][rows(n), :], t_[:])
        kb.P.emit()
    return nc


def consts():
    c = np.zeros((128, 512), np.float32)
    c[:, 0:128] = np.eye(128)
    c[:, 128:256] = np.eye(128)[::-1]
    j = np.arange(128)[:, None]
    i = np.arange(128)[None, :]
    c[:, 256:384] = (j <= i)
    c[:, 384:512] = 1.0
    return c


def shared_inputs(w):
    f = lambda a: np.ascontiguousarray(a, dtype=np.float32)
    m = {}
    m["consts"] = consts()
    m["ev_w_mod"] = f(w["ev_w_mod"][0])
    m["ev_b_modT"] = f(w["ev_b_mod"][0].reshape(24, 128).T)
    m["ev_b_mod"] = f(w["ev_b_mod"][0].reshape(1, -1))
    m["ev_w_in"] = f(w["ev_w_in"][0])
    wi = w["ev_w_in"][0]
    m["ev_w_blT"] = f(np.concatenate([wi[:, 3584:3600].T, wi[:, 3600:3616].T], axis=1))
    m["gla_w_gk"] = f(np.concatenate([w["gla_w_gk"][0, 0], w["gla_w_gk"][0, 1]], axis=1))
    m["gla_b_gkT"] = f(w["gla_b_gk"][0].reshape(2, 2, 128).transpose(2, 0, 1).reshape(128, 4))
    m["ev_rows"] = f(np.concatenate([w["hgrn_norm"][0], w["gla_norm"][0], w["ev_ln_g"][0], w["ev_ln_b"][0]]).reshape(1, -1))
    m["ev_w_out"] = f(w["ev_w_out"][0])
    m["lblT"] = f(w["hgrn_lb_logits"].reshape(2, 2, 4, 128).transpose(3, 0, 1, 2).reshape(128, 16))
    return m


def consts2():
    c = np.zeros((128, 6 * 512 + 256), np.float32)
    r = (np.arange(128) % 64)[:, None]
    q = (np.arange(512) % 64)[None, :]
    c[:, 0:512] = r < q
    c[:, 512:1024] = r <= q
    c[:, 1024:1536] = r > q
    c[:, 1536:2048] = r == q
    j = np.arange(128)[:, None]
    i = (np.arange(512) % 128)[None, :]
    c[:, 2048:2560] = j >= i
    c[:, 2560:3072] = j <= i
    a = np.arange(128)
    c[:, 3072:3200] = (a[:, None] // 64) == (a[None, :] // 64)
    cm = np.ones((128, 128), np.float32)
    cm[:, 0] = 0.0
    cm[:, 64] = 0.0
    c[:, 3200:3328] = cm
    return c


def rope_table(pos):
    inv = (10000.0 ** (-np.arange(0, 64, 2, dtype=np.float32) / np.float32(64))).astype(np.float32)
    ang = (pos.astype(np.float32)[:, None] * inv[None, :]).astype(np.float32)
    cos, sin = np.cos(ang).astype(np.float32), np.sin(ang).astype(np.float32)
    cc = np.concatenate([cos, cos], axis=1).T
    ss = np.concatenate([-sin, sin], axis=1).T
    t = np.stack([np.concatenate([cc, cc], 0), np.concatenate([ss, ss], 0)], axis=1)
    return np.ascontiguousarray(t, dtype=np.float32)


def od_shared(w):
    f = lambda a: np.ascontiguousarray(a, dtype=np.float32)
    m = {}
    wi = w["od_w_in"][0]
    qcols = np.concatenate([np.concatenate([np.arange(r * 64, r * 64 + 64), np.arange((4 + r) * 64, (4 + r) * 64 + 64)])
                            for r in range(4)])
    swap = lambda cols: np.concatenate([np.concatenate([cols[i * 64 + 32:i * 64 + 64], cols[i * 64:i * 64 + 32]])
                                        for i in range(len(cols) // 64)])
    kcols = np.arange(512, 640)
    order = np.concatenate([qcols, np.arange(512, 3424), swap(qcols), swap(kcols)])
    m["od_w_in"] = f(wi[:, order])
    m["od_w_mod"] = f(w["od_w_mod"][0])
    m["od_b_modT"] = f(w["od_b_mod"][0].reshape(24, 128).T)
    m["od_b_mod"] = f(w["od_b_mod"][0].reshape(1, -1))
    m["od_rows"] = f(np.concatenate([np.zeros(D, np.float32), w["od_ln_g"][0], w["od_ln_b"][0]]).reshape(1, -1))
    m["od_w_out"] = f(w["od_w_out"][0])
    m["consts2"] = consts2()
    mix = w["rwkv_mix"][0]
    mt = np.zeros((128, 2, 19), np.float32)
    starts = [0, 128, 256, 384, 512, 640, 768, 896, 1024, 1152, 1280, 1408, 1632, 1760, 1888, 2016, 1536, 1568, 1600]
    for j, st in enumerate(starts):
        n_ = 32 if j >= 16 else 128
        mt[0:n_, :, j] = mix[:, st:st + n_].T
    m["mixT"] = mt
    vecs = [w["rwkv_a0"][0], w["rwkv_k_k"][0], w["rwkv_k_a"][0], w["rwkv_r_k"][0], w["rwkv_ln_w"][0], w["rwkv_ln_b"][0],
            w["rwkv_w0"][0, 0], w["rwkv_w0"][0, 1]]
    m["vecT"] = f(np.stack([v.reshape(4, 128) for v in vecs], axis=-1).transpose(1, 0, 2))
    m["w_w2"] = f(w["rwkv_w_w2"][0].transpose(1, 0, 2))
    m["a_w2"] = f(w["rwkv_a_w2"][0])
    m["sink"] = f(w["swa_sink"][0].reshape(1, 8))
    return m


_NC_CACHE = {}


def kernel(**inputs):
    NSEG, L = 4, 4096
    w = {k: np.asarray(v) for k, v in inputs.items()}
    xp, xs, cp, cs = w["x_prompt"], w["x_sample"], w["c_prompt"], w["c_sample"]
    shared = shared_inputs(w)
    shared.update(od_shared(w))
    plan = []
    for b in range(2):
        plan.append([("p", b, k) for k in range(4)])
    samp = [[0, 1, 2], [3, 4, 5], [6, 7, 8], [9, 10, 11], [12, 13], [14, 15]]
    for lst in samp:
        plan.append([("s", i, 0) for i in lst] + [("s", lst[0], 0)] * (4 - len(lst)))
    in_maps = []
    for core in range(8):
        m = dict(shared)
        xs_, cs_, pos = [], [], []
        fl = np.zeros((128, 2 * NSEG), np.float32)
        for s_, (kind, b, k) in enumerate(plan[core]):
            if kind == "p":
                xs_.append(xp[b, k * L:(k + 1) * L])
                cs_.append(cp[b])
                pos.append(np.arange(k * L, (k + 1) * L, dtype=np.float32))
                if k > 0:
                    fl[:, s_] = 1.0
                if k < 3:
                    fl[:, NSEG + s_] = 1.0
            else:
                xs_.append(xs[b])
                cs_.append(cs[b])
                pos.append(np.arange(L, dtype=np.float32))
        m["x"] = np.ascontiguousarray(np.concatenate(xs_, axis=0), dtype=np.float32)
        cc = np.stack(cs_, axis=0)
        m["cT"] = np.ascontiguousarray(cc.reshape(NSEG, 8, 128).transpose(2, 1, 0).reshape(128, 8 * NSEG), dtype=np.float32)
        m["flags"] = fl
        m["ropeT"] = rope_table(np.concatenate(pos))
        in_maps.append(m)
    if "nc" not in _NC_CACHE:
        _NC_CACHE["nc"] = build(NSEG, L)
    res = run_bass_kernel_spmd(_NC_CACHE["nc"], in_maps, core_ids=list(range(8)))
    yp = np.zeros_like(xp)
    ys = np.zeros_like(xs)
    for core in range(8):
        y = res.results[core]["y"].reshape(NSEG, L, D)
        seen = set()
        for s_, (kind, b, k) in enumerate(plan[core]):
            if kind == "p":
                yp[b, k * L:(k + 1) * L] = y[s_]
            elif b not in seen:
                ys[b] = y[s_]
                seen.add(b)
    return (yp, ys)
```
